# Optimizing a Trainium2 kernel written in Bass

```python
import math, functools
import jax, jax.numpy as jnp
from jax import lax
import numpy as np

D_MODEL = 1024
BATCH = 8
SEQ = 8192
DEPTH = 2

CTX_LEN = 256
GRID_W = 64
EPS = 1e-6

DA_HEADS = 4
DA_QK = 64
DA_V = 2 * DA_QK
DA_WIDTH = DA_HEADS * DA_V
Q_BLOCK = 128
ROPE_BASE = 10000.0

SSM_HEADS = 8
SSM_P = 64
SSM_WIDTH = SSM_HEADS * SSM_P
SSM_GROUPS = 2
SSM_N = 64
SSM_CONV = 3
SSM_CONV_DIM = SSM_WIDTH + 2 * SSM_GROUPS * SSM_N
SSM_CHUNK = 128

GLA_HEADS = 4
GLA_DK = 64
GLA_DV = 128
GLA_WIDTH = GLA_HEADS * GLA_DV
GLA_RANK = 16
GLA_GATE_NORM = 16.0
GLA_CHUNK = 64

IN_SPLITS = (
    2 * DA_HEADS * DA_QK, 2 * DA_HEADS * DA_QK, DA_WIDTH, DA_WIDTH,
    SSM_WIDTH, SSM_WIDTH, SSM_GROUPS * SSM_N, SSM_GROUPS * SSM_N, 2 * SSM_HEADS,
    GLA_HEADS * GLA_DK, GLA_HEADS * GLA_DK, GLA_WIDTH, GLA_WIDTH, 2 * GLA_RANK,
    D_MODEL, D_MODEL, D_MODEL)
IN_WIDTH = 4 * 512 + 2 * 512 + 2 * 128 + 16 + 2 * 256 + 2 * 512 + 32 + 3 * D_MODEL

kernel_name = "hybrid_diffattn_ssd_gla_prefix_trunk"


def rmsnorm(x, w):
    xf = x.astype(jnp.float32)
    y = xf * lax.rsqrt(jnp.mean(xf * xf, axis=-1, keepdims=True) + EPS)
    return (y * w.astype(jnp.float32)).astype(x.dtype)


def group_rmsnorm(y, w, groups):
    shp = y.shape
    yg = y.reshape(*shp[:-1], groups, shp[-1] // groups)
    return rmsnorm(yg, w.reshape(groups, -1)).reshape(shp)


def split_cols(p, sizes):
    idx = np.cumsum(sizes)[:-1].tolist()
    return jnp.split(p, idx, axis=-1)


def rope_2d_tables(n, dtype):
    rows = n // GRID_W
    row = jnp.repeat(jnp.arange(rows), GRID_W)
    col = jnp.tile(jnp.arange(GRID_W), rows)
    pos = jnp.stack([row, col], axis=-1).astype(jnp.float32)
    nf = DA_QK // 4
    inv = ROPE_BASE ** (-jnp.arange(nf, dtype=jnp.float32) / nf)
    ang = jnp.broadcast_to(pos[:, :, None, None] * inv, (n, 2, 2, nf)).reshape(n, DA_QK)
    return jnp.cos(ang).astype(dtype), jnp.sin(ang).astype(dtype)


def apply_rope_2d(x, cos, sin):
    xr = x.reshape(*x.shape[:-1], 2, 2, DA_QK // 4)
    rot = jnp.stack([-xr[..., 1, :], xr[..., 0, :]], axis=-2).reshape(x.shape)
    return x * cos[:, None, :] + rot * sin[:, None, :]


def depthwise_conv_centred(x, w, b):
    y = lax.conv_general_dilated(x, w[:, None, :].astype(x.dtype), window_strides=(1,), padding='SAME',
                                 dimension_numbers=('NWC', 'WIO', 'NWC'), feature_group_count=x.shape[-1])
    return y + b.astype(x.dtype)


def diff_softmax_attend(q, k, v, lam):
    s = jnp.einsum('bqhd,bkhd->bhqk', q, k).astype(jnp.float32) * (DA_QK ** -0.5)
    pr = jax.nn.softmax(s, axis=-1)
    bsz, _, tq, tk = pr.shape
    pr = pr.reshape(bsz, DA_HEADS, 2, tq, tk)
    wts = pr[:, :, 0] - lam * pr[:, :, 1]
    return jnp.einsum('bhqk,bkhv->bqhv', wts.astype(v.dtype), v)


def da_heads(p):
    bsz, t = p[0].shape[:2]
    q = p[0].reshape(bsz, t, 2 * DA_HEADS, DA_QK)
    k = p[1].reshape(bsz, t, 2 * DA_HEADS, DA_QK)
    v = p[2].reshape(bsz, t, DA_HEADS, DA_V)
    return q, k, v


def da_post(o, z, norm_w, lam_init):
    o = rmsnorm(o, norm_w) * (1.0 - lam_init)
    return o.reshape(*o.shape[:2], DA_WIDTH) * jax.nn.silu(z)


def ssd_chunked(x, dt, bm, cm, h0, a, d_skip):
    f32 = jnp.float32
    bsz, t, nh, p = x.shape
    g, n = bm.shape[2], bm.shape[3]
    hg = nh // g
    L = SSM_CHUNK
    nc = t // L
    xf = x.astype(f32).reshape(bsz, nc, L, g, hg, p)
    dtc = dt.astype(f32).reshape(bsz, nc, L, g, hg)
    bc = bm.astype(f32).reshape(bsz, nc, L, g, n)
    cc = cm.astype(f32).reshape(bsz, nc, L, g, n)
    acs = jnp.cumsum(dtc * a.astype(f32).reshape(g, hg), axis=2)
    tri = jnp.tril(jnp.ones((L, L), dtype=bool))
    seg = acs[:, :, :, None] - acs[:, :, None, :]
    decay = jnp.exp(jnp.where(tri[:, :, None, None], seg, -jnp.inf))
    scores = jnp.einsum('bclgn,bcsgn->bclsg', cc, bc)
    wts = scores[..., None] * decay * dtc[:, :, None]
    y = jnp.einsum('bclsgh,bcsghp->bclghp', wts, xf)
    dte = jnp.exp(acs[:, :, -1:] - acs) * dtc
    states = jnp.einsum('bclgn,bclgh,bclghp->bcghpn', bc, dte, xf)
    chunk_decay = jnp.exp(acs[:, :, -1])

    def step(h, inp):
        st, dec = inp
        return dec[..., None, None] * h + st, h

    h_last, h_starts = lax.scan(step, h0, (jnp.moveaxis(states, 1, 0), jnp.moveaxis(chunk_decay, 1, 0)))
    h_starts = jnp.moveaxis(h_starts, 0, 1)
    y = y + jnp.einsum('bclgn,bcghpn,bclgh->bclghp', cc, h_starts, jnp.exp(acs))
    y = y + d_skip.astype(f32).reshape(g, hg)[:, :, None] * xf
    return y.reshape(bsz, t, nh, p).astype(x.dtype), h_last


def gla_chunked(q, k, v, g, h0):
    f32 = jnp.float32
    bsz, t, nh, dk = q.shape
    dv = v.shape[-1]
    L = GLA_CHUNK
    nc = t // L
    qf = q.astype(f32).reshape(bsz, nc, L, nh, dk) * (dk ** -0.5)
    kf = k.astype(f32).reshape(bsz, nc, L, nh, dk)
    vf = v.astype(f32).reshape(bsz, nc, L, nh, dv)
    gc = jnp.cumsum(g.astype(f32).reshape(bsz, nc, L, nh, dk), axis=2)
    q_dec = qf * jnp.exp(gc)
    k_dec = kf * jnp.exp(-gc)
    tri = jnp.tril(jnp.ones((L, L), dtype=bool))
    att = jnp.where(tri, jnp.einsum('bclhk,bcshk->bchls', q_dec, k_dec), 0.0)
    y = jnp.einsum('bchls,bcshv->bclhv', att, vf)
    g_last = gc[:, :, -1]
    k_end = kf * jnp.exp(g_last[:, :, None] - gc)
    states = jnp.einsum('bclhk,bclhv->bchkv', k_end, vf)

    def step(h, inp):
        st, dec = inp
        return dec[..., None] * h + st, h

    h_last, h_starts = lax.scan(step, h0, (jnp.moveaxis(states, 1, 0), jnp.moveaxis(jnp.exp(g_last), 1, 0)))
    h_starts = jnp.moveaxis(h_starts, 0, 1)
    y = y + jnp.einsum('bclhk,bchkv->bclhv', q_dec, h_starts)
    return y.reshape(bsz, t, nh, dv).astype(v.dtype), h_last


def scan_ctx_then_latent(scan, ctx_in, lat_in, h0, reverse):
    def flip(arrs):
        return tuple(jnp.flip(a, axis=1) for a in arrs)
    if reverse:
        ctx_in, lat_in = flip(ctx_in), flip(lat_in)
    y_ctx, h_ctx = scan(*ctx_in, h0)
    y_lat, _ = scan(*lat_in, h_ctx)
    if reverse:
        y_ctx, y_lat = jnp.flip(y_ctx, axis=1), jnp.flip(y_lat, axis=1)
    return y_ctx, y_lat


def ssm_prep(p, conv_w, conv_b, dt_bias):
    xbc = jax.nn.silu(depthwise_conv_centred(jnp.concatenate([p[4], p[6], p[7]], axis=-1), conv_w, conv_b))
    xs, bm, cm = jnp.split(xbc, [SSM_WIDTH, SSM_WIDTH + SSM_GROUPS * SSM_N], axis=-1)
    bsz, t = xs.shape[:2]
    dt = jax.nn.softplus(p[8].astype(jnp.float32).reshape(bsz, t, 2, SSM_HEADS) + dt_bias.astype(jnp.float32))
    return (xs.reshape(bsz, t, SSM_HEADS, SSM_P), bm.reshape(bsz, t, SSM_GROUPS, SSM_N),
            cm.reshape(bsz, t, SSM_GROUPS, SSM_N), dt)


def ssm_post(y, z, norm_w):
    y = y.reshape(*y.shape[:2], SSM_WIDTH) * jax.nn.silu(z)
    return group_rmsnorm(y, norm_w, SSM_GROUPS)


def gla_prep(p, w_gate, b_gate):
    bsz, t = p[9].shape[:2]
    q = p[9].reshape(bsz, t, GLA_HEADS, GLA_DK)
    k = p[10].reshape(bsz, t, GLA_HEADS, GLA_DK)
    v = p[11].reshape(bsz, t, GLA_HEADS, GLA_DV)
    lr = p[13].reshape(bsz, t, 2, GLA_RANK)
    g = jax.nn.log_sigmoid((jnp.einsum('btdr,drk->btdk', lr, w_gate) + b_gate).astype(jnp.float32)) / GLA_GATE_NORM
    return q, k, v, g.reshape(bsz, t, 2, GLA_HEADS, GLA_DK)


def gla_post(o, z, norm_w):
    o = rmsnorm(o, norm_w)
    return o.reshape(*o.shape[:2], GLA_WIDTH) * jax.nn.silu(z)


def merge_branches(o_da, o_ssm, o_gla, p, w_out_da, w_out_ssm, w_out_gla, w_o):
    u = (jax.nn.sigmoid(p[14]) * (o_da @ w_out_da)
         + jax.nn.sigmoid(p[15]) * (o_ssm @ w_out_ssm)
         + jax.nn.sigmoid(p[16]) * (o_gla @ w_out_gla))
    return u @ w_o


def hybrid_layer(x, cx, c, c_ctx, cos, sin, lam_init, need_ctx,
                 w_mod, b_mod, norm_w, w_in, da_lambda, da_norm_w, w_out_da,
                 conv_w, conv_b, dt_bias, a_log, d_skip, ssm_norm_w, w_out_ssm,
                 gla_w_gate, gla_b_gate, gla_norm_w, w_out_gla, w_o):
    f32 = jnp.float32
    bsz, n = x.shape[:2]
    shift, scale, gate = jnp.split((jax.nn.silu(c) @ w_mod + b_mod)[:, None, :], 3, axis=-1)
    shift_c, scale_c, gate_c = jnp.split(jax.nn.silu(c_ctx) @ w_mod + b_mod, 3, axis=-1)
    h = rmsnorm(x, norm_w) * (1.0 + scale) + shift
    hc = rmsnorm(cx, norm_w) * (1.0 + scale_c) + shift_c
    pl = split_cols(h @ w_in, IN_SPLITS)
    pc = split_cols(hc @ w_in, IN_SPLITS)

    q_l, k_l, v_l = da_heads(pl)
    q_l, k_l = apply_rope_2d(q_l, cos, sin), apply_rope_2d(k_l, cos, sin)
    q_c, k_c, v_c = da_heads(pc)
    lam_f = da_lambda.astype(f32)
    lam = jnp.exp(jnp.sum(lam_f[0] * lam_f[1])) - jnp.exp(jnp.sum(lam_f[2] * lam_f[3])) + lam_init
    k_all = jnp.concatenate([k_c, k_l], axis=1)
    v_all = jnp.concatenate([v_c, v_l], axis=1)
    nb = n // Q_BLOCK
    qb = jnp.swapaxes(q_l.reshape(bsz, nb, Q_BLOCK, 2 * DA_HEADS, DA_QK), 0, 1)
    o = lax.map(lambda qi: diff_softmax_attend(qi, k_all, v_all, lam), qb)
    a_lat = da_post(jnp.swapaxes(o, 0, 1).reshape(bsz, n, DA_HEADS, DA_V), pl[3], da_norm_w, lam_init)

    xs_l, b_l, c_l, dt_l = ssm_prep(pl, conv_w, conv_b, dt_bias)
    xs_c, b_c, c_c, dt_c = ssm_prep(pc, conv_w, conv_b, dt_bias)
    a_neg = -jnp.exp(a_log.astype(f32))
    h0_ssm = jnp.zeros((bsz, SSM_GROUPS, SSM_HEADS // SSM_GROUPS, SSM_P, SSM_N), f32)
    ys_c, ys_l = 0.0, 0.0
    for d in range(2):
        scan = functools.partial(ssd_chunked, a=a_neg[d], d_skip=d_skip[d])
        yc, yl = scan_ctx_then_latent(scan, (xs_c, dt_c[:, :, d], b_c, c_c),
                                      (xs_l, dt_l[:, :, d], b_l, c_l), h0_ssm, d == 1)
        ys_c, ys_l = ys_c + yc, ys_l + yl
    s_lat = ssm_post(ys_l, pl[5], ssm_norm_w)

    gq_l, gk_l, gv_l, gg_l = gla_prep(pl, gla_w_gate, gla_b_gate)
    gq_c, gk_c, gv_c, gg_c = gla_prep(pc, gla_w_gate, gla_b_gate)
    h0_gla = jnp.zeros((bsz, GLA_HEADS, GLA_DK, GLA_DV), f32)
    yg_c, yg_l = 0.0, 0.0
    for d in range(2):
        yc, yl = scan_ctx_then_latent(gla_chunked, (gq_c, gk_c, gv_c, gg_c[:, :, d]),
                                      (gq_l, gk_l, gv_l, gg_l[:, :, d]), h0_gla, d == 1)
        yg_c, yg_l = yg_c + yc, yg_l + yl
    g_lat = gla_post(yg_l, pl[12], gla_norm_w)

    x = x + gate * merge_branches(a_lat, s_lat, g_lat, pl, w_out_da, w_out_ssm, w_out_gla, w_o)
    if need_ctx:
        a_ctx = da_post(diff_softmax_attend(q_c, k_c, v_c, lam), pc[3], da_norm_w, lam_init)
        s_ctx = ssm_post(ys_c, pc[5], ssm_norm_w)
        g_ctx = gla_post(yg_c, pc[12], gla_norm_w)
        cx = cx + gate_c * merge_branches(a_ctx, s_ctx, g_ctx, pc, w_out_da, w_out_ssm, w_out_gla, w_o)
    return x, cx


def setup_inputs(seed: int = 0) -> dict:
    key = jax.random.key(seed)
    ks = jax.random.split(key, 26)
    f32 = jnp.float32

    def nrm(k, shape, scale):
        return jax.random.normal(k, shape, f32) * scale

    dt0 = jnp.exp(jax.random.uniform(ks[13], (DEPTH, 2, SSM_HEADS), f32)
                  * (math.log(0.1) - math.log(0.001)) + math.log(0.001))
    return {
        "x": nrm(ks[0], (BATCH, SEQ, D_MODEL), 1.0),
        "c": nrm(ks[1], (BATCH, D_MODEL), 1.0),
        "ctx": nrm(ks[2], (BATCH, CTX_LEN, D_MODEL), 1.0),
        "c_ctx": nrm(ks[3], (D_MODEL,), 1.0),
        "w_mod": nrm(ks[4], (DEPTH, D_MODEL, 3 * D_MODEL), 0.5 * D_MODEL ** -0.5),
        "b_mod": nrm(ks[5], (DEPTH, 3 * D_MODEL), 0.02),
        "norm_w": 1.0 + nrm(ks[6], (DEPTH, D_MODEL), 0.02),
        "w_in": nrm(ks[7], (DEPTH, D_MODEL, IN_WIDTH), D_MODEL ** -0.5),
        "da_lambda": nrm(ks[8], (DEPTH, 4, DA_QK), 0.1),
        "da_norm_w": 1.0 + nrm(ks[9], (DEPTH, DA_V), 0.02),
        "w_out_da": nrm(ks[10], (DEPTH, DA_WIDTH, D_MODEL), DA_WIDTH ** -0.5),
        "ssm_conv_w": nrm(ks[11], (DEPTH, SSM_CONV, SSM_CONV_DIM), SSM_CONV ** -0.5),
        "ssm_conv_b": nrm(ks[12], (DEPTH, SSM_CONV_DIM), 0.02),
        "ssm_dt_bias": dt0 + jnp.log(-jnp.expm1(-dt0)),
        "ssm_a_log": jnp.log(jax.random.uniform(ks[14], (DEPTH, 2, SSM_HEADS), f32, 1.0, 16.0)),
        "ssm_d": 1.0 + nrm(ks[15], (DEPTH, 2, SSM_HEADS), 0.1),
        "ssm_norm_w": 1.0 + nrm(ks[16], (DEPTH, SSM_WIDTH), 0.02),
        "w_out_ssm": nrm(ks[17], (DEPTH, SSM_WIDTH, D_MODEL), SSM_WIDTH ** -0.5),
        "gla_w_gate": nrm(ks[18], (DEPTH, 2, GLA_RANK, GLA_HEADS * GLA_DK), GLA_RANK ** -0.5),
        "gla_b_gate": nrm(ks[19], (DEPTH, 2, GLA_HEADS * GLA_DK), 0.02),
        "gla_norm_w": 1.0 + nrm(ks[20], (DEPTH, GLA_DV), 0.02),
        "w_out_gla": nrm(ks[21], (DEPTH, GLA_WIDTH, D_MODEL), GLA_WIDTH ** -0.5),
        "w_o": nrm(ks[22], (DEPTH, D_MODEL, D_MODEL), D_MODEL ** -0.5),
        "final_norm_w": 1.0 + nrm(ks[23], (D_MODEL,), 0.02),
    }


def reference(x, c, ctx, c_ctx, w_mod, b_mod, norm_w, w_in, da_lambda, da_norm_w, w_out_da,
              ssm_conv_w, ssm_conv_b, ssm_dt_bias, ssm_a_log, ssm_d, ssm_norm_w, w_out_ssm,
              gla_w_gate, gla_b_gate, gla_norm_w, w_out_gla, w_o, final_norm_w):
    n = x.shape[1]
    cos, sin = rope_2d_tables(n, x.dtype)
    cx = ctx
    for l in range(DEPTH):
        lam_init = 0.8 - 0.6 * math.exp(-0.3 * l)
        x, cx = hybrid_layer(x, cx, c, c_ctx, cos, sin, lam_init, l < DEPTH - 1,
                             w_mod[l], b_mod[l], norm_w[l], w_in[l], da_lambda[l], da_norm_w[l], w_out_da[l],
                             ssm_conv_w[l], ssm_conv_b[l], ssm_dt_bias[l], ssm_a_log[l], ssm_d[l],
                             ssm_norm_w[l], w_out_ssm[l],
                             gla_w_gate[l], gla_b_gate[l], gla_norm_w[l], w_out_gla[l], w_o[l])
    return rmsnorm(x, final_norm_w)
```

```python
import math
from bisect import bisect_left
from contextlib import ExitStack

import numpy as np
import ml_dtypes
import concourse.bass as bass
import concourse.mybir as mybir
from concourse.bass_utils import run_bass_kernel_spmd

F32 = mybir.dt.float32
BF16 = mybir.dt.bfloat16
AF = mybir.ActivationFunctionType
ALU = mybir.AluOpType
AX = mybir.AxisListType

D = 1024
CTX = 256
KC = 8
EPS = 1e-6
IN_W = 7984
C_Q, C_K, C_V, C_ZA = 0, 512, 1024, 1536
C_BX, C_BZ, C_BB, C_BC, C_DT = 2048, 2560, 3072, 3200, 3328
C_GQ, C_GK, C_GV, C_GZ, C_LR = 3344, 3600, 3856, 4368, 4880
C_SG = 4912
DEPTH = 2


class Res:
    __slots__ = ("writers", "rd", "rdma", "excl")

    def __init__(self, excl=False):
        self.writers = []
        self.rd = {}
        self.rdma = []
        self.excl = excl


class Prog:
    def __init__(self, nc, es, n_dma=48):
        self.nc = nc
        self.E = {"pe": nc.tensor, "act": nc.scalar, "dve": nc.vector, "pool": nc.gpsimd, "sp": nc.sync}
        self.sem = {k: es.enter_context(nc.semaphore("s_" + k)) for k in ("pe", "act", "dve", "pool")}
        self.dsem = [es.enter_context(nc.semaphore("d%d" % i)) for i in range(n_dma)]
        self.dcnt = [0] * n_dma
        self.drr = 0
        self.cnt = {k: 0 for k in self.sem}
        self.nops = {k: 0 for k in self.sem}
        self.sigs = {k: ([], []) for k in self.sem}
        self.waited = {k: {} for k in self.E}
        self.pending = {k: [] for k in self.E}
        self.lastop = {k: None for k in self.sem}
        self.dma_open = []
        self.n_ins = 0

    def _resolve(self, tok):
        if tok[0] == "d":
            return ("d", tok[1]), tok[2]
        _, eng, idx = tok
        idxs, vals = self.sigs[eng]
        j = bisect_left(idxs, idx)
        assert j < len(idxs), "dependency on a non-signalling op with no later signal on " + eng
        return eng, vals[j]

    def op(self, eng, fn, reads=(), writes=(), sig=True, partial=False, dma=False):
        toks = []
        raw = []
        for r in reads:
            raw += r.writers
            if r.excl:
                toks += list(r.rd.values())
        toks += raw
        for w in writes:
            toks += w.writers
            toks += list(w.rd.values())
            toks += w.rdma
        toks += self.pending[eng]
        self.pending[eng] = []
        need = {}
        for t in toks:
            if t[0] == "c" and t[1] == eng and t not in raw:
                continue
            key, v = self._resolve(t)
            if need.get(key, 0) < v:
                need[key] = v
        k = None
        if dma:
            k = self.drr
            self.drr = (k + 1) % len(self.dsem)
            if self.dcnt[k]:
                need[("d", k)] = max(need.get(("d", k), 0), 16 * self.dcnt[k])
        E = self.E[eng]
        wd = self.waited[eng]
        for key, v in need.items():
            if wd.get(key, 0) < v:
                E.wait_ge(self.sem[key] if isinstance(key, str) else self.dsem[key[1]], v)
                wd[key] = v
        ins = fn(E)
        self.n_ins += 1
        if dma:
            self.dcnt[k] += 1
            ins.then_inc(self.dsem[k], 16)
            tok = ("d", k, 16 * self.dcnt[k])
            self.dma_open.append(tok)
        else:
            idx = self.nops[eng]
            self.nops[eng] += 1
            tok = ("c", eng, idx)
            if sig:
                self.cnt[eng] += 1
                ins.then_inc(self.sem[eng], 1)
                self.sigs[eng][0].append(idx)
                self.sigs[eng][1].append(self.cnt[eng])
            self.lastop[eng] = tok
        for r in reads:
            if dma:
                r.rdma.append(tok)
            else:
                r.rd[eng] = tok
        for w in writes:
            if partial:
                w.writers.append(tok)
            else:
                w.writers = [tok]
                w.rd = {}
                w.rdma = []
        return tok

    def barrier(self):
        toks = [t for t in self.lastop.values() if t is not None] + self.dma_open
        for e in self.E:
            self.pending[e] = self.pending[e] + toks
        self.dma_open = []

    def flush(self, eng):
        need = {}
        for t in self.pending[eng]:
            key, v = self._resolve(t)
            if need.get(key, 0) < v:
                need[key] = v
        self.pending[eng] = []
        E = self.E[eng]
        wd = self.waited[eng]
        for key, v in need.items():
            if wd.get(key, 0) < v:
                E.wait_ge(self.sem[key] if isinstance(key, str) else self.dsem[key[1]], v)
                wd[key] = v

    def dma(self, out, in_, reads=(), writes=(), q="sp", partial=False):
        return self.op(q, lambda e: e.dma_start(out=out, in_=in_), reads, writes, dma=True, partial=partial)

    def mm(self, out, lhsT, rhs, start, stop, reads=(), writes=(), sig=None, partial=None, **kw):
        if sig is None:
            sig = stop
        if partial is None:
            partial = not start
        return self.op("pe", lambda e: e.matmul(out, lhsT=lhsT, rhs=rhs, start=start, stop=stop, **kw), reads, writes,
                       sig=sig, partial=partial)

    def tr(self, out, in_, ident, reads=(), writes=(), sig=True, partial=False):
        return self.op("pe", lambda e: e.transpose(out=out, in_=in_, identity=ident), reads, writes, sig=sig, partial=partial)

    def act(self, out, in_, func, reads=(), writes=(), partial=False, **kw):
        return self.op("act", lambda e: e.activation(out=out, in_=in_, func=func, **kw), reads, writes, partial=partial)

    def tt(self, eng, out, in0, in1, op, reads=(), writes=(), partial=False):
        return self.op(eng, lambda e: e.tensor_tensor(out=out, in0=in0, in1=in1, op=op), reads, writes, partial=partial)

    def ts(self, eng, out, in0, s1, s2, op0, op1=None, reads=(), writes=(), partial=False, **kw):
        if op1 is None:
            return self.op(eng, lambda e: e.tensor_scalar(out=out, in0=in0, scalar1=s1, scalar2=None, op0=op0, **kw), reads, writes, partial=partial)
        return self.op(eng, lambda e: e.tensor_scalar(out=out, in0=in0, scalar1=s1, scalar2=s2, op0=op0, op1=op1, **kw), reads, writes, partial=partial)

    def stt(self, out, in0, scalar, in1, op0, op1, reads=(), writes=(), partial=False):
        return self.op("dve", lambda e: e.scalar_tensor_tensor(out=out, in0=in0, scalar=scalar, in1=in1, op0=op0, op1=op1), reads, writes, partial=partial)

    def cp(self, eng, out, in_, reads=(), writes=(), partial=False):
        if eng == "act":
            return self.op("act", lambda e: e.copy(out=out, in_=in_), reads, writes, partial=partial)
        return self.op(eng, lambda e: e.tensor_copy(out=out, in_=in_), reads, writes, partial=partial)


class Ring:
    def __init__(self, aps):
        self.aps = aps
        self.res = [Res() for _ in aps]
        self.i = 0

    def next(self):
        j = self.i % len(self.aps)
        self.i += 1
        return self.aps[j], self.res[j]


def rope_tables(n):
    rows = n // 64
    row = np.repeat(np.arange(rows), 64)
    col = np.tile(np.arange(64), rows)
    pos = np.stack([row, col], axis=-1).astype(np.float32)
    nf = 16
    inv = (np.float32(10000.0) ** (-np.arange(nf, dtype=np.float32) / np.float32(nf))).astype(np.float32)
    ang = np.broadcast_to(pos[:, :, None, None] * inv, (n, 2, 2, nf)).reshape(n, 64).astype(np.float32)
    cos = np.cos(ang).astype(np.float32)
    sin = np.sin(ang).astype(np.float32)
    sgn = np.tile(np.concatenate([-np.ones(16), np.ones(16)]), 2).astype(np.float32)
    sin = sin * sgn[None, :]
    cosT = np.ascontiguousarray(np.concatenate([cos.T, cos.T], axis=0))
    sinT = np.ascontiguousarray(np.concatenate([sin.T, sin.T], axis=0))
    return cosT, sinT


def host_consts(TL):
    cosT, sinT = rope_tables(TL)
    k = np.arange(128)
    tri_f = (k[:, None] <= k[None, :]).astype(np.float32)
    tri_b = (k[:, None] >= k[None, :]).astype(np.float32)
    blk = (k[:, None] // 64) == (k[None, :] // 64)
    gm_f = (tri_f * blk).astype(np.float32)
    gm_b = (tri_b * blk).astype(np.float32)
    rst = np.ones((128, 512), np.float32)
    rst[:, ::64] = 0.0
    cmat = np.concatenate([np.eye(128, dtype=np.float32), tri_f, tri_b, gm_f, gm_b, np.ones((128, 128), np.float32)], axis=1)
    return {"cst_cos": cosT, "cst_sin": sinT, "cst_mat": np.ascontiguousarray(cmat), "cst_rst": rst}


WEIGHT_SPECS = [
    ("w_mod", [DEPTH, D, 3 * D]), ("b_mod", [DEPTH, 3 * D]), ("norm_w", [DEPTH, D]), ("w_in", [DEPTH, D, IN_W]),
    ("da_lambda", [DEPTH, 4, 64]), ("da_norm_w", [DEPTH, 128]), ("w_out_da", [DEPTH, 512, D]),
    ("ssm_conv_w", [DEPTH, 3, 768]), ("ssm_conv_b", [DEPTH, 768]), ("ssm_dt_bias", [DEPTH, 2, 8]),
    ("ssm_a_log", [DEPTH, 2, 8]), ("ssm_d", [DEPTH, 2, 8]), ("ssm_norm_w", [DEPTH, 512]), ("w_out_ssm", [DEPTH, 512, D]),
    ("gla_w_gate", [DEPTH, 2, 16, 256]), ("gla_b_gate", [DEPTH, 2, 256]), ("gla_norm_w", [DEPTH, 128]),
    ("w_out_gla", [DEPTH, 512, D]), ("w_o", [DEPTH, D, D]), ("final_norm_w", [D]),
]


def build(TL, n_layers=DEPTH, dbg=(), stop_after=None):
    TALL = CTX + TL
    NT = TALL // 128
    NCH = TALL // 64
    nc = bass.Bass("TRN2", target_bir_lowering=False)

    def din(name, shape):
        return nc.dram_tensor(name, list(shape), F32, kind="ExternalInput").ap()

    _cnt = [0]

    def SBT(name, shape, dt):
        _cnt[0] += 1
        return nc.sbuf_tensor("%s_%d" % (name, _cnt[0]), shape, dt)

    x_in = din("x", [TL, D])
    c_in = din("c", [D])
    ctx_in = din("ctx", [CTX, D])
    cctx_in = din("c_ctx", [D])
    W = {n: din(n, s) for n, s in WEIGHT_SPECS}
    cst_cos = din("cst_cos", [128, TL])
    cst_sin = din("cst_sin", [128, TL])
    cst_mat = din("cst_mat", [128, 6 * 128])
    cst_rst = din("cst_rst", [128, 512])
    y_out = nc.dram_tensor("y", [TL, D], F32, kind="ExternalOutput").ap()

    def scr(name, shape, dt):
        kind = "ExternalOutput" if name in dbg else "Internal"
        return nc.dram_tensor(name, list(shape), dt, kind=kind).ap()

    S = dict(
        QT=scr("QT", [4, 128, TALL], BF16), KT=scr("KT", [4, 128, TALL], BF16), V=scr("V", [TALL, 512], BF16),
        ZAT=scr("ZAT", [512, TALL], BF16), ZB=scr("ZB", [TALL, 512], BF16), ZC=scr("ZC", [TALL, 512], BF16),
        XBC=scr("XBC", [768, TALL], F32), DT=scr("DT", [TALL, 16], F32), GV=scr("GV", [TALL, 512], BF16),
        GQ=scr("GQ", [2, 256, TALL], BF16), GK=scr("GK", [2, 256, TALL], BF16), GKE=scr("GKE", [2, TALL, 256], BF16),
        GEL=scr("GEL", [2, 256, NCH], F32), SG=scr("SG", [3, D, TALL], BF16),
        AOT=scr("AOT", [512, TALL], BF16), SO=scr("SO", [TALL, 512], BF16), GO=scr("GO", [TALL, 512], BF16),
        XR=scr("XR", [TL, D], F32), CXR=scr("CXR", [CTX, D], F32),
        XS=scr("XS", [TALL, 512], BF16), BT=scr("BT", [128, TALL], BF16), CT=scr("CT", [128, TALL], BF16),
        BTM=scr("BTM", [TALL, 128], BF16), YF=scr("YF", [TALL, 512], F32), GYF=scr("GYF", [TALL, 512], F32),
        MODT=scr("MODT", [128, 64], F32),
    )

    es = ExitStack()
    with es:
        es.enter_context(nc.allow_non_contiguous_dma(reason="tiny transposed parameter loads"))
        P = Prog(nc, es)
        cmat32 = es.enter_context(SBT("cmat32", [128, 768], F32))
        cmatb = es.enter_context(SBT("cmatb", [128, 768], BF16))
        rst = es.enter_context(SBT("rst", [128, 512], F32))
        cs = es.enter_context(SBT("cs", [128, 8, 2], F32))
        csb = es.enter_context(SBT("csb", [128, 8, 2, 128], F32))
        modA = es.enter_context(SBT("modA", [128, 2, 8], F32))
        modB = es.enter_context(SBT("modB", [128, 2, 8], F32))
        gate_bc = es.enter_context(SBT("gate_bc", [128, 2, D], F32))
        PSALL = es.enter_context(nc.psum_tensor("psall", [128, 8, 512], F32))
        psb = [PSALL[:, i, :] for i in range(8)]
        psr = [Res(excl=True) for _ in range(8)]
        ident32, trif32, trib32 = cmat32[:, 0:128], cmat32[:, 128:256], cmat32[:, 256:384]
        ones32 = cmat32[:, 640:768]
        identb = cmatb[:, 0:128]
        gmfb, gmbb = cmatb[:, 384:512], cmatb[:, 512:640]

        r0 = Res()
        P.dma(cmat32[:], cst_mat[:, :], writes=[r0])
        P.cp("dve", cmatb[:], cmat32[:], reads=[r0], writes=[Res()])
        P.dma(rst[:], cst_rst[:, :], writes=[Res()])
        r1 = Res()
        P.dma(cs[:, :, 0], c_in.rearrange("(k p) -> p k", p=128), writes=[r1], partial=True)
        P.dma(cs[:, :, 1], cctx_in.rearrange("(k p) -> p k", p=128), writes=[r1], partial=True)
        P.act(cs[:], cs[:], AF.Silu, reads=[r1], writes=[r1])
        for who in range(2):
            P.cp("dve", csb[:, :, who, :], cs[:, :, who:who + 1].to_broadcast([128, 8, 128]), reads=[r1], writes=[Res()])
        P.barrier()

        psi = [0]

        def nextps():
            j = psi[0] % 8
            psi[0] += 1
            return psb[j], psr[j]

        def tok_src(layer, i):
            if layer == 0:
                return ctx_in[i * 128:(i + 1) * 128, :] if i < 2 else x_in[(i - 2) * 128:(i - 1) * 128, :]
            return S["CXR"][i * 128:(i + 1) * 128, :] if i < 2 else S["XR"][(i - 2) * 128:(i - 1) * 128, :]

        groups = [(0, CTX, True)] + [(CTX + 512 * g, 512, False) for g in range(TL // 512)]

        for layer in range(n_layers):
            need_ctx = layer < DEPTH - 1
            w_in = W["w_in"][layer].rearrange("(k p) c -> p k c", p=128)
            with ExitStack() as ph:
                wm = [ph.enter_context(SBT("wm%d" % i, [128, 8, 512], F32)) for i in range(2)]
                wmr = Ring([t[:] for t in wm])
                bmT = ph.enter_context(SBT("bmT", [128, 24], F32))
                nwT = ph.enter_context(SBT("nwT", [128, 8], F32))
                bmg = ph.enter_context(SBT("bmg", [128, D], F32))
                modT = ph.enter_context(SBT("modT", [128, 24, 2], F32))
                rb, rn, rg, rm = Res(), Res(), Res(), Res()
                P.dma(bmT[:], W["b_mod"][layer].rearrange("(j p) -> p j", p=128), writes=[rb])
                P.dma(nwT[:], W["norm_w"][layer].rearrange("(j p) -> p j", p=128), writes=[rn])
                P.dma(bmg[:], W["b_mod"][layer][2 * D:3 * D].partition_broadcast(128), writes=[rg])
                w_mod = W["w_mod"][layer].rearrange("(k p) c -> p k c", p=128)
                for t in range(6):
                    wt, wr = wmr.next()
                    P.dma(wt, w_mod[:, :, t * 512:(t + 1) * 512], writes=[wr])
                    for jj in range(4):
                        j = t * 4 + jj
                        ps, pr = nextps()
                        for k in range(KC):
                            P.mm(ps[:, 0:2], lhsT=wt[:, k, jj * 128:(jj + 1) * 128], rhs=cs[:, k, :], start=(k == 0), stop=(k == KC - 1),
                                 reads=[wr], writes=[pr])
                        P.cp("dve", modT[:, j, :], ps[:, 0:2], reads=[pr], writes=[rm], partial=True)
                    if t >= 4:
                        for who in range(2 if need_ctx else 1):
                            ps, pr = nextps()
                            for k in range(KC):
                                P.mm(ps[:], lhsT=csb[:, k, who, :], rhs=wt[:, k, :], start=(k == 0), stop=(k == KC - 1),
                                     reads=[wr], writes=[pr])
                            P.tt("dve", gate_bc[:, who, (t - 4) * 512:(t - 3) * 512], ps[:], bmg[:, (t - 4) * 512:(t - 3) * 512], ALU.add,
                                 reads=[pr, rg], writes=[Res()])
                ra = Res()
                for who in range(2):
                    P.tt("dve", modB[:, who, :], modT[:, 0:8, who], bmT[:, 0:8], ALU.add, reads=[rm, rb], writes=[ra], partial=True)
                    P.tt("dve", modA[:, who, :], modT[:, 8:16, who], bmT[:, 8:16], ALU.add, reads=[rm, rb], writes=[ra], partial=True)
                    P.stt(modA[:, who, :], modA[:, who, :], 1.0, nwT[:], ALU.add, ALU.mult, reads=[ra, rn], writes=[ra])
                if "MODT" in dbg:
                    P.dma(S["MODT"][:, 0:16], modA[:].rearrange("p a b -> p (a b)"), reads=[ra])
                    P.dma(S["MODT"][:, 16:32], modB[:].rearrange("p a b -> p (a b)"), reads=[ra])
                P.barrier()
            if stop_after == "mod":
                break

            with ExitStack() as ph:
                hT = ph.enter_context(SBT("hT", [128, KC, TALL], BF16))
                with ExitStack() as ph2:
                    xts = [ph2.enter_context(SBT("xt%d" % i, [128, D], F32)) for i in range(2)]
                    xns = [ph2.enter_context(SBT("xn%d" % i, [128, D], BF16)) for i in range(2)]
                    junk = ph2.enter_context(SBT("junk", [128, D], BF16))
                    sst = ph2.enter_context(SBT("sst", [128, 2, 4], F32))
                    xr_ = Ring([t[:] for t in xts])
                    xn_ = Ring([t[:] for t in xns])
                    ss_ = Ring([sst[:, i, :] for i in range(2)])
                    jr = Res()
                    for i in range(NT):
                        xt, xr = xr_.next()
                        xn, xnr = xn_.next()
                        st, sr = ss_.next()
                        import os
                        stage = int(os.environ.get("DBG_1A", "9"))
                        P.dma(xt, tok_src(layer, i), writes=[xr])
                        if stage < 1: continue
                        P.act(junk[:], xt, AF.Square, reads=[xr], writes=[jr, sr], accum_out=st[:, 0:1])
                        if stage < 2: continue
                        P.act(st[:, 1:2], st[:, 0:1], AF.Sqrt, reads=[sr], writes=[sr], scale=1.0 / D, bias=EPS)
                        if stage < 3: continue
                        P.op("dve", lambda e, st=st: e.reciprocal(out=st[:, 2:3], in_=st[:, 1:2]), reads=[sr], writes=[sr])
                        P.ts("dve", xn, xt, st[:, 2:3], None, ALU.mult, reads=[xr, sr], writes=[xnr])
                        if stage < 4: continue
                        ps, pr = nextps()
                        pb = ps.bitcast(BF16).rearrange("p (j t) -> p j t", t=128)
                        for j in range(KC):
                            P.tr(pb[:, j, :], xn[:, j * 128:(j + 1) * 128], identb, reads=[xnr], writes=[pr], sig=(j == KC - 1), partial=(j > 0))
                        if stage < 5: continue
                        who = 1 if i < 2 else 0
                        for j in range(KC):
                            dst = hT[:, j, i * 128:(i + 1) * 128]
                            if i % 2 == 0:
                                P.act(dst, pb[:, j, :], AF.Identity, reads=[pr], scale=modA[:, who, j:j + 1], bias=modB[:, who, j:j + 1])
                            else:
                                P.ts("dve", dst, pb[:, j, :], modA[:, who, j:j + 1], modB[:, who, j:j + 1], ALU.mult, ALU.add, reads=[pr])
                    P.barrier()
                if stop_after == "norm":
                    break

                with ExitStack() as ph2:
                    wts = [ph2.enter_context(SBT("wt%d" % i, [128, KC, 512], BF16)) for i in range(2)]
                    wring = Ring([t[:] for t in wts])
                    stg = [ph2.enter_context(SBT("stg%d" % i, [128, 512], F32)) for i in range(4)]
                    sring = Ring([t[:] for t in stg])
                    sgb = [ph2.enter_context(SBT("sgb%d" % i, [128, 512], BF16)) for i in range(4)]
                    bring = Ring([t[:] for t in sgb])

                    def load_w(c0, n):
                        wt, wr = wring.next()
                        P.dma(wt[:, :, 0:n], w_in[:, :, c0:c0 + n], writes=[wr], q="pool")
                        return wt, wr

                    def mm_fm(wt, wr, cc, ncol, t0, n):
                        ps, pr = nextps()
                        for k in range(KC):
                            P.mm(ps[0:ncol, 0:n], lhsT=wt[:, k, cc:cc + ncol], rhs=hT[:, k, t0:t0 + n], start=(k == 0), stop=(k == KC - 1),
                                 reads=[wr], writes=[pr])
                        return ps, pr

                    def mm_tm(wt, wr, ncol, i):
                        ps, pr = nextps()
                        for k in range(KC):
                            P.mm(ps[:, 0:ncol], lhsT=hT[:, k, i * 128:(i + 1) * 128], rhs=wt[:, k, 0:ncol], start=(k == 0), stop=(k == KC - 1),
                                 reads=[wr], writes=[pr])
                        return ps, pr

                    for (c0, name, func) in ((C_V, "V", AF.Copy), (C_BZ, "ZB", AF.Silu), (C_GZ, "ZC", AF.Silu), (C_GV, "GV", AF.Copy)):
                        wt, wr = load_w(c0, 512)
                        for i in range(NT):
                            ps, pr = mm_tm(wt, wr, 512, i)
                            sb, sr = bring.next()
                            if func == AF.Copy and i % 2 == 1:
                                P.cp("dve", sb, ps[:], reads=[pr], writes=[sr])
                            else:
                                P.act(sb, ps[:], func, reads=[pr], writes=[sr])
                            P.dma(S[name][i * 128:(i + 1) * 128, :], sb, reads=[sr])
                    with ExitStack() as ph3:
                        dtb = ph3.enter_context(SBT("dtb", [128, 16], F32))
                        dtt = ph3.enter_context(SBT("dtt", [128, 4, 16], F32))
                        dring = Ring([dtt[:, i, :] for i in range(4)])
                        rdb = Res()
                        P.dma(dtb[:], W["ssm_dt_bias"][layer].rearrange("a b -> (a b)").partition_broadcast(128), writes=[rdb])
                        wt, wr = load_w(C_DT, 16)
                        for i in range(NT):
                            ps, pr = mm_tm(wt, wr, 16, i)
                            d_, dr = dring.next()
                            P.tt("dve", d_, ps[:, 0:16], dtb[:], ALU.add, reads=[pr, rdb], writes=[dr])
                            P.act(d_, d_, AF.Exp, reads=[dr], writes=[dr])
                            P.act(d_, d_, AF.Ln, reads=[dr], writes=[dr], bias=1.0)
                            P.dma(S["DT"][i * 128:(i + 1) * 128, :], d_, reads=[dr])
                    for (c0, n, row0) in ((C_BX, 512, 0), (C_BB, 256, 512)):
                        wt, wr = load_w(c0, n)
                        for (t0, nt, isc) in groups:
                            for cc in range(n // 128):
                                ps, pr = mm_fm(wt, wr, cc * 128, 128, t0, nt)
                                sb, sr = sring.next()
                                P.act(sb[:, 0:nt], ps[:, 0:nt], AF.Copy, reads=[pr], writes=[sr])
                                P.dma(S["XBC"][row0 + cc * 128:row0 + (cc + 1) * 128, t0:t0 + nt], sb[:, 0:nt], reads=[sr])
                    wt, wr = load_w(C_ZA, 512)
                    for (t0, nt, isc) in groups:
                        if isc and not need_ctx:
                            continue
                        for cc in range(4):
                            ps, pr = mm_fm(wt, wr, cc * 128, 128, t0, nt)
                            sb, sr = bring.next()
                            P.act(sb[:, 0:nt], ps[:, 0:nt], AF.Silu, reads=[pr], writes=[sr])
                            P.dma(S["ZAT"][cc * 128:(cc + 1) * 128, t0:t0 + nt], sb[:, 0:nt], reads=[sr])
                    for t in range(6):
                        wt, wr = load_w(C_SG + t * 512, 512)
                        for (t0, nt, isc) in groups:
                            if isc and not need_ctx:
                                continue
                            for cc in range(4):
                                ps, pr = mm_fm(wt, wr, cc * 128, 128, t0, nt)
                                sb, sr = bring.next()
                                P.act(sb[:, 0:nt], ps[:, 0:nt], AF.Sigmoid, reads=[pr], writes=[sr])
                                r_ = t * 512 + cc * 128
                                P.dma(S["SG"][r_ // D][r_ % D:r_ % D + 128, t0:t0 + nt], sb[:, 0:nt], reads=[sr])

                    with ExitStack() as ph3:
                        wg32 = ph3.enter_context(SBT("wg32", [16, 2, 256], F32))
                        wgb = ph3.enter_context(SBT("wgb", [16, 2, 256], BF16))
                        nbg = ph3.enter_context(SBT("nbg", [128, 2, 2], F32))
                        lrb = [ph3.enter_context(SBT("lrb%d" % i, [16, 512], BF16)) for i in range(2)]
                        qks = [ph3.enter_context(SBT("qks%d" % i, [128, 512], F32)) for i in range(4)]
                        tmpf = [ph3.enter_context(SBT("gtmp%d" % i, [128, 512], F32)) for i in range(5)]
                        tring = Ring([t[:] for t in tmpf])
                        egs = ph3.enter_context(SBT("egs", [128, 4, 8], F32))
                        ering = Ring([egs[:, i, :] for i in range(4)])
                        rw, rnb = Res(), Res()
                        P.dma(wg32[:], W["gla_w_gate"][layer].rearrange("d r c -> r d c"), writes=[rw])
                        P.cp("dve", wgb[:], wg32[:], reads=[rw], writes=[rw])
                        P.dma(nbg[:], W["gla_b_gate"][layer].rearrange("d (h p) -> p d h", p=128), writes=[rnb])
                        P.ts("dve", nbg[:], nbg[:], -1.0, None, ALU.mult, reads=[rnb], writes=[rnb])
                        wgq, wgqr = load_w(C_GQ, 256)
                        wgk, wgkr = load_w(C_GK, 256)
                        wlr_t = ph3.enter_context(SBT("wlr", [128, KC, 32], BF16))
                        wlr, wlrr = wlr_t[:], Res()
                        P.dma(wlr, w_in[:, :, C_LR:C_LR + 32], writes=[wlrr], q="pool")
                        lrr = [Res(), Res()]
                        qkr = [Res() for _ in range(4)]
                        for (t0, n, isc) in groups:
                            ncks = n // 64
                            for d in range(2):
                                ps, pr = mm_fm(wlr, wlrr, d * 16, 16, t0, n)
                                P.cp("dve", lrb[d][:, 0:n], ps[0:16, 0:n], reads=[pr], writes=[lrr[d]])
                            for hp in range(2):
                                ps, pr = mm_fm(wgq, wgqr, hp * 128, 128, t0, n)
                                P.act(qks[hp][:, 0:n], ps[:, 0:n], AF.Copy, reads=[pr], writes=[qkr[hp]])
                                ps, pr = mm_fm(wgk, wgkr, hp * 128, 128, t0, n)
                                P.act(qks[2 + hp][:, 0:n], ps[:, 0:n], AF.Copy, reads=[pr], writes=[qkr[2 + hp]])
                            for d in range(2):
                                for hp in range(2):
                                    ps, pr = nextps()
                                    P.mm(ps[:, 0:n], lhsT=wgb[:, d, hp * 128:(hp + 1) * 128], rhs=lrb[d][:, 0:n], start=True, stop=True,
                                         reads=[rw, lrr[d]], writes=[pr])
                                    A_, ar = tring.next()
                                    B_, br = tring.next()
                                    P.act(A_[:, 0:n], ps[:, 0:n], AF.Exp, reads=[pr, rnb], writes=[ar], scale=-1.0, bias=nbg[:, d, hp:hp + 1])
                                    P.act(A_[:, 0:n], A_[:, 0:n], AF.Ln, reads=[ar], writes=[ar], bias=1.0)
                                    if d == 0:
                                        so_, si_ = B_[:, 0:n], A_[:, 0:n]
                                    else:
                                        so_, si_ = B_[:, 0:n][:, ::-1], A_[:, 0:n][:, ::-1]
                                    P.op("dve", lambda e, so_=so_, si_=si_, n=n: e.tensor_tensor_scan(out=so_, data0=rst[:, 0:n], data1=si_, initial=0.0,
                                                                                         op0=ALU.mult, op1=ALU.add), reads=[ar], writes=[br])
                                    Bv = B_[:, 0:n].rearrange("p (c t) -> p c t", t=64)
                                    gl = Bv[:, :, 63:64] if d == 0 else Bv[:, :, 0:1]
                                    C_, cr_ = tring.next()
                                    P.act(C_[:, 0:n], B_[:, 0:n], AF.Exp, reads=[br], writes=[cr_], scale=-1.0 / 16.0)
                                    ob, obr = bring.next()
                                    P.stt(ob[:, 0:n], qks[hp][:, 0:n], 0.125, C_[:, 0:n], ALU.mult, ALU.mult, reads=[qkr[hp], cr_], writes=[obr])
                                    P.dma(S["GQ"][d][hp * 128:(hp + 1) * 128, t0:t0 + n], ob[:, 0:n], reads=[obr])
                                    C2, cr2 = tring.next()
                                    P.act(C2[:, 0:n], B_[:, 0:n], AF.Exp, reads=[br], writes=[cr2], scale=1.0 / 16.0)
                                    ob, obr = bring.next()
                                    P.tt("dve", ob[:, 0:n], qks[2 + hp][:, 0:n], C2[:, 0:n], ALU.mult, reads=[qkr[2 + hp], cr2], writes=[obr])
                                    P.dma(S["GK"][d][hp * 128:(hp + 1) * 128, t0:t0 + n], ob[:, 0:n], reads=[obr])
                                    D_, dr_ = tring.next()
                                    P.tt("dve", D_[:, 0:n].rearrange("p (c t) -> p c t", t=64), gl.to_broadcast([128, ncks, 64]), Bv, ALU.subtract,
                                         reads=[br], writes=[dr_])
                                    P.act(D_[:, 0:n], D_[:, 0:n], AF.Exp, reads=[dr_], writes=[dr_], scale=-1.0 / 16.0)
                                    ke, ker = bring.next()
                                    P.tt("dve", ke[:, 0:n], qks[2 + hp][:, 0:n], D_[:, 0:n], ALU.mult, reads=[qkr[2 + hp], dr_], writes=[ker])
                                    ps, pr = nextps()
                                    pb = ps.bitcast(BF16).rearrange("p (j t) -> p j t", t=128)
                                    nst = n // 128
                                    for j in range(nst):
                                        P.tr(pb[:, j, :], ke[:, j * 128:(j + 1) * 128], identb, reads=[ker], writes=[pr], sig=(j == nst - 1), partial=(j > 0))
                                    kt_, ktr = bring.next()
                                    P.cp("dve", kt_[:, 0:n], ps.bitcast(BF16)[:, 0:n], reads=[pr], writes=[ktr])
                                    P.dma(S["GKE"][d][t0:t0 + n, hp * 128:(hp + 1) * 128].rearrange("(j p) c -> p j c", p=128),
                                          kt_[:, 0:n].rearrange("p (j c) -> p j c", c=128), reads=[ktr])
                                    eg, egr = ering.next()
                                    P.act(eg[:, 0:ncks], gl.rearrange("p c o -> p (c o)"), AF.Exp, reads=[br], writes=[egr], scale=-1.0 / 16.0)
                                    P.dma(S["GEL"][d][hp * 128:(hp + 1) * 128, t0 // 64:t0 // 64 + ncks], eg[:, 0:ncks], reads=[egr])
                    with ExitStack() as ph3:
                        cst = [ph3.enter_context(SBT("cst%d" % i, [128, 2, 512], F32)) for i in range(2)]
                        cring = Ring([t[:] for t in cst])
                        for (c0, name) in ((C_Q, "QT"), (C_K, "KT")):
                            wt, wr = load_w(c0, 512)
                            wp, wpr = wring.next()
                            wv = wt.rearrange("p k (a h f) -> p k a h f", h=2, f=16)
                            wpv = wp.rearrange("p k (a h f) -> p k a h f", h=2, f=16)
                            for k in range(KC):
                                for h in range(2):
                                    P.cp("dve" if (k + h) % 2 else "pool", wpv[:, k, :, h, :], wv[:, k, :, 1 - h, :], reads=[wr], writes=[wpr], partial=(k + h > 0))
                            for (t0, nt, isc) in groups:
                                if not isc:
                                    ct, cr = cring.next()
                                    P.dma(ct[:, 0, :], cst_cos[:, t0 - CTX:t0 - CTX + 512], writes=[cr], partial=True)
                                    P.dma(ct[:, 1, :], cst_sin[:, t0 - CTX:t0 - CTX + 512], writes=[cr], partial=True)
                                for cp_ in range(4):
                                    ps, pr = mm_fm(wt, wr, cp_ * 128, 128, t0, nt)
                                    ob, obr = bring.next()
                                    if isc:
                                        P.act(ob[:, 0:nt], ps[:, 0:nt], AF.Copy, reads=[pr], writes=[obr])
                                    else:
                                        ps2, pr2 = mm_fm(wp, wpr, cp_ * 128, 128, t0, nt)
                                        s1, s1r = sring.next()
                                        s2, s2r = sring.next()
                                        P.tt("dve", s1, ps[:], ct[:, 0, :], ALU.mult, reads=[pr, cr], writes=[s1r])
                                        P.tt("dve", s2, ps2[:], ct[:, 1, :], ALU.mult, reads=[pr2, cr], writes=[s2r])
                                        P.tt("pool", ob, s1, s2, ALU.add, reads=[s1r, s2r], writes=[obr])
                                    P.dma(S[name][cp_][:, t0:t0 + nt], ob[:, 0:nt], reads=[obr])
                    P.barrier()
            if stop_after == "proj":
                break


            lam_init = 0.8 - 0.6 * math.exp(-0.3 * layer)
            with ExitStack() as ph:
                KTs = ph.enter_context(SBT("KTs", [128, 4, TALL], BF16))
                Vs = ph.enter_context(SBT("Vs", [128, NT, 4, 130], BF16))
                lamt = ph.enter_context(SBT("lamt", [128, 4, 64], F32))
                lsc = ph.enter_context(SBT("lsc", [128, 8], F32))
                nwc = ph.enter_context(SBT("nwc", [128, 1], F32))

                def mk(name, shape, dt, n):
                    ts_ = [ph.enter_context(SBT("%s%d" % (name, i), shape, dt)) for i in range(n)]
                    return Ring([t[:] for t in ts_])
                qring = mk("qt", [128, 2, 256], BF16, 3)
                for qa, qres in zip(qring.aps, qring.res):
                    P.op("pool", lambda e, qa=qa: e.memset(qa, 0.0), writes=[qres])
                pring = mk("pt", [128, 2, 256], BF16, 4)
                zring = mk("za", [128, 256], BF16, 3)
                rsring = mk("ars", [128, 512], F32, 2)
                t0ring = mk("at0", [128, 512], F32, 2)
                oring = mk("ao_", [128, 256], F32, 2)
                sqring = mk("asq", [128, 256], BF16, 2)
                msring = mk("ams", [128, 256], F32, 2)
                aoring = mk("aob", [128, 256], BF16, 2)
                rk, rv, rl, rnw = Res(), Res(), Res(), Res()
                for h in range(4):
                    P.dma(KTs[:, h, :], S["KT"][h], writes=[rk], partial=True)
                for i in range(NT):
                    P.dma(Vs[:, i, :, 0:128], S["V"][i * 128:(i + 1) * 128, :].rearrange("p (h v) -> p h v", v=128), writes=[rv], partial=True)
                P.dma(lamt[:], W["da_lambda"][layer].rearrange("a b -> (a b)").partition_broadcast(128), writes=[rl])
                P.dma(nwc[:], W["da_norm_w"][layer].rearrange("(p o) -> p o", o=1), writes=[rnw])
                P.ts("dve", nwc[:], nwc[:], 1.0 - lam_init, None, ALU.mult, reads=[rnw], writes=[rnw])
                P.stt(lamt[:, 0, :], lamt[:, 0, :], 1.0, lamt[:, 1, :], ALU.mult, ALU.mult, reads=[rl], writes=[rl])
                P.stt(lamt[:, 2, :], lamt[:, 2, :], 1.0, lamt[:, 3, :], ALU.mult, ALU.mult, reads=[rl], writes=[rl])
                P.op("dve", lambda e: e.reduce_sum(out=lsc[:, 0:1], in_=lamt[:, 0, :], axis=AX.X), reads=[rl], writes=[rl])
                P.op("dve", lambda e: e.reduce_sum(out=lsc[:, 1:2], in_=lamt[:, 2, :], axis=AX.X), reads=[rl], writes=[rl])
                P.act(lsc[:, 2:4], lsc[:, 0:2], AF.Exp, reads=[rl], writes=[rl])
                P.tt("dve", lsc[:, 4:5], lsc[:, 3:4], lsc[:, 2:3], ALU.subtract, reads=[rl], writes=[rl])
                P.ts("dve", lsc[:, 4:5], lsc[:, 4:5], -lam_init, None, ALU.add, reads=[rl], writes=[rl])
                neglam = lsc[:, 4:5]
                onesb = cmatb[:, 640:768]
                acc_set = [0]
                pend_epi = [None]

                def flush_epi():
                    if pend_epi[0] is not None:
                        for _ in pend_epi[0]:
                            pass
                        pend_epi[0] = None

                def epilogue(h, q0, OUT, outr, SUM, sumr, za, zr):
                    rs, rsr = rsring.next()
                    P.op("dve", lambda e: e.reciprocal(out=rs, in_=SUM), reads=[sumr], writes=[rsr])
                    t0_, t0r = t0ring.next()
                    P.tt("dve", t0_, OUT, rs, ALU.mult, reads=[outr, rsr], writes=[t0r])
                    o_, o_r = oring.next()
                    P.stt(o_, t0_[:, 256:512], neglam, t0_[:, 0:256], ALU.mult, ALU.add, reads=[t0r, rl], writes=[o_r])
                    sq, sqr = sqring.next()
                    P.tt("pool", sq, o_, o_, ALU.mult, reads=[o_r], writes=[sqr])
                    yield 1
                    P.mm(SUM[:, 0:256], lhsT=onesb, rhs=sq, start=True, stop=True, reads=[sqr], writes=[sumr])
                    ms, msr = msring.next()
                    P.ts("dve", ms, SUM[:, 0:256], 1.0 / 128.0, EPS, ALU.mult, ALU.add, reads=[sumr], writes=[msr])
                    yield 2
                    P.act(ms, ms, AF.Ln, reads=[msr], writes=[msr])
                    P.act(ms, ms, AF.Exp, reads=[msr], writes=[msr], scale=-0.5)
                    yield 3
                    P.stt(o_, o_, nwc[:, 0:1], ms, ALU.mult, ALU.mult, reads=[o_r, msr, rnw], writes=[o_r])
                    ao, aor = aoring.next()
                    P.tt("dve", ao, o_, za, ALU.mult, reads=[o_r, zr], writes=[aor])
                    P.dma(S["AOT"][h * 128:(h + 1) * 128, q0:q0 + 256], ao, reads=[aor])

                tiles = []
                for h in range(4):
                    if need_ctx:
                        tiles.append((h, 0, [0, 1]))
                    for qi in range(TL // 256):
                        tiles.append((h, CTX + qi * 256, list(range(NT))))
                flat = [(ti, ii) for ti, (h_, q0_, kbs_) in enumerate(tiles) for ii in range(len(kbs_))]
                tstate = {}

                def tile_setup(ti):
                    h, q0, kbs = tiles[ti]
                    si = ti % 2
                    qt, qr = qring.next()
                    P.dma(qt[0:64, 0, :], S["QT"][h][0:64, q0:q0 + 256], writes=[qr], partial=True)
                    P.dma(qt[64:128, 1, :], S["QT"][h][64:128, q0:q0 + 256], writes=[qr], partial=True)
                    za, zr = zring.next()
                    P.dma(za, S["ZAT"][h * 128:(h + 1) * 128, q0:q0 + 256], writes=[zr])
                    tstate[ti] = (psb[4 + 2 * si], psr[4 + 2 * si], psb[5 + 2 * si], psr[5 + 2 * si], qt.rearrange("p c q -> p (c q)"), qr, za, zr)

                def emit_qk(f):
                    ti, ii = flat[f]
                    h, q0, kbs = tiles[ti]
                    kb = kbs[ii]
                    b_ = f % 3
                    P.mm(psb[b_], lhsT=KTs[:, h, kb * 128:(kb + 1) * 128], rhs=tstate[ti][4], start=True, stop=True, reads=[rk, tstate[ti][5]], writes=[psr[b_]])

                tile_setup(0)
                emit_qk(0)
                if len(flat) > 1:
                    if flat[1][0] != 0:
                        tile_setup(flat[1][0])
                    emit_qk(1)
                for f, (ti, ii) in enumerate(flat):
                    h, q0, kbs = tiles[ti]
                    nk = len(kbs)
                    kb = kbs[ii]
                    OUT, outr, SUM, sumr, qtf, qr, za, zr = tstate[ti]
                    if ii == 0 and ti + 1 < len(tiles) and (ti + 1) not in tstate:
                        tile_setup(ti + 1)
                    if f + 2 < len(flat):
                        if flat[f + 2][0] not in tstate:
                            tile_setup(flat[f + 2][0])
                        emit_qk(f + 2)
                    b_ = f % 3
                    pt, ptr = pring.next()
                    ptf = pt.rearrange("p c q -> p (c q)")
                    P.act(ptf, psb[b_], AF.Exp, reads=[psr[b_]], writes=[ptr], scale=0.125)
                    P.mm(OUT, lhsT=Vs[:, kb, h, 0:128], rhs=ptf, start=(ii == 0), stop=(ii == nk - 1), reads=[ptr, rv], writes=[outr])
                    P.mm(SUM, lhsT=onesb, rhs=ptf, start=(ii == 0), stop=(ii == nk - 1), reads=[ptr], writes=[sumr])
                    if ii in (8, 28, 40, 50) and pend_epi[0] is not None:
                        if next(pend_epi[0], "done") == "done":
                            pend_epi[0] = None
                    if ii == nk - 1:
                        flush_epi()
                        pend_epi[0] = epilogue(h, q0, OUT, outr, SUM, sumr, za, zr)
                        del tstate[ti]
                flush_epi()
                P.barrier()
            if stop_after == "attn":
                break


            with ExitStack() as ph:
                cw = ph.enter_context(SBT("cw", [128, 6, 3], F32))
                cb = ph.enter_context(SBT("cb", [128, 6], F32))
                xins = [ph.enter_context(SBT("xin%d" % i, [128, 516], F32)) for i in range(5)]
                xiring = Ring([t[:] for t in xins])
                cacc = [ph.enter_context(SBT("cacc%d" % i, [128, 512], F32)) for i in range(3)]
                caring = Ring([t[:] for t in cacc])
                cyb = [ph.enter_context(SBT("cyb%d" % i, [128, 512], BF16)) for i in range(3)]
                cyring = Ring([t[:] for t in cyb])
                ctb = [ph.enter_context(SBT("ctb%d" % i, [128, 512], BF16)) for i in range(3)]
                ctring = Ring([t[:] for t in ctb])
                rcw = Res()
                for j in range(3):
                    P.dma(cw[:, :, j], W["ssm_conv_w"][layer][j].rearrange("(f p) -> p f", p=128), writes=[rcw], partial=True)
                P.dma(cb[:], W["ssm_conv_b"][layer].rearrange("(f p) -> p f", p=128), writes=[rcw], partial=True)
                items = [(t0, n, isc, fc) for (t0, n, isc) in groups for fc in range(6)]
                loaded = {}

                def issue_xin(i):
                    t0, n, isc, fc = items[i]
                    seg_lo, seg_hi = (0, CTX) if isc else (CTX, TALL)
                    lo = max(t0 - 1, seg_lo)
                    hi = min(t0 + n + 1, seg_hi)
                    xin, xir = xiring.next()
                    if lo > t0 - 1:
                        P.op("pool", lambda e, xin=xin: e.memset(xin[:, 0:1], 0.0), writes=[xir])
                    if hi < t0 + n + 1:
                        P.op("pool", lambda e, xin=xin, n=n: e.memset(xin[:, n + 1:n + 2], 0.0), writes=[xir], partial=True)
                    P.dma(xin[:, lo - (t0 - 1):hi - (t0 - 1)], S["XBC"][fc * 128:(fc + 1) * 128, lo:hi], writes=[xir], partial=True)
                    loaded[i] = (xin, xir)
                for i in range(min(3, len(items))):
                    issue_xin(i)
                for i, (t0, n, isc, fc) in enumerate(items):
                    if True:
                        if i + 3 < len(items):
                            issue_xin(i + 3)
                        xin, xir = loaded.pop(i)
                        ca, car = caring.next()
                        P.ts("dve", ca[:, 0:n], xin[:, 1:n + 1], cw[:, fc, 1:2], cb[:, fc:fc + 1], ALU.mult, ALU.add, reads=[xir, rcw], writes=[car])
                        P.stt(ca[:, 0:n], xin[:, 0:n], cw[:, fc, 0:1], ca[:, 0:n], ALU.mult, ALU.add, reads=[xir, rcw, car], writes=[car])
                        P.stt(ca[:, 0:n], xin[:, 2:n + 2], cw[:, fc, 2:3], ca[:, 0:n], ALU.mult, ALU.add, reads=[xir, rcw, car], writes=[car])
                        cy, cyr = cyring.next()
                        P.act(cy[:, 0:n], ca[:, 0:n], AF.Silu, reads=[car], writes=[cyr])
                        if fc == 4:
                            P.dma(S["BT"][:, t0:t0 + n], cy[:, 0:n], reads=[cyr])
                        if fc == 5:
                            P.dma(S["CT"][:, t0:t0 + n], cy[:, 0:n], reads=[cyr])
                            continue
                        ps, pr = nextps()
                        pb = ps.bitcast(BF16).rearrange("p (j t) -> p j t", t=128)
                        nst = n // 128
                        for j in range(nst):
                            P.tr(pb[:, j, :], cy[:, j * 128:(j + 1) * 128], identb, reads=[cyr], writes=[pr], sig=(j == nst - 1), partial=(j > 0))
                        ct_, ctr = ctring.next()
                        P.cp("dve", ct_[:, 0:n], ps.bitcast(BF16)[:, 0:n], reads=[pr], writes=[ctr])
                        if fc < 4:
                            dst = S["XS"][t0:t0 + n, fc * 128:(fc + 1) * 128]
                        else:
                            dst = S["BTM"][t0:t0 + n, :]
                        P.dma(dst.rearrange("(j p) c -> p j c", p=128), ct_[:, 0:n].rearrange("p (j c) -> p j c", c=128), reads=[ctr])
                P.barrier()
            if stop_after == "ssmprep":
                break

            with ExitStack() as ph:
                XSs = ph.enter_context(SBT("XSs", [128, NT, 512], BF16))
                BTs = ph.enter_context(SBT("BTs", [128, TALL], BF16))
                CTs = ph.enter_context(SBT("CTs", [128, TALL], BF16))
                BTMs = ph.enter_context(SBT("BTMs", [128, NT, 128], BF16))
                DTs = ph.enter_context(SBT("DTs", [128, NT, 16], F32))
                aneg = ph.enter_context(SBT("aneg", [128, 16], F32))
                dsk = ph.enter_context(SBT("dsk", [128, 2, 8], F32))
                snw = ph.enter_context(SBT("snw", [128, 512], F32))
                hs32 = ph.enter_context(SBT("hs32", [128, 2, 4, 64], F32))
                hsb = ph.enter_context(SBT("hsb", [128, 2, 4, 64], BF16))

                def mk(name, shape, dt, n):
                    ts_ = [ph.enter_context(SBT("%s%d" % (name, i), shape, dt)) for i in range(n)]
                    return Ring([t[:] for t in ts_])
                dAr = mk("dA", [128, 16], F32, 4)
                xdtr = mk("xdt", [128, 8, 64], BF16, 3)
                rcr = mk("rcum", [128, 4, 128], F32, 4)
                ssr = mk("ssub", [128, 4, 128], F32, 4)
                scr_ = mk("scm", [128, 128], F32, 4)
                wr_ = mk("wmat", [128, 4, 128], BF16, 4)
                ear = mk("eacs", [128, 4, 128], F32, 4)
                cdr = mk("cdec", [128, 4, 128], BF16, 4)
                xder = mk("xdec", [128, 4, 64], BF16, 4)
                yr_ = mk("ysb", [128, 512], F32, 3)
                yfr = mk("yfl", [128, 512], F32, 2)
                zbr = mk("zbl", [128, 512], BF16, 2)
                y2r = mk("y2", [128, 512], F32, 2)
                sor = mk("sob", [128, 512], BF16, 2)
                smr_ = mk("ssm_sm", [128, 8], F32, 3)
                r_res, r_par = Res(), Res()
                for i0 in range(0, NT, 8):
                    i1 = min(NT, i0 + 8)
                    P.dma(XSs[:, i0:i1, :], S["XS"][i0 * 128:i1 * 128, :].rearrange("(i p) c -> p i c", p=128), writes=[r_res], partial=True)
                P.dma(BTs[:], S["BT"], writes=[r_res], partial=True)
                P.dma(CTs[:], S["CT"], writes=[r_res], partial=True)
                P.dma(BTMs[:], S["BTM"].rearrange("(i p) c -> p i c", p=128), writes=[r_res], partial=True)
                P.dma(DTs[:], S["DT"].rearrange("(i p) c -> p i c", p=128), writes=[r_res], partial=True)
                P.dma(aneg[:], W["ssm_a_log"][layer].rearrange("a b -> (a b)").partition_broadcast(128), writes=[r_par], partial=True)
                P.dma(dsk[:], W["ssm_d"][layer].rearrange("a b -> (a b)").partition_broadcast(128), writes=[r_par], partial=True)
                P.dma(snw[:], W["ssm_norm_w"][layer].partition_broadcast(128), writes=[r_par], partial=True)
                P.act(aneg[:], aneg[:], AF.Exp, reads=[r_par], writes=[r_par])
                P.ts("dve", aneg[:], aneg[:], -1.0, None, ALU.mult, reads=[r_par], writes=[r_par])
                P.tt("dve", dsk[:, 0, :], dsk[:, 0, :], dsk[:, 1, :], ALU.add, reads=[r_par], writes=[r_par])
                ydone = {}

                def ssd_pass(d):
                    rh = Res()
                    hs32d, hsbd = hs32[:, d], hsb[:, d]
                    tri32 = trif32 if d == 0 else trib32
                    lend = 127 if d == 0 else 0
                    order = [0, 1] + list(range(2, NT)) if d == 0 else [1, 0] + list(range(NT - 1, 1, -1))
                    for ci, c in enumerate(order):
                        first = (ci == 0)
                        tsl = slice(c * 128, (c + 1) * 128)
                        dA, dAr_ = dAr.next()
                        P.tt("dve", dA[:, 0:8], DTs[:, c, d * 8:(d + 1) * 8], aneg[:, d * 8:(d + 1) * 8], ALU.mult, reads=[r_res, r_par], writes=[dAr_])
                        ps, pr = nextps()
                        P.mm(ps[:, 0:8], lhsT=tri32, rhs=dA[:, 0:8], start=True, stop=True, reads=[dAr_], writes=[pr])
                        xdt, xdtr_ = xdtr.next()
                        P.tt("dve", xdt, XSs[:, c, :].rearrange("p (h q) -> p h q", q=64), DTs[:, c, d * 8:(d + 1) * 8].unsqueeze(2).to_broadcast([128, 8, 64]),
                             ALU.mult, reads=[r_res], writes=[xdtr_])
                        rcs = []
                        for g in range(2):
                            rc, rcr_ = rcr.next()
                            P.tt("pool", rc, tri32.unsqueeze(1).to_broadcast([128, 4, 128]), dA[:, g * 4:(g + 1) * 4].unsqueeze(2).to_broadcast([128, 4, 128]),
                                 ALU.mult, reads=[dAr_], writes=[rcr_])
                            rcs.append((rc, rcr_))
                        yield
                        P.cp("dve", dA[:, 8:16], ps[:, 0:8], reads=[pr], writes=[dAr_], partial=True)
                        sps, spr = nextps()
                        yps = []
                        for g in range(2):
                            gs = slice(g * 64, (g + 1) * 64)
                            ps, pr = nextps()
                            P.mm(ps[:, 0:128], lhsT=BTs[gs, tsl], rhs=CTs[gs, tsl], start=True, stop=True, reads=[r_res], writes=[pr])
                            rc, rcr_ = rcs[g]
                            ps2, pr2 = nextps()
                            P.mm(ps2[:], lhsT=ones32, rhs=rc.rearrange("p h l -> p (h l)"), start=True, stop=True, reads=[rcr_], writes=[pr2])
                            yield
                            scm, scmr = scr_.next()
                            P.tt("dve", scm, ps[:, 0:128], tri32, ALU.mult, reads=[pr], writes=[scmr])
                            ps, pr = ps2, pr2
                            psv = ps.rearrange("p (h l) -> p h l", l=128)
                            ss, ssr_ = ssr.next()
                            for h in range(4):
                                P.ts("dve", ss[:, h, :], psv[:, h, :], dA[:, 8 + g * 4 + h:9 + g * 4 + h], 0.0, ALU.subtract, ALU.min, reads=[pr, dAr_], writes=[ssr_],
                                     partial=(h > 0))
                            ea, ear_ = ear.next()
                            P.act(ea[gs], psv[gs], AF.Exp, reads=[pr], writes=[ear_])
                            P.act(ss, ss, AF.Exp, reads=[ssr_], writes=[ssr_])
                            yield
                            wm_, wmr_ = wr_.next()
                            P.tt("dve", wm_, ss, scm.unsqueeze(1).to_broadcast([128, 4, 128]), ALU.mult, reads=[ssr_, scmr], writes=[wmr_])
                            yp, ypr = nextps()
                            yps.append((yp, ypr))
                            if not first:
                                cd, cdr_ = cdr.next()
                                P.tt("dve", cd[gs], ea[gs], CTs[gs, tsl].unsqueeze(1).to_broadcast([64, 4, 128]), ALU.mult, reads=[ear_, r_res], writes=[cdr_])
                            xde, xder_ = xder.next()
                            P.tt("dve", xde, xdt[:, g * 4:(g + 1) * 4, :], ss[:, :, lend:lend + 1].to_broadcast([128, 4, 64]), ALU.mult, reads=[xdtr_, ssr_], writes=[xder_])
                            yield
                            for h in range(4):
                                P.mm(yp[:, h * 64:(h + 1) * 64], lhsT=wm_[:, h, :], rhs=xdt[:, g * 4 + h, :], start=True, stop=first,
                                     reads=[wmr_, xdtr_], writes=[ypr], sig=(first and h == 3), partial=(h > 0))
                                if not first:
                                    P.mm(yp[:, h * 64:(h + 1) * 64], lhsT=cd[gs, h, :], rhs=hsbd[gs, h, :], start=False, stop=True,
                                         reads=[cdr_, rh], writes=[ypr], sig=(h == 3), partial=True)
                            P.mm(sps[gs, 0:256], lhsT=BTMs[:, c, gs], rhs=xde.rearrange("p h q -> p (h q)"), start=True, stop=True,
                                 reads=[r_res, xder_], writes=[spr], partial=(g > 0))
                            yield
                            if first:
                                P.cp("dve", hs32d[gs], sps[gs, 0:256].rearrange("p (h q) -> p h q", q=64), reads=[spr], writes=[rh], partial=(g > 0))
                            else:
                                P.tt("dve", hs32d[gs], hs32d[gs], ea[gs, :, lend:lend + 1].to_broadcast([64, 4, 64]), ALU.mult, reads=[rh, ear_], writes=[rh], partial=True)
                                P.tt("dve", hs32d[gs], hs32d[gs], sps[gs, 0:256].rearrange("p (h q) -> p h q", q=64), ALU.add, reads=[rh, spr], writes=[rh], partial=True)
                            P.act(hsbd[gs], hs32d[gs], AF.Copy, reads=[rh], writes=[rh], partial=True)
                        if c not in ydone:
                            ysb, ysr = yr_.next()
                            for g in range(2):
                                P.act(ysb[:, g * 256:(g + 1) * 256], yps[g][0][:, 0:256], AF.Copy, reads=[yps[g][1]], writes=[ysr], partial=(g > 0))
                            yield
                            ydone[c] = Res()
                            P.dma(S["YF"][tsl, :], ysb, reads=[ysr], writes=[ydone[c]])
                        elif c >= 2 or need_ctx:
                            yf, yfr_ = yfr.next()
                            zb, zbr_ = zbr.next()
                            P.dma(yf, S["YF"][tsl, :], reads=[ydone[c]], writes=[yfr_])
                            P.dma(zb, S["ZB"][tsl, :], writes=[zbr_])
                            y2, y2r_ = y2r.next()
                            for g in range(2):
                                P.tt("dve", y2[:, g * 256:(g + 1) * 256], yps[g][0][:, 0:256], yf[:, g * 256:(g + 1) * 256], ALU.add, reads=[yps[g][1], yfr_], writes=[y2r_],
                                     partial=(g > 0))
                            ysb, ysr = yr_.next()
                            P.tt("pool", ysb.rearrange("p (h q) -> p h q", q=64), XSs[:, c, :].rearrange("p (h q) -> p h q", q=64),
                                 dsk[:, 0, :].unsqueeze(2).to_broadcast([128, 8, 64]), ALU.mult, reads=[r_res, r_par], writes=[ysr])
                            yield
                            P.tt("dve", y2, y2, ysb, ALU.add, reads=[y2r_, ysr], writes=[y2r_])
                            P.tt("dve", y2, y2, zb, ALU.mult, reads=[y2r_, zbr_], writes=[y2r_])
                            sm, smr2 = smr_.next()
                            for g in range(2):
                                P.op("dve", lambda e, g=g, ysb=ysb, y2=y2, sm=sm: e.scalar_tensor_tensor(out=ysb[:, g * 256:(g + 1) * 256], in0=y2[:, g * 256:(g + 1) * 256], scalar=1.0,
                                                                      in1=y2[:, g * 256:(g + 1) * 256], op0=ALU.mult, op1=ALU.mult, accum_out=sm[:, g:g + 1]),
                                     reads=[y2r_], writes=[ysr, smr2], partial=(g > 0))
                            P.ts("dve", sm[:, 2:4], sm[:, 0:2], 1.0 / 256.0, EPS, ALU.mult, ALU.add, reads=[smr2], writes=[smr2])
                            yield
                            P.act(sm[:, 4:6], sm[:, 2:4], AF.Ln, reads=[smr2], writes=[smr2])
                            P.act(sm[:, 6:8], sm[:, 4:6], AF.Exp, reads=[smr2], writes=[smr2], scale=-0.5)
                            yield
                            so, sor_ = sor.next()
                            for g in range(2):
                                P.stt(so[:, g * 256:(g + 1) * 256], y2[:, g * 256:(g + 1) * 256], sm[:, 6 + g:7 + g], snw[:, g * 256:(g + 1) * 256], ALU.mult, ALU.mult,
                                      reads=[y2r_, smr2, r_par], writes=[sor_], partial=(g > 0))
                            yield
                            P.dma(S["SO"][tsl, :], so, reads=[sor_])
                        yield

                gens = [ssd_pass(0), ssd_pass(1)]
                while gens:
                    for g_ in list(gens):
                        if next(g_, "done") == "done":
                            gens.remove(g_)
                P.barrier()
            if stop_after == "ssd":
                break


            with ExitStack() as ph:
                GQa = ph.enter_context(SBT("GQs", [128, 2, 2, TALL], BF16))
                GKa = ph.enter_context(SBT("GKs", [128, 2, 2, TALL], BF16))
                GELa = ph.enter_context(SBT("GELs", [128, 2, 2, NCH], F32))
                gnw = ph.enter_context(SBT("gnw", [128, 128], F32))
                ghsa = ph.enter_context(SBT("ghs", [128, 2, 2, 128], F32))

                def mk(name, shape, dt, n):
                    ts_ = [ph.enter_context(SBT("%s%d" % (name, i), shape, dt)) for i in range(n)]
                    return Ring([t[:] for t in ts_])
                gvr = mk("gvl", [128, 512], BF16, 5)
                gker = mk("gkel", [128, 256], BF16, 5)
                atr = mk("attm", [128, 4, 128], BF16, 4)
                hyr = mk("ghy", [128, 2, 128], F32, 4)
                hxbr = mk("ghxb", [128, 2, 128], BF16, 4)
                hybr = mk("ghyb", [128, 2, 128], BF16, 4)
                gyr = mk("gysb", [128, 512], F32, 4)
                gyfr = mk("gyfl", [128, 512], F32, 3)
                zcr = mk("zcl", [128, 512], BF16, 3)
                gy2r = mk("gy2", [128, 512], F32, 3)
                gor = mk("gob", [128, 512], BF16, 3)
                gsmr = mk("gsm", [128, 16], F32, 3)
                rg_par = Res()
                P.dma(gnw[:], W["gla_norm_w"][layer].partition_broadcast(128), writes=[rg_par])
                rq = Res()
                for d in range(2):
                    for hp in range(2):
                        P.dma(GQa[:, d, hp, :], S["GQ"][d][hp * 128:(hp + 1) * 128, :], writes=[rq], partial=True)
                        P.dma(GKa[:, d, hp, :], S["GK"][d][hp * 128:(hp + 1) * 128, :], writes=[rq], partial=True)
                        P.dma(GELa[:, d, hp, :], S["GEL"][d][hp * 128:(hp + 1) * 128, :], writes=[rq], partial=True)
                gdone = {}

                def gla_pass(d):
                    GQs, GKs, GELs, ghs = GQa[:, d], GKa[:, d], GELa[:, d], ghsa[:, d]
                    gmask = gmfb if d == 0 else gmbb
                    order = [0, 1] + list(range(2, NT)) if d == 0 else [1, 0] + list(range(NT - 1, 1, -1))
                    rgh = Res()
                    loads = {}

                    def issue_loads(c):
                        gv, gvr_ = gvr.next()
                        gke, gker_ = gker.next()
                        P.dma(gv, S["GV"][c * 128:(c + 1) * 128, :], writes=[gvr_])
                        P.dma(gke, S["GKE"][d][c * 128:(c + 1) * 128, :], writes=[gker_])
                        loads[c] = (gv, gvr_, gke, gker_)
                    issue_loads(order[0])
                    for ci, c in enumerate(order):
                        first = (ci == 0)
                        tsl = slice(c * 128, (c + 1) * 128)
                        if ci + 1 < len(order):
                            issue_loads(order[ci + 1])
                        gv, gvr_, gke, gker_ = loads.pop(c)
                        X, Y = (0, 1) if d == 0 else (1, 0)
                        chX, chY = 2 * c + X, 2 * c + Y
                        attm, atr_ = atr.next()
                        aps = []
                        for j in range(2):
                            js = slice(j * 64, (j + 1) * 64)
                            ps, pr = nextps()
                            for hp in range(2):
                                P.mm(ps[:, hp * 128:(hp + 1) * 128], lhsT=GKs[js, hp, tsl], rhs=GQs[js, hp, tsl], start=True, stop=True, reads=[rq], writes=[pr],
                                     partial=(hp > 0))
                            aps.append((ps, pr))
                        sp = []
                        for ch in range(2):
                            cs_ = slice(ch * 64, (ch + 1) * 64)
                            ps, pr = nextps()
                            for hp in range(2):
                                for j in range(2):
                                    hd = hp * 2 + j
                                    P.mm(ps[j * 64:(j + 1) * 64, hp * 128:(hp + 1) * 128], lhsT=gke[cs_, hd * 64:(hd + 1) * 64], rhs=gv[cs_, hd * 128:(hd + 1) * 128],
                                         start=True, stop=True, reads=[gker_, gvr_], writes=[pr], partial=(hp + j > 0), sig=(hp + j == 2))
                            sp.append((ps, pr))
                        yield
                        for j in range(2):
                            ps, pr = aps[j]
                            for hp in range(2):
                                P.tt("dve", attm[:, hp * 2 + j, :], ps[:, hp * 128:(hp + 1) * 128], gmask, ALU.mult, reads=[pr], writes=[atr_], partial=(j + hp > 0))
                        hxb, hxbr_ = hxbr.next()
                        hyb, hybr_ = hybr.next()
                        hy, hyr_ = hyr.next()
                        if first:
                            P.cp("dve", hy, sp[X][0][:, 0:256].rearrange("p (a v) -> p a v", v=128), reads=[sp[X][1]], writes=[hyr_])
                        else:
                            P.act(hxb, ghs, AF.Copy, reads=[rgh], writes=[hxbr_])
                            for hp in range(2):
                                P.stt(hy[:, hp, :], ghs[:, hp, :], GELs[:, hp, chX:chX + 1], sp[X][0][:, hp * 128:(hp + 1) * 128], ALU.mult, ALU.add,
                                      reads=[rgh, rq, sp[X][1]], writes=[hyr_], partial=(hp > 0))
                        yield
                        P.act(hyb, hy, AF.Copy, reads=[hyr_], writes=[hybr_])
                        for hp in range(2):
                            P.stt(ghs[:, hp, :], hy[:, hp, :], GELs[:, hp, chY:chY + 1], sp[Y][0][:, hp * 128:(hp + 1) * 128], ALU.mult, ALU.add,
                                  reads=[hyr_, rq, sp[Y][1]], writes=[rgh], partial=(hp > 0))
                        yield
                        yps = []
                        for j in range(2):
                            js = slice(j * 64, (j + 1) * 64)
                            yp, ypr = nextps()
                            for hp in range(2):
                                hd = hp * 2 + j
                                ysl = yp[:, hp * 128:(hp + 1) * 128]
                                P.mm(ysl, lhsT=attm[:, hd, :], rhs=gv[:, hd * 128:(hd + 1) * 128], start=True, stop=False, reads=[atr_, gvr_], writes=[ypr],
                                     sig=False, partial=(hp > 0))
                                if not first:
                                    P.mm(yp[X * 64:(X + 1) * 64, hp * 128:(hp + 1) * 128], lhsT=GQs[js, hp, c * 128 + X * 64:c * 128 + (X + 1) * 64], rhs=hxb[js, hp, :],
                                         start=False, stop=False, reads=[rq, hxbr_], writes=[ypr], sig=False, partial=True)
                                P.mm(yp[Y * 64:(Y + 1) * 64, hp * 128:(hp + 1) * 128], lhsT=GQs[js, hp, c * 128 + Y * 64:c * 128 + (Y + 1) * 64], rhs=hyb[js, hp, :],
                                     start=False, stop=True, reads=[rq, hybr_], writes=[ypr], sig=(hp == 1), partial=True)
                            yps.append((yp, ypr))
                        yield
                        if c not in gdone:
                            ysb, ysr = gyr.next()
                            ysv = ysb.rearrange("p (hp j v) -> p hp j v", j=2, v=128)
                            for j in range(2):
                                P.act(ysv[:, :, j, :], yps[j][0][:, 0:256].rearrange("p (hp v) -> p hp v", v=128), AF.Copy, reads=[yps[j][1]], writes=[ysr], partial=(j > 0))
                            yield
                            gdone[c] = Res()
                            P.dma(S["GYF"][tsl, :], ysb, reads=[ysr], writes=[gdone[c]])
                        elif c >= 2 or need_ctx:
                            yf, yfr_ = gyfr.next()
                            zc, zcr_ = zcr.next()
                            P.dma(yf, S["GYF"][tsl, :], reads=[gdone[c]], writes=[yfr_])
                            P.dma(zc, S["ZC"][tsl, :], writes=[zcr_])
                            y2, y2r_ = gy2r.next()
                            y2v = y2.rearrange("p (hp j v) -> p hp j v", j=2, v=128)
                            yfv = yf.rearrange("p (hp j v) -> p hp j v", j=2, v=128)
                            for j in range(2):
                                P.tt("dve", y2v[:, :, j, :], yps[j][0][:, 0:256].rearrange("p (hp v) -> p hp v", v=128), yfv[:, :, j, :], ALU.add,
                                     reads=[yps[j][1], yfr_], writes=[y2r_], partial=(j > 0))
                            ysb, ysr = gyr.next()
                            sm, smr2 = gsmr.next()
                            for hd in range(4):
                                P.op("dve", lambda e, hd=hd, ysb=ysb, y2=y2, sm=sm: e.scalar_tensor_tensor(out=ysb[:, hd * 128:(hd + 1) * 128], in0=y2[:, hd * 128:(hd + 1) * 128], scalar=1.0,
                                                                       in1=y2[:, hd * 128:(hd + 1) * 128], op0=ALU.mult, op1=ALU.mult, accum_out=sm[:, hd:hd + 1]),
                                     reads=[y2r_], writes=[ysr, smr2], partial=(hd > 0))
                            P.ts("dve", sm[:, 4:8], sm[:, 0:4], 1.0 / 128.0, EPS, ALU.mult, ALU.add, reads=[smr2], writes=[smr2])
                            yield
                            P.act(sm[:, 8:12], sm[:, 4:8], AF.Ln, reads=[smr2], writes=[smr2])
                            P.act(sm[:, 12:16], sm[:, 8:12], AF.Exp, reads=[smr2], writes=[smr2], scale=-0.5)
                            yield
                            for hd in range(4):
                                P.stt(y2[:, hd * 128:(hd + 1) * 128], y2[:, hd * 128:(hd + 1) * 128], sm[:, 12 + hd:13 + hd], gnw[:], ALU.mult, ALU.mult,
                                      reads=[y2r_, smr2, rg_par], writes=[y2r_], partial=(hd > 0))
                            go, gor_ = gor.next()
                            P.tt("dve", go, y2, zc, ALU.mult, reads=[y2r_, zcr_], writes=[gor_])
                            yield
                            P.dma(S["GO"][tsl, :], go, reads=[gor_])
                        yield

                gens = [gla_pass(0), gla_pass(1)]
                while gens:
                    for g_ in list(gens):
                        if next(g_, "done") == "done":
                            gens.remove(g_)
                P.barrier()
            if stop_after == "gla":
                break


            last = (layer == DEPTH - 1)
            with ExitStack() as ph:
                wob = [ph.enter_context(SBT("wob%d" % i, [128, 4, D], BF16)) for i in range(3)]
                wo = ph.enter_context(SBT("wo", [128, 8, D], BF16))
                fnw = ph.enter_context(SBT("fnw", [128, D], F32))
                oT = [ph.enter_context(SBT("oT%d" % i, [128, 4, 512], BF16)) for i in range(3)]
                uT = ph.enter_context(SBT("uT", [128, 8, 512], BF16))

                def mk(name, shape, dt, n):
                    ts_ = [ph.enter_context(SBT("%s%d" % (name, i), shape, dt)) for i in range(n)]
                    return Ring([t[:] for t in ts_])
                btr = mk("mbt", [128, 4, 512], BF16, 2)
                sgr = mk("msg", [128, 512], BF16, 12)
                mtr = mk("mt", [128, 512], F32, 6)
                mxr = mk("mx", [128, D], F32, 5)
                mnr = mk("mxn", [128, D], F32, 2)
                msm = mk("msm", [128, 8], F32, 2)
                mjk = ph.enter_context(SBT("mjk", [128, D], BF16))
                rw_ = Res()
                for b, nm in enumerate(("w_out_da", "w_out_ssm", "w_out_gla")):
                    P.dma(wob[b][:], W[nm][layer].rearrange("(k p) c -> p k c", p=128), writes=[rw_], q="pool", partial=True)
                for k0 in range(0, 8, 2):
                    P.dma(wo[:, k0:k0 + 2, :], W["w_o"][layer].rearrange("(k p) c -> p k c", p=128)[:, k0:k0 + 2, :], writes=[rw_], q="pool", partial=True)
                P.dma(fnw[:], W["final_norm_w"].partition_broadcast(128), writes=[rw_], partial=True)
                oTr = [Res() for _ in range(3)]
                uTr = Res()
                jkr = Res()
                for (t0, n, isc) in groups:
                    if isc and not need_ctx:
                        continue
                    who = 1 if isc else 0
                    nst = n // 128
                    for kc in range(4):
                        P.dma(oT[0][:, kc, 0:n], S["AOT"][kc * 128:(kc + 1) * 128, t0:t0 + n], writes=[oTr[0]], partial=(kc > 0))
                    for b, nm in ((1, "SO"), (2, "GO")):
                        bt, btr_ = btr.next()
                        P.dma(bt[:, 0:nst, :], S[nm][t0:t0 + n, :].rearrange("(j p) c -> p j c", p=128), writes=[btr_])
                        for kc in range(4):
                            ps, pr = nextps()
                            pb = ps.bitcast(BF16).rearrange("p (j t) -> p j t", t=128)
                            for j in range(nst):
                                P.tr(pb[:, j, :], bt[:, j, kc * 128:(kc + 1) * 128], identb, reads=[btr_], writes=[pr], sig=(j == nst - 1), partial=(j > 0))
                            if kc % 2 == 0:
                                P.act(oT[b][:, kc, 0:n], ps.bitcast(BF16)[:, 0:n], AF.Copy, reads=[pr], writes=[oTr[b]], partial=(kc > 0))
                            else:
                                P.cp("dve", oT[b][:, kc, 0:n], ps.bitcast(BF16)[:, 0:n], reads=[pr], writes=[oTr[b]], partial=True)
                    xld = []
                    for j in range(nst):
                        xt, xr = mxr.next()
                        P.dma(xt, tok_src(layer, t0 // 128 + j), writes=[xr])
                        xld.append((xt, xr))
                    sgl = {}

                    def issue_sg(oc):
                        for b in range(3):
                            sg, sgr_ = sgr.next()
                            P.dma(sg[:, 0:n], S["SG"][b][oc * 128:(oc + 1) * 128, t0:t0 + n], writes=[sgr_])
                            sgl[(oc, b)] = (sg, sgr_)
                    issue_sg(0)
                    issue_sg(1)
                    for oc in range(8):
                        if oc + 2 < 8:
                            issue_sg(oc + 2)
                        tms = []
                        for b in range(3):
                            sg, sgr_ = sgl.pop((oc, b))
                            ps, pr = nextps()
                            for kc in range(4):
                                P.mm(ps[:, 0:n], lhsT=wob[b][:, kc, oc * 128:(oc + 1) * 128], rhs=oT[b][:, kc, 0:n], start=(kc == 0), stop=(kc == 3),
                                     reads=[rw_, oTr[b]], writes=[pr])
                            tm, tmr = mtr.next()
                            P.tt("dve", tm[:, 0:n], ps[:, 0:n], sg[:, 0:n], ALU.mult, reads=[pr, sgr_], writes=[tmr])
                            tms.append((tm, tmr))
                        P.tt("pool", tms[0][0][:, 0:n], tms[0][0][:, 0:n], tms[1][0][:, 0:n], ALU.add, reads=[tms[0][1], tms[1][1]], writes=[tms[0][1]])
                        P.tt("dve", uT[:, oc, 0:n], tms[0][0][:, 0:n], tms[2][0][:, 0:n], ALU.add, reads=[tms[0][1], tms[2][1]], writes=[uTr], partial=(oc > 0))
                    for j in range(nst):
                        ti = t0 // 128 + j
                        xt, xr = xld[j]
                        xn, xnr = mnr.next()
                        for hf in range(2):
                            ps, pr = nextps()
                            for k in range(KC):
                                P.mm(ps[:], lhsT=uT[:, k, j * 128:(j + 1) * 128], rhs=wo[:, k, hf * 512:(hf + 1) * 512], start=(k == 0), stop=(k == KC - 1),
                                     reads=[rw_, uTr], writes=[pr])
                            P.tt("dve", xn[:, hf * 512:(hf + 1) * 512], ps[:], gate_bc[:, who, hf * 512:(hf + 1) * 512], ALU.mult, reads=[pr], writes=[xnr], partial=(hf > 0))
                        P.tt("pool", xn, xn, xt, ALU.add, reads=[xnr, xr], writes=[xnr])
                        if not last:
                            dst = S["CXR"][ti * 128:(ti + 1) * 128, :] if isc else S["XR"][(ti - 2) * 128:(ti - 1) * 128, :]
                            P.dma(dst, xn, reads=[xnr])
                        else:
                            sm, smr2 = msm.next()
                            P.act(mjk[:], xn, AF.Square, reads=[xnr], writes=[jkr, smr2], accum_out=sm[:, 0:1])
                            P.ts("dve", sm[:, 1:2], sm[:, 0:1], 1.0 / D, EPS, ALU.mult, ALU.add, reads=[smr2], writes=[smr2])
                            P.act(sm[:, 2:3], sm[:, 1:2], AF.Ln, reads=[smr2], writes=[smr2])
                            P.act(sm[:, 3:4], sm[:, 2:3], AF.Exp, reads=[smr2], writes=[smr2], scale=-0.5)
                            P.stt(xt, xn, sm[:, 3:4], fnw[:], ALU.mult, ALU.mult, reads=[xnr, smr2, rw_], writes=[xr])
                            P.dma(y_out[(ti - 2) * 128:(ti - 1) * 128, :], xt, reads=[xr])
                P.barrier()

        P.barrier()
        for e in ("sp", "pool", "act", "dve", "pe"):
            P.flush(e)
    return nc


_NC_CACHE = {}


def kernel(x, c, ctx, c_ctx, **weights):
    x = np.asarray(x, dtype=np.float32)
    B, TL, _ = x.shape
    if TL not in _NC_CACHE:
        _NC_CACHE[TL] = build(TL)
    nc = _NC_CACHE[TL]
    consts = host_consts(TL)
    shared = {"c_ctx": np.ascontiguousarray(np.asarray(c_ctx, dtype=np.float32))}
    for n, _s in WEIGHT_SPECS:
        shared[n] = np.ascontiguousarray(np.asarray(weights[n], dtype=np.float32))
    shared.update(consts)
    in_maps = []
    for b in range(B):
        m = dict(shared)
        m["x"] = np.ascontiguousarray(x[b])
        m["c"] = np.ascontiguousarray(np.asarray(c, dtype=np.float32)[b])
        m["ctx"] = np.ascontiguousarray(np.asarray(ctx, dtype=np.float32)[b])
        in_maps.append(m)
    res = run_bass_kernel_spmd(nc, in_maps, core_ids=list(range(B)))
    return np.stack([np.asarray(r["y"], dtype=np.float32) for r in res.results], axis=0)
```

```python
import math
from bisect import bisect_left
from contextlib import ExitStack

import numpy as np
import ml_dtypes
import concourse.bass as bass
import concourse.mybir as mybir
from concourse.bass_utils import run_bass_kernel_spmd

F32 = mybir.dt.float32
BF16 = mybir.dt.bfloat16
AF = mybir.ActivationFunctionType
ALU = mybir.AluOpType
AX = mybir.AxisListType

D = 1024
CTX = 256
KC = 8
EPS = 1e-6
IN_W = 7984
C_Q, C_K, C_V, C_ZA = 0, 512, 1024, 1536
C_BX, C_BZ, C_BB, C_BC, C_DT = 2048, 2560, 3072, 3200, 3328
C_GQ, C_GK, C_GV, C_GZ, C_LR = 3344, 3600, 3856, 4368, 4880
C_SG = 4912
DEPTH = 2


class Res:
    __slots__ = ("writers", "rd", "rdma", "excl")

    def __init__(self, excl=False):
        self.writers = []
        self.rd = {}
        self.rdma = []
        self.excl = excl


class Prog:
    def __init__(self, nc, es, n_dma=48):
        self.nc = nc
        self.E = {"pe": nc.tensor, "act": nc.scalar, "dve": nc.vector, "pool": nc.gpsimd, "sp": nc.sync}
        self.sem = {k: es.enter_context(nc.semaphore("s_" + k)) for k in ("pe", "act", "dve", "pool")}
        self.dsem = [es.enter_context(nc.semaphore("d%d" % i)) for i in range(n_dma)]
        self.dcnt = [0] * n_dma
        self.n_sw = 8
        self.drr = {"hw": 0, "sw": 0}
        self.cnt = {k: 0 for k in self.sem}
        self.nops = {k: 0 for k in self.sem}
        self.sigs = {k: ([], []) for k in self.sem}
        self.waited = {k: {} for k in self.E}
        self.pending = {k: [] for k in self.E}
        self.lastop = {k: None for k in self.sem}
        self.dma_open = []
        self.n_ins = 0

    def _resolve(self, tok):
        if tok[0] == "d":
            return ("d", tok[1]), tok[2]
        _, eng, idx = tok
        idxs, vals = self.sigs[eng]
        j = bisect_left(idxs, idx)
        assert j < len(idxs), "dependency on a non-signalling op with no later signal on " + eng
        return eng, vals[j]

    def op(self, eng, fn, reads=(), writes=(), sig=True, partial=False, dma=False):
        toks = []
        raw = []
        for r in reads:
            raw += r.writers
            if r.excl:
                toks += list(r.rd.values())
        toks += raw
        for w in writes:
            toks += w.writers
            toks += list(w.rd.values())
            toks += w.rdma
        toks += self.pending[eng]
        self.pending[eng] = []
        need = {}
        for t in toks:
            if t[0] == "c" and t[1] == eng and t not in raw:
                continue
            key, v = self._resolve(t)
            if need.get(key, 0) < v:
                need[key] = v
        k = None
        if dma:
            n_hw = len(self.dsem) - self.n_sw
            if eng == "pool":
                k = n_hw + self.drr["sw"]
                self.drr["sw"] = (self.drr["sw"] + 1) % self.n_sw
            else:
                k = self.drr["hw"]
                self.drr["hw"] = (self.drr["hw"] + 1) % n_hw
            if self.dcnt[k]:
                need[("d", k)] = max(need.get(("d", k), 0), 16 * self.dcnt[k])
        E = self.E[eng]
        wd = self.waited[eng]
        for key, v in need.items():
            if wd.get(key, 0) < v:
                E.wait_ge(self.sem[key] if isinstance(key, str) else self.dsem[key[1]], v)
                wd[key] = v
        ins = fn(E)
        self.n_ins += 1
        if dma:
            self.dcnt[k] += 1
            ins.then_inc(self.dsem[k], 16)
            tok = ("d", k, 16 * self.dcnt[k])
            self.dma_open.append(tok)
        else:
            idx = self.nops[eng]
            self.nops[eng] += 1
            tok = ("c", eng, idx)
            if sig:
                self.cnt[eng] += 1
                ins.then_inc(self.sem[eng], 1)
                self.sigs[eng][0].append(idx)
                self.sigs[eng][1].append(self.cnt[eng])
            self.lastop[eng] = tok
        for r in reads:
            if dma:
                r.rdma.append(tok)
            else:
                r.rd[eng] = tok
        for w in writes:
            if partial:
                w.writers.append(tok)
            else:
                w.writers = [tok]
                w.rd = {}
                w.rdma = []
        return tok

    def barrier(self):
        toks = [t for t in self.lastop.values() if t is not None] + self.dma_open
        for e in self.E:
            self.pending[e] = self.pending[e] + toks
        self.dma_open = []

    def flush(self, eng):
        need = {}
        for t in self.pending[eng]:
            key, v = self._resolve(t)
            if need.get(key, 0) < v:
                need[key] = v
        self.pending[eng] = []
        E = self.E[eng]
        wd = self.waited[eng]
        for key, v in need.items():
            if wd.get(key, 0) < v:
                E.wait_ge(self.sem[key] if isinstance(key, str) else self.dsem[key[1]], v)
                wd[key] = v

    def dma(self, out, in_, reads=(), writes=(), q="sp", partial=False):
        return self.op(q, lambda e: e.dma_start(out=out, in_=in_), reads, writes, dma=True, partial=partial)

    def mm(self, out, lhsT, rhs, start, stop, reads=(), writes=(), sig=None, partial=None, **kw):
        if sig is None:
            sig = stop
        if partial is None:
            partial = not start
        return self.op("pe", lambda e: e.matmul(out, lhsT=lhsT, rhs=rhs, start=start, stop=stop, **kw), reads, writes,
                       sig=sig, partial=partial)

    def tr(self, out, in_, ident, reads=(), writes=(), sig=True, partial=False):
        return self.op("pe", lambda e: e.transpose(out=out, in_=in_, identity=ident), reads, writes, sig=sig, partial=partial)

    def act(self, out, in_, func, reads=(), writes=(), partial=False, **kw):
        return self.op("act", lambda e: e.activation(out=out, in_=in_, func=func, **kw), reads, writes, partial=partial)

    def tt(self, eng, out, in0, in1, op, reads=(), writes=(), partial=False):
        return self.op(eng, lambda e: e.tensor_tensor(out=out, in0=in0, in1=in1, op=op), reads, writes, partial=partial)

    def ts(self, eng, out, in0, s1, s2, op0, op1=None, reads=(), writes=(), partial=False, **kw):
        if op1 is None:
            return self.op(eng, lambda e: e.tensor_scalar(out=out, in0=in0, scalar1=s1, scalar2=None, op0=op0, **kw), reads, writes, partial=partial)
        return self.op(eng, lambda e: e.tensor_scalar(out=out, in0=in0, scalar1=s1, scalar2=s2, op0=op0, op1=op1, **kw), reads, writes, partial=partial)

    def stt(self, out, in0, scalar, in1, op0, op1, reads=(), writes=(), partial=False):
        return self.op("dve", lambda e: e.scalar_tensor_tensor(out=out, in0=in0, scalar=scalar, in1=in1, op0=op0, op1=op1), reads, writes, partial=partial)

    def cp(self, eng, out, in_, reads=(), writes=(), partial=False):
        if eng == "act":
            return self.op("act", lambda e: e.copy(out=out, in_=in_), reads, writes, partial=partial)
        return self.op(eng, lambda e: e.tensor_copy(out=out, in_=in_), reads, writes, partial=partial)


class Ring:
    def __init__(self, aps):
        self.aps = aps
        self.res = [Res() for _ in aps]
        self.i = 0

    def next(self):
        j = self.i % len(self.aps)
        self.i += 1
        return self.aps[j], self.res[j]


def rope_tables(n):
    rows = n // 64
    row = np.repeat(np.arange(rows), 64)
    col = np.tile(np.arange(64), rows)
    pos = np.stack([row, col], axis=-1).astype(np.float32)
    nf = 16
    inv = (np.float32(10000.0) ** (-np.arange(nf, dtype=np.float32) / np.float32(nf))).astype(np.float32)
    ang = np.broadcast_to(pos[:, :, None, None] * inv, (n, 2, 2, nf)).reshape(n, 64).astype(np.float32)
    cos = np.cos(ang).astype(np.float32)
    sin = np.sin(ang).astype(np.float32)
    sgn = np.tile(np.concatenate([-np.ones(16), np.ones(16)]), 2).astype(np.float32)
    sin = sin * sgn[None, :]
    cosT = np.ascontiguousarray(np.concatenate([cos.T, cos.T], axis=0))
    sinT = np.ascontiguousarray(np.concatenate([sin.T, sin.T], axis=0))
    return cosT, sinT


def host_consts(TL):
    cosT, sinT = rope_tables(TL)
    k = np.arange(128)
    tri_f = (k[:, None] <= k[None, :]).astype(np.float32)
    tri_b = (k[:, None] >= k[None, :]).astype(np.float32)
    blk = (k[:, None] // 64) == (k[None, :] // 64)
    gm_f = (tri_f * blk).astype(np.float32)
    gm_b = (tri_b * blk).astype(np.float32)
    rst = np.ones((128, 512), np.float32)
    rst[:, ::64] = 0.0
    cmat = np.concatenate([np.eye(128, dtype=np.float32), tri_f, tri_b, gm_f, gm_b, np.ones((128, 128), np.float32)], axis=1)
    return {"cst_cos": cosT, "cst_sin": sinT, "cst_mat": np.ascontiguousarray(cmat), "cst_rst": rst}


WEIGHT_SPECS = [
    ("w_mod", [DEPTH, D, 3 * D]), ("b_mod", [DEPTH, 3 * D]), ("norm_w", [DEPTH, D]), ("w_in", [DEPTH, D, IN_W]),
    ("da_lambda", [DEPTH, 4, 64]), ("da_norm_w", [DEPTH, 128]), ("w_out_da", [DEPTH, 512, D]),
    ("ssm_conv_w", [DEPTH, 3, 768]), ("ssm_conv_b", [DEPTH, 768]), ("ssm_dt_bias", [DEPTH, 2, 8]),
    ("ssm_a_log", [DEPTH, 2, 8]), ("ssm_d", [DEPTH, 2, 8]), ("ssm_norm_w", [DEPTH, 512]), ("w_out_ssm", [DEPTH, 512, D]),
    ("gla_w_gate", [DEPTH, 2, 16, 256]), ("gla_b_gate", [DEPTH, 2, 256]), ("gla_norm_w", [DEPTH, 128]),
    ("w_out_gla", [DEPTH, 512, D]), ("w_o", [DEPTH, D, D]), ("final_norm_w", [D]),
]


def build(TL, n_layers=DEPTH, dbg=(), stop_after=None):
    TALL = CTX + TL
    NT = TALL // 128
    NCH = TALL // 64
    nc = bass.Bass("TRN2", target_bir_lowering=False)

    def din(name, shape):
        return nc.dram_tensor(name, list(shape), F32, kind="ExternalInput").ap()

    _cnt = [0]

    def SBT(name, shape, dt):
        _cnt[0] += 1
        return nc.sbuf_tensor("%s_%d" % (name, _cnt[0]), shape, dt)

    x_in = din("x", [TL, D])
    c_in = din("c", [D])
    ctx_in = din("ctx", [CTX, D])
    cctx_in = din("c_ctx", [D])
    W = {n: din(n, s) for n, s in WEIGHT_SPECS}
    cst_cos = din("cst_cos", [128, TL])
    cst_sin = din("cst_sin", [128, TL])
    cst_mat = din("cst_mat", [128, 6 * 128])
    cst_rst = din("cst_rst", [128, 512])
    y_out = nc.dram_tensor("y", [TL, D], F32, kind="ExternalOutput").ap()

    def scr(name, shape, dt):
        kind = "ExternalOutput" if name in dbg else "Internal"
        return nc.dram_tensor(name, list(shape), dt, kind=kind).ap()

    S = dict(
        QT=scr("QT", [4, 128, TALL], BF16), KT=scr("KT", [4, 128, TALL], BF16), V=scr("V", [TALL, 512], BF16),
        ZAT=scr("ZAT", [512, TALL], BF16), ZB=scr("ZB", [TALL, 512], BF16), ZC=scr("ZC", [TALL, 512], BF16),
        XBC=scr("XBC", [768, TALL], F32), DT=scr("DT", [TALL, 16], F32), GV=scr("GV", [TALL, 512], BF16),
        GQ=scr("GQ", [2, 256, TALL], BF16), GK=scr("GK", [2, 256, TALL], BF16), GKE=scr("GKE", [2, TALL, 256], BF16),
        GEL=scr("GEL", [2, 256, NCH], F32), SG=scr("SG", [3, D, TALL], BF16),
        AOT=scr("AOT", [512, TALL], BF16), SO=scr("SO", [TALL, 512], BF16), GO=scr("GO", [TALL, 512], BF16),
        XR=scr("XR", [TL, D], F32), CXR=scr("CXR", [CTX, D], F32),
        XS=scr("XS", [TALL, 512], BF16), BT=scr("BT", [128, TALL], BF16), CT=scr("CT", [128, TALL], BF16),
        BTM=scr("BTM", [TALL, 128], BF16), YF=scr("YF", [TALL, 512], F32), GYF=scr("GYF", [TALL, 512], F32),
        MODT=scr("MODT", [128, 64], F32),
    )

    es = ExitStack()
    with es:
        es.enter_context(nc.allow_non_contiguous_dma(reason="tiny transposed parameter loads"))
        P = Prog(nc, es)
        cmat32 = es.enter_context(SBT("cmat32", [128, 768], F32))
        cmatb = es.enter_context(SBT("cmatb", [128, 768], BF16))
        rst = es.enter_context(SBT("rst", [128, 512], F32))
        cs = es.enter_context(SBT("cs", [128, 8, 2], F32))
        csb = es.enter_context(SBT("csb", [128, 8, 2, 128], F32))
        modA = es.enter_context(SBT("modA", [128, 2, 8], F32))
        modB = es.enter_context(SBT("modB", [128, 2, 8], F32))
        gate_bc = es.enter_context(SBT("gate_bc", [128, 2, D], F32))
        PSALL = es.enter_context(nc.psum_tensor("psall", [128, 8, 512], F32))
        psb = [PSALL[:, i, :] for i in range(8)]
        psr = [Res(excl=True) for _ in range(8)]
        ident32, trif32, trib32 = cmat32[:, 0:128], cmat32[:, 128:256], cmat32[:, 256:384]
        ones32 = cmat32[:, 640:768]
        identb = cmatb[:, 0:128]
        gmfb, gmbb = cmatb[:, 384:512], cmatb[:, 512:640]

        r0 = Res()
        P.dma(cmat32[:], cst_mat[:, :], writes=[r0])
        P.cp("dve", cmatb[:], cmat32[:], reads=[r0], writes=[Res()])
        P.dma(rst[:], cst_rst[:, :], writes=[Res()])
        r1 = Res()
        P.dma(cs[:, :, 0], c_in.rearrange("(k p) -> p k", p=128), writes=[r1], partial=True)
        P.dma(cs[:, :, 1], cctx_in.rearrange("(k p) -> p k", p=128), writes=[r1], partial=True)
        P.act(cs[:], cs[:], AF.Silu, reads=[r1], writes=[r1])
        for who in range(2):
            P.cp("dve", csb[:, :, who, :], cs[:, :, who:who + 1].to_broadcast([128, 8, 128]), reads=[r1], writes=[Res()])
        P.barrier()

        psi = [0]

        def nextps():
            j = psi[0] % 8
            psi[0] += 1
            return psb[j], psr[j]

        def tok_src(layer, i):
            if layer == 0:
                return ctx_in[i * 128:(i + 1) * 128, :] if i < 2 else x_in[(i - 2) * 128:(i - 1) * 128, :]
            return S["CXR"][i * 128:(i + 1) * 128, :] if i < 2 else S["XR"][(i - 2) * 128:(i - 1) * 128, :]

        groups = [(0, CTX, True)] + [(CTX + 512 * g, 512, False) for g in range(TL // 512)]

        for layer in range(n_layers):
            need_ctx = layer < DEPTH - 1
            w_in = W["w_in"][layer].rearrange("(k p) c -> p k c", p=128)
            with ExitStack() as ph:
                wm = [ph.enter_context(SBT("wm%d" % i, [128, 8, 512], F32)) for i in range(2)]
                wmr = Ring([t[:] for t in wm])
                bmT = ph.enter_context(SBT("bmT", [128, 24], F32))
                nwT = ph.enter_context(SBT("nwT", [128, 8], F32))
                bmg = ph.enter_context(SBT("bmg", [128, D], F32))
                modT = ph.enter_context(SBT("modT", [128, 24, 2], F32))
                rb, rn, rg, rm = Res(), Res(), Res(), Res()
                P.dma(bmT[:], W["b_mod"][layer].rearrange("(j p) -> p j", p=128), writes=[rb])
                P.dma(nwT[:], W["norm_w"][layer].rearrange("(j p) -> p j", p=128), writes=[rn])
                P.dma(bmg[:], W["b_mod"][layer][2 * D:3 * D].partition_broadcast(128), writes=[rg])
                w_mod = W["w_mod"][layer].rearrange("(k p) c -> p k c", p=128)
                for t in range(6):
                    wt, wr = wmr.next()
                    P.dma(wt, w_mod[:, :, t * 512:(t + 1) * 512], writes=[wr])
                    for jj in range(4):
                        j = t * 4 + jj
                        ps, pr = nextps()
                        for k in range(KC):
                            P.mm(ps[:, 0:2], lhsT=wt[:, k, jj * 128:(jj + 1) * 128], rhs=cs[:, k, :], start=(k == 0), stop=(k == KC - 1),
                                 reads=[wr], writes=[pr])
                        P.cp("dve", modT[:, j, :], ps[:, 0:2], reads=[pr], writes=[rm], partial=True)
                    if t >= 4:
                        for who in range(2 if need_ctx else 1):
                            ps, pr = nextps()
                            for k in range(KC):
                                P.mm(ps[:], lhsT=csb[:, k, who, :], rhs=wt[:, k, :], start=(k == 0), stop=(k == KC - 1),
                                     reads=[wr], writes=[pr])
                            P.tt("dve", gate_bc[:, who, (t - 4) * 512:(t - 3) * 512], ps[:], bmg[:, (t - 4) * 512:(t - 3) * 512], ALU.add,
                                 reads=[pr, rg], writes=[Res()])
                ra = Res()
                for who in range(2):
                    P.tt("dve", modB[:, who, :], modT[:, 0:8, who], bmT[:, 0:8], ALU.add, reads=[rm, rb], writes=[ra], partial=True)
                    P.tt("dve", modA[:, who, :], modT[:, 8:16, who], bmT[:, 8:16], ALU.add, reads=[rm, rb], writes=[ra], partial=True)
                    P.stt(modA[:, who, :], modA[:, who, :], 1.0, nwT[:], ALU.add, ALU.mult, reads=[ra, rn], writes=[ra])
                if "MODT" in dbg:
                    P.dma(S["MODT"][:, 0:16], modA[:].rearrange("p a b -> p (a b)"), reads=[ra])
                    P.dma(S["MODT"][:, 16:32], modB[:].rearrange("p a b -> p (a b)"), reads=[ra])
                P.barrier()
            if stop_after == "mod":
                break

            with ExitStack() as ph:
                hT = ph.enter_context(SBT("hT", [128, KC, TALL], BF16))
                with ExitStack() as ph2:
                    xts = [ph2.enter_context(SBT("xt%d" % i, [128, D], F32)) for i in range(2)]
                    xns = [ph2.enter_context(SBT("xn%d" % i, [128, D], BF16)) for i in range(2)]
                    junk = ph2.enter_context(SBT("junk", [128, D], BF16))
                    sst = ph2.enter_context(SBT("sst", [128, 2, 4], F32))
                    xr_ = Ring([t[:] for t in xts])
                    xn_ = Ring([t[:] for t in xns])
                    ss_ = Ring([sst[:, i, :] for i in range(2)])
                    jr = Res()
                    for i in range(NT):
                        xt, xr = xr_.next()
                        xn, xnr = xn_.next()
                        st, sr = ss_.next()
                        import os
                        stage = int(os.environ.get("DBG_1A", "9"))
                        P.dma(xt, tok_src(layer, i), writes=[xr])
                        if stage < 1: continue
                        P.act(junk[:], xt, AF.Square, reads=[xr], writes=[jr, sr], accum_out=st[:, 0:1])
                        if stage < 2: continue
                        P.act(st[:, 1:2], st[:, 0:1], AF.Sqrt, reads=[sr], writes=[sr], scale=1.0 / D, bias=EPS)
                        if stage < 3: continue
                        P.op("dve", lambda e, st=st: e.reciprocal(out=st[:, 2:3], in_=st[:, 1:2]), reads=[sr], writes=[sr])
                        P.ts("dve", xn, xt, st[:, 2:3], None, ALU.mult, reads=[xr, sr], writes=[xnr])
                        if stage < 4: continue
                        ps, pr = nextps()
                        pb = ps.bitcast(BF16).rearrange("p (j t) -> p j t", t=128)
                        for j in range(KC):
                            P.tr(pb[:, j, :], xn[:, j * 128:(j + 1) * 128], identb, reads=[xnr], writes=[pr], sig=(j == KC - 1), partial=(j > 0))
                        if stage < 5: continue
                        who = 1 if i < 2 else 0
                        for j in range(KC):
                            dst = hT[:, j, i * 128:(i + 1) * 128]
                            if i % 2 == 0:
                                P.act(dst, pb[:, j, :], AF.Identity, reads=[pr], scale=modA[:, who, j:j + 1], bias=modB[:, who, j:j + 1])
                            else:
                                P.ts("dve", dst, pb[:, j, :], modA[:, who, j:j + 1], modB[:, who, j:j + 1], ALU.mult, ALU.add, reads=[pr])
                    P.barrier()
                if stop_after == "norm":
                    break

                with ExitStack() as ph2:
                    wts = [ph2.enter_context(SBT("wt%d" % i, [128, KC, 512], BF16)) for i in range(2)]
                    wring = Ring([t[:] for t in wts])
                    stg = [ph2.enter_context(SBT("stg%d" % i, [128, 512], F32)) for i in range(4)]
                    sring = Ring([t[:] for t in stg])
                    sgb = [ph2.enter_context(SBT("sgb%d" % i, [128, 512], BF16)) for i in range(4)]
                    bring = Ring([t[:] for t in sgb])

                    def load_w(c0, n):
                        wt, wr = wring.next()
                        P.dma(wt[:, :, 0:n], w_in[:, :, c0:c0 + n], writes=[wr], q="pool")
                        return wt, wr

                    def mm_fm(wt, wr, cc, ncol, t0, n):
                        ps, pr = nextps()
                        for k in range(KC):
                            P.mm(ps[0:ncol, 0:n], lhsT=wt[:, k, cc:cc + ncol], rhs=hT[:, k, t0:t0 + n], start=(k == 0), stop=(k == KC - 1),
                                 reads=[wr], writes=[pr])
                        return ps, pr

                    def mm_tm(wt, wr, ncol, i):
                        ps, pr = nextps()
                        for k in range(KC):
                            P.mm(ps[:, 0:ncol], lhsT=hT[:, k, i * 128:(i + 1) * 128], rhs=wt[:, k, 0:ncol], start=(k == 0), stop=(k == KC - 1),
                                 reads=[wr], writes=[pr])
                        return ps, pr

                    for (c0, name, func) in ((C_V, "V", AF.Copy), (C_BZ, "ZB", AF.Silu), (C_GZ, "ZC", AF.Silu), (C_GV, "GV", AF.Copy)):
                        wt, wr = load_w(c0, 512)
                        for i in range(NT):
                            ps, pr = mm_tm(wt, wr, 512, i)
                            sb, sr = bring.next()
                            if func == AF.Copy and i % 2 == 1:
                                P.cp("dve", sb, ps[:], reads=[pr], writes=[sr])
                            else:
                                P.act(sb, ps[:], func, reads=[pr], writes=[sr])
                            P.dma(S[name][i * 128:(i + 1) * 128, :], sb, reads=[sr])
                    with ExitStack() as ph3:
                        dtb = ph3.enter_context(SBT("dtb", [128, 16], F32))
                        dtt = ph3.enter_context(SBT("dtt", [128, 4, 16], F32))
                        dring = Ring([dtt[:, i, :] for i in range(4)])
                        rdb = Res()
                        P.dma(dtb[:], W["ssm_dt_bias"][layer].rearrange("a b -> (a b)").partition_broadcast(128), writes=[rdb])
                        wt, wr = load_w(C_DT, 16)
                        for i in range(NT):
                            ps, pr = mm_tm(wt, wr, 16, i)
                            d_, dr = dring.next()
                            P.tt("dve", d_, ps[:, 0:16], dtb[:], ALU.add, reads=[pr, rdb], writes=[dr])
                            P.act(d_, d_, AF.Exp, reads=[dr], writes=[dr])
                            P.act(d_, d_, AF.Ln, reads=[dr], writes=[dr], bias=1.0)
                            P.dma(S["DT"][i * 128:(i + 1) * 128, :], d_, reads=[dr])
                    for (c0, n, row0) in ((C_BX, 512, 0), (C_BB, 256, 512)):
                        wt, wr = load_w(c0, n)
                        for (t0, nt, isc) in groups:
                            for cc in range(n // 128):
                                ps, pr = mm_fm(wt, wr, cc * 128, 128, t0, nt)
                                sb, sr = sring.next()
                                P.act(sb[:, 0:nt], ps[:, 0:nt], AF.Copy, reads=[pr], writes=[sr])
                                P.dma(S["XBC"][row0 + cc * 128:row0 + (cc + 1) * 128, t0:t0 + nt], sb[:, 0:nt], reads=[sr])
                    wt, wr = load_w(C_ZA, 512)
                    for (t0, nt, isc) in groups:
                        if isc and not need_ctx:
                            continue
                        for cc in range(4):
                            ps, pr = mm_fm(wt, wr, cc * 128, 128, t0, nt)
                            sb, sr = bring.next()
                            P.act(sb[:, 0:nt], ps[:, 0:nt], AF.Silu, reads=[pr], writes=[sr])
                            P.dma(S["ZAT"][cc * 128:(cc + 1) * 128, t0:t0 + nt], sb[:, 0:nt], reads=[sr])
                    for t in range(6):
                        wt, wr = load_w(C_SG + t * 512, 512)
                        for (t0, nt, isc) in groups:
                            if isc and not need_ctx:
                                continue
                            for cc in range(4):
                                ps, pr = mm_fm(wt, wr, cc * 128, 128, t0, nt)
                                sb, sr = bring.next()
                                P.act(sb[:, 0:nt], ps[:, 0:nt], AF.Sigmoid, reads=[pr], writes=[sr])
                                r_ = t * 512 + cc * 128
                                P.dma(S["SG"][r_ // D][r_ % D:r_ % D + 128, t0:t0 + nt], sb[:, 0:nt], reads=[sr])

                    with ExitStack() as ph3:
                        wg32 = ph3.enter_context(SBT("wg32", [16, 2, 256], F32))
                        wgb = ph3.enter_context(SBT("wgb", [16, 2, 256], BF16))
                        nbg = ph3.enter_context(SBT("nbg", [128, 2, 2], F32))
                        lrb = [ph3.enter_context(SBT("lrb%d" % i, [16, 512], BF16)) for i in range(2)]
                        qks = [ph3.enter_context(SBT("qks%d" % i, [128, 512], F32)) for i in range(4)]
                        tmpf = [ph3.enter_context(SBT("gtmp%d" % i, [128, 512], F32)) for i in range(5)]
                        tring = Ring([t[:] for t in tmpf])
                        egs = ph3.enter_context(SBT("egs", [128, 4, 8], F32))
                        ering = Ring([egs[:, i, :] for i in range(4)])
                        rw, rnb = Res(), Res()
                        P.dma(wg32[:], W["gla_w_gate"][layer].rearrange("d r c -> r d c"), writes=[rw])
                        P.cp("dve", wgb[:], wg32[:], reads=[rw], writes=[rw])
                        P.dma(nbg[:], W["gla_b_gate"][layer].rearrange("d (h p) -> p d h", p=128), writes=[rnb])
                        P.ts("dve", nbg[:], nbg[:], -1.0, None, ALU.mult, reads=[rnb], writes=[rnb])
                        wgq, wgqr = load_w(C_GQ, 256)
                        wgk, wgkr = load_w(C_GK, 256)
                        wlr_t = ph3.enter_context(SBT("wlr", [128, KC, 32], BF16))
                        wlr, wlrr = wlr_t[:], Res()
                        P.dma(wlr, w_in[:, :, C_LR:C_LR + 32], writes=[wlrr], q="pool")
                        lrr = [Res(), Res()]
                        qkr = [Res() for _ in range(4)]
                        for (t0, n, isc) in groups:
                            ncks = n // 64
                            for d in range(2):
                                ps, pr = mm_fm(wlr, wlrr, d * 16, 16, t0, n)
                                P.cp("dve", lrb[d][:, 0:n], ps[0:16, 0:n], reads=[pr], writes=[lrr[d]])
                            for hp in range(2):
                                ps, pr = mm_fm(wgq, wgqr, hp * 128, 128, t0, n)
                                P.act(qks[hp][:, 0:n], ps[:, 0:n], AF.Copy, reads=[pr], writes=[qkr[hp]])
                                ps, pr = mm_fm(wgk, wgkr, hp * 128, 128, t0, n)
                                P.act(qks[2 + hp][:, 0:n], ps[:, 0:n], AF.Copy, reads=[pr], writes=[qkr[2 + hp]])
                            for d in range(2):
                                for hp in range(2):
                                    ps, pr = nextps()
                                    P.mm(ps[:, 0:n], lhsT=wgb[:, d, hp * 128:(hp + 1) * 128], rhs=lrb[d][:, 0:n], start=True, stop=True,
                                         reads=[rw, lrr[d]], writes=[pr])
                                    A_, ar = tring.next()
                                    B_, br = tring.next()
                                    P.act(A_[:, 0:n], ps[:, 0:n], AF.Exp, reads=[pr, rnb], writes=[ar], scale=-1.0, bias=nbg[:, d, hp:hp + 1])
                                    P.act(A_[:, 0:n], A_[:, 0:n], AF.Ln, reads=[ar], writes=[ar], bias=1.0)
                                    if d == 0:
                                        so_, si_ = B_[:, 0:n], A_[:, 0:n]
                                    else:
                                        so_, si_ = B_[:, 0:n][:, ::-1], A_[:, 0:n][:, ::-1]
                                    P.op("dve", lambda e, so_=so_, si_=si_, n=n: e.tensor_tensor_scan(out=so_, data0=rst[:, 0:n], data1=si_, initial=0.0,
                                                                                         op0=ALU.mult, op1=ALU.add), reads=[ar], writes=[br])
                                    Bv = B_[:, 0:n].rearrange("p (c t) -> p c t", t=64)
                                    gl = Bv[:, :, 63:64] if d == 0 else Bv[:, :, 0:1]
                                    C_, cr_ = tring.next()
                                    P.act(C_[:, 0:n], B_[:, 0:n], AF.Exp, reads=[br], writes=[cr_], scale=-1.0 / 16.0)
                                    ob, obr = bring.next()
                                    P.stt(ob[:, 0:n], qks[hp][:, 0:n], 0.125, C_[:, 0:n], ALU.mult, ALU.mult, reads=[qkr[hp], cr_], writes=[obr])
                                    P.dma(S["GQ"][d][hp * 128:(hp + 1) * 128, t0:t0 + n], ob[:, 0:n], reads=[obr])
                                    C2, cr2 = tring.next()
                                    P.act(C2[:, 0:n], B_[:, 0:n], AF.Exp, reads=[br], writes=[cr2], scale=1.0 / 16.0)
                                    ob, obr = bring.next()
                                    P.tt("dve", ob[:, 0:n], qks[2 + hp][:, 0:n], C2[:, 0:n], ALU.mult, reads=[qkr[2 + hp], cr2], writes=[obr])
                                    P.dma(S["GK"][d][hp * 128:(hp + 1) * 128, t0:t0 + n], ob[:, 0:n], reads=[obr])
                                    D_, dr_ = tring.next()
                                    P.tt("dve", D_[:, 0:n].rearrange("p (c t) -> p c t", t=64), gl.to_broadcast([128, ncks, 64]), Bv, ALU.subtract,
                                         reads=[br], writes=[dr_])
                                    P.act(D_[:, 0:n], D_[:, 0:n], AF.Exp, reads=[dr_], writes=[dr_], scale=-1.0 / 16.0)
                                    ke, ker = bring.next()
                                    P.tt("dve", ke[:, 0:n], qks[2 + hp][:, 0:n], D_[:, 0:n], ALU.mult, reads=[qkr[2 + hp], dr_], writes=[ker])
                                    ps, pr = nextps()
                                    pb = ps.bitcast(BF16).rearrange("p (j t) -> p j t", t=128)
                                    nst = n // 128
                                    for j in range(nst):
                                        P.tr(pb[:, j, :], ke[:, j * 128:(j + 1) * 128], identb, reads=[ker], writes=[pr], sig=(j == nst - 1), partial=(j > 0))
                                    kt_, ktr = bring.next()
                                    P.cp("dve", kt_[:, 0:n], ps.bitcast(BF16)[:, 0:n], reads=[pr], writes=[ktr])
                                    P.dma(S["GKE"][d][t0:t0 + n, hp * 128:(hp + 1) * 128].rearrange("(j p) c -> p j c", p=128),
                                          kt_[:, 0:n].rearrange("p (j c) -> p j c", c=128), reads=[ktr])
                                    eg, egr = ering.next()
                                    P.act(eg[:, 0:ncks], gl.rearrange("p c o -> p (c o)"), AF.Exp, reads=[br], writes=[egr], scale=-1.0 / 16.0)
                                    P.dma(S["GEL"][d][hp * 128:(hp + 1) * 128, t0 // 64:t0 // 64 + ncks], eg[:, 0:ncks], reads=[egr])
                    with ExitStack() as ph3:
                        cst = [ph3.enter_context(SBT("cst%d" % i, [128, 2, 512], F32)) for i in range(2)]
                        cring = Ring([t[:] for t in cst])
                        for (c0, name) in ((C_Q, "QT"), (C_K, "KT")):
                            wt, wr = load_w(c0, 512)
                            wp, wpr = wring.next()
                            wv = wt.rearrange("p k (a h f) -> p k a h f", h=2, f=16)
                            wpv = wp.rearrange("p k (a h f) -> p k a h f", h=2, f=16)
                            for k in range(KC):
                                for h in range(2):
                                    P.cp("dve" if (k + h) % 2 else "pool", wpv[:, k, :, h, :], wv[:, k, :, 1 - h, :], reads=[wr], writes=[wpr], partial=(k + h > 0))
                            for (t0, nt, isc) in groups:
                                if not isc:
                                    ct, cr = cring.next()
                                    P.dma(ct[:, 0, :], cst_cos[:, t0 - CTX:t0 - CTX + 512], writes=[cr], partial=True)
                                    P.dma(ct[:, 1, :], cst_sin[:, t0 - CTX:t0 - CTX + 512], writes=[cr], partial=True)
                                for cp_ in range(4):
                                    ps, pr = mm_fm(wt, wr, cp_ * 128, 128, t0, nt)
                                    ob, obr = bring.next()
                                    if isc:
                                        P.act(ob[:, 0:nt], ps[:, 0:nt], AF.Copy, reads=[pr], writes=[obr])
                                    else:
                                        ps2, pr2 = mm_fm(wp, wpr, cp_ * 128, 128, t0, nt)
                                        s1, s1r = sring.next()
                                        s2, s2r = sring.next()
                                        P.tt("dve", s1, ps[:], ct[:, 0, :], ALU.mult, reads=[pr, cr], writes=[s1r])
                                        P.tt("dve", s2, ps2[:], ct[:, 1, :], ALU.mult, reads=[pr2, cr], writes=[s2r])
                                        P.tt("pool", ob, s1, s2, ALU.add, reads=[s1r, s2r], writes=[obr])
                                    P.dma(S[name][cp_][:, t0:t0 + nt], ob[:, 0:nt], reads=[obr])
                    P.barrier()
            if stop_after == "proj":
                break


            lam_init = 0.8 - 0.6 * math.exp(-0.3 * layer)
            with ExitStack() as ph:
                KTs = ph.enter_context(SBT("KTs", [128, 4, TALL], BF16))
                Vs = ph.enter_context(SBT("Vs", [128, NT, 4, 130], BF16))
                lamt = ph.enter_context(SBT("lamt", [128, 4, 64], F32))
                lsc = ph.enter_context(SBT("lsc", [128, 8], F32))
                nwc = ph.enter_context(SBT("nwc", [128, 1], F32))

                def mk(name, shape, dt, n):
                    ts_ = [ph.enter_context(SBT("%s%d" % (name, i), shape, dt)) for i in range(n)]
                    return Ring([t[:] for t in ts_])
                qring = mk("qt", [128, 2, 256], BF16, 3)
                for qa, qres in zip(qring.aps, qring.res):
                    P.op("pool", lambda e, qa=qa: e.memset(qa, 0.0), writes=[qres])
                pring = mk("pt", [128, 2, 256], BF16, 3)
                zring = mk("za", [128, 256], BF16, 3)
                rsring = mk("ars", [128, 512], F32, 2)
                t0ring = mk("at0", [128, 512], F32, 2)
                oring = mk("ao_", [128, 256], F32, 2)
                sqring = mk("asq", [128, 256], F32, 2)
                msring = mk("ams", [128, 256], F32, 2)
                aoring = mk("aob", [128, 256], BF16, 2)
                rk, rv, rl, rnw = Res(), Res(), Res(), Res()
                for h in range(4):
                    P.dma(KTs[:, h, :], S["KT"][h], writes=[rk], partial=True)
                for i in range(NT):
                    P.dma(Vs[:, i, :, 0:128], S["V"][i * 128:(i + 1) * 128, :].rearrange("p (h v) -> p h v", v=128), writes=[rv], partial=True)
                P.dma(lamt[:], W["da_lambda"][layer].rearrange("a b -> (a b)").partition_broadcast(128), writes=[rl])
                P.dma(nwc[:], W["da_norm_w"][layer].rearrange("(p o) -> p o", o=1), writes=[rnw])
                P.ts("dve", nwc[:], nwc[:], 1.0 - lam_init, None, ALU.mult, reads=[rnw], writes=[rnw])
                P.stt(lamt[:, 0, :], lamt[:, 0, :], 1.0, lamt[:, 1, :], ALU.mult, ALU.mult, reads=[rl], writes=[rl])
                P.stt(lamt[:, 2, :], lamt[:, 2, :], 1.0, lamt[:, 3, :], ALU.mult, ALU.mult, reads=[rl], writes=[rl])
                P.op("dve", lambda e: e.reduce_sum(out=lsc[:, 0:1], in_=lamt[:, 0, :], axis=AX.X), reads=[rl], writes=[rl])
                P.op("dve", lambda e: e.reduce_sum(out=lsc[:, 1:2], in_=lamt[:, 2, :], axis=AX.X), reads=[rl], writes=[rl])
                P.act(lsc[:, 2:4], lsc[:, 0:2], AF.Exp, reads=[rl], writes=[rl])
                P.tt("dve", lsc[:, 4:5], lsc[:, 3:4], lsc[:, 2:3], ALU.subtract, reads=[rl], writes=[rl])
                P.ts("dve", lsc[:, 4:5], lsc[:, 4:5], -lam_init, None, ALU.add, reads=[rl], writes=[rl])
                neglam = lsc[:, 4:5]
                onesb = cmatb[:, 640:768]
                acc_set = [0]
                pend_epi = [None]

                def flush_epi():
                    if pend_epi[0] is not None:
                        for _ in pend_epi[0]:
                            pass
                        pend_epi[0] = None

                def epilogue(h, q0, OUT, outr, SUM, sumr, za, zr):
                    rs, rsr = rsring.next()
                    P.op("dve", lambda e: e.reciprocal(out=rs, in_=SUM), reads=[sumr], writes=[rsr])
                    t0_, t0r = t0ring.next()
                    P.tt("dve", t0_, OUT, rs, ALU.mult, reads=[outr, rsr], writes=[t0r])
                    o_, o_r = oring.next()
                    P.stt(o_, t0_[:, 256:512], neglam, t0_[:, 0:256], ALU.mult, ALU.add, reads=[t0r, rl], writes=[o_r])
                    sq, sqr = sqring.next()
                    P.tt("pool", sq, o_, o_, ALU.mult, reads=[o_r], writes=[sqr])
                    yield 1
                    P.mm(SUM[:, 0:256], lhsT=ones32, rhs=sq, start=True, stop=True, reads=[sqr], writes=[sumr])
                    ms, msr = msring.next()
                    P.ts("dve", ms, SUM[:, 0:256], 1.0 / 128.0, EPS, ALU.mult, ALU.add, reads=[sumr], writes=[msr])
                    yield 2
                    P.act(ms, ms, AF.Ln, reads=[msr], writes=[msr])
                    P.act(ms, ms, AF.Exp, reads=[msr], writes=[msr], scale=-0.5)
                    yield 3
                    P.stt(o_, o_, nwc[:, 0:1], ms, ALU.mult, ALU.mult, reads=[o_r, msr, rnw], writes=[o_r])
                    ao, aor = aoring.next()
                    P.tt("dve", ao, o_, za, ALU.mult, reads=[o_r, zr], writes=[aor])
                    P.dma(S["AOT"][h * 128:(h + 1) * 128, q0:q0 + 256], ao, reads=[aor])

                tiles = []
                for h in range(4):
                    if need_ctx:
                        tiles.append((h, 0, [0, 1]))
                    for qi in range(TL // 256):
                        tiles.append((h, CTX + qi * 256, list(range(NT))))
                flat = [(ti, ii) for ti, (h_, q0_, kbs_) in enumerate(tiles) for ii in range(len(kbs_))]
                tstate = {}

                def tile_setup(ti):
                    h, q0, kbs = tiles[ti]
                    si = ti % 2
                    qt, qr = qring.next()
                    P.dma(qt[0:64, 0, :], S["QT"][h][0:64, q0:q0 + 256], writes=[qr], partial=True)
                    P.dma(qt[64:128, 1, :], S["QT"][h][64:128, q0:q0 + 256], writes=[qr], partial=True)
                    za, zr = zring.next()
                    P.dma(za, S["ZAT"][h * 128:(h + 1) * 128, q0:q0 + 256], writes=[zr])
                    tstate[ti] = (psb[4 + 2 * si], psr[4 + 2 * si], psb[5 + 2 * si], psr[5 + 2 * si], qt.rearrange("p c q -> p (c q)"), qr, za, zr)

                def emit_qk(f):
                    ti, ii = flat[f]
                    h, q0, kbs = tiles[ti]
                    kb = kbs[ii]
                    b_ = f % 3
                    P.mm(psb[b_], lhsT=KTs[:, h, kb * 128:(kb + 1) * 128], rhs=tstate[ti][4], start=True, stop=True, reads=[rk, tstate[ti][5]], writes=[psr[b_]])

                tile_setup(0)
                emit_qk(0)
                if len(flat) > 1:
                    if flat[1][0] != 0:
                        tile_setup(flat[1][0])
                    emit_qk(1)
                for f, (ti, ii) in enumerate(flat):
                    h, q0, kbs = tiles[ti]
                    nk = len(kbs)
                    kb = kbs[ii]
                    OUT, outr, SUM, sumr, qtf, qr, za, zr = tstate[ti]
                    if ii == 0 and ti + 1 < len(tiles) and (ti + 1) not in tstate:
                        tile_setup(ti + 1)
                    if f + 2 < len(flat):
                        if flat[f + 2][0] not in tstate:
                            tile_setup(flat[f + 2][0])
                        emit_qk(f + 2)
                    b_ = f % 3
                    pt, ptr = pring.next()
                    ptf = pt.rearrange("p c q -> p (c q)")
                    P.act(ptf, psb[b_], AF.Exp, reads=[psr[b_]], writes=[ptr], scale=0.125)
                    P.mm(OUT, lhsT=Vs[:, kb, h, 0:128], rhs=ptf, start=(ii == 0), stop=(ii == nk - 1), reads=[ptr, rv], writes=[outr])
                    P.mm(SUM, lhsT=onesb, rhs=ptf, start=(ii == 0), stop=(ii == nk - 1), reads=[ptr], writes=[sumr])
                    if ii in (8, 28, 40, 50) and pend_epi[0] is not None:
                        if next(pend_epi[0], "done") == "done":
                            pend_epi[0] = None
                    if ii == nk - 1:
                        flush_epi()
                        pend_epi[0] = epilogue(h, q0, OUT, outr, SUM, sumr, za, zr)
                        del tstate[ti]
                flush_epi()
                P.barrier()
            if stop_after == "attn":
                break


            with ExitStack() as ph:
                cw = ph.enter_context(SBT("cw", [128, 6, 3], F32))
                cb = ph.enter_context(SBT("cb", [128, 6], F32))
                xins = [ph.enter_context(SBT("xin%d" % i, [128, 516], F32)) for i in range(5)]
                xiring = Ring([t[:] for t in xins])
                cacc = [ph.enter_context(SBT("cacc%d" % i, [128, 512], F32)) for i in range(3)]
                caring = Ring([t[:] for t in cacc])
                cyb = [ph.enter_context(SBT("cyb%d" % i, [128, 512], BF16)) for i in range(3)]
                cyring = Ring([t[:] for t in cyb])
                ctb = [ph.enter_context(SBT("ctb%d" % i, [128, 512], BF16)) for i in range(3)]
                ctring = Ring([t[:] for t in ctb])
                rcw = Res()
                for j in range(3):
                    P.dma(cw[:, :, j], W["ssm_conv_w"][layer][j].rearrange("(f p) -> p f", p=128), writes=[rcw], partial=True)
                P.dma(cb[:], W["ssm_conv_b"][layer].rearrange("(f p) -> p f", p=128), writes=[rcw], partial=True)
                items = [(t0, n, isc, fc) for (t0, n, isc) in groups for fc in range(6)]
                loaded = {}

                def issue_xin(i):
                    t0, n, isc, fc = items[i]
                    seg_lo, seg_hi = (0, CTX) if isc else (CTX, TALL)
                    lo = max(t0 - 1, seg_lo)
                    hi = min(t0 + n + 1, seg_hi)
                    xin, xir = xiring.next()
                    if lo > t0 - 1:
                        P.op("pool", lambda e, xin=xin: e.memset(xin[:, 0:1], 0.0), writes=[xir])
                    if hi < t0 + n + 1:
                        P.op("pool", lambda e, xin=xin, n=n: e.memset(xin[:, n + 1:n + 2], 0.0), writes=[xir], partial=True)
                    P.dma(xin[:, lo - (t0 - 1):hi - (t0 - 1)], S["XBC"][fc * 128:(fc + 1) * 128, lo:hi], writes=[xir], partial=True)
                    loaded[i] = (xin, xir)
                for i in range(min(3, len(items))):
                    issue_xin(i)
                for i, (t0, n, isc, fc) in enumerate(items):
                    if True:
                        if i + 3 < len(items):
                            issue_xin(i + 3)
                        xin, xir = loaded.pop(i)
                        ca, car = caring.next()
                        P.ts("dve", ca[:, 0:n], xin[:, 1:n + 1], cw[:, fc, 1:2], cb[:, fc:fc + 1], ALU.mult, ALU.add, reads=[xir, rcw], writes=[car])
                        P.stt(ca[:, 0:n], xin[:, 0:n], cw[:, fc, 0:1], ca[:, 0:n], ALU.mult, ALU.add, reads=[xir, rcw, car], writes=[car])
                        P.stt(ca[:, 0:n], xin[:, 2:n + 2], cw[:, fc, 2:3], ca[:, 0:n], ALU.mult, ALU.add, reads=[xir, rcw, car], writes=[car])
                        cy, cyr = cyring.next()
                        P.act(cy[:, 0:n], ca[:, 0:n], AF.Silu, reads=[car], writes=[cyr])
                        if fc == 4:
                            P.dma(S["BT"][:, t0:t0 + n], cy[:, 0:n], reads=[cyr])
                        if fc == 5:
                            P.dma(S["CT"][:, t0:t0 + n], cy[:, 0:n], reads=[cyr])
                            continue
                        ps, pr = nextps()
                        pb = ps.bitcast(BF16).rearrange("p (j t) -> p j t", t=128)
                        nst = n // 128
                        for j in range(nst):
                            P.tr(pb[:, j, :], cy[:, j * 128:(j + 1) * 128], identb, reads=[cyr], writes=[pr], sig=(j == nst - 1), partial=(j > 0))
                        ct_, ctr = ctring.next()
                        P.cp("dve", ct_[:, 0:n], ps.bitcast(BF16)[:, 0:n], reads=[pr], writes=[ctr])
                        if fc < 4:
                            dst = S["XS"][t0:t0 + n, fc * 128:(fc + 1) * 128]
                        else:
                            dst = S["BTM"][t0:t0 + n, :]
                        P.dma(dst.rearrange("(j p) c -> p j c", p=128), ct_[:, 0:n].rearrange("p (j c) -> p j c", c=128), reads=[ctr])
                P.barrier()
            if stop_after == "ssmprep":
                break

            with ExitStack() as ph:
                XSs = ph.enter_context(SBT("XSs", [128, NT, 512], BF16))
                BTs = ph.enter_context(SBT("BTs", [128, TALL], BF16))
                CTs = ph.enter_context(SBT("CTs", [128, TALL], BF16))
                BTMs = ph.enter_context(SBT("BTMs", [128, NT, 128], BF16))
                DTs = ph.enter_context(SBT("DTs", [128, NT, 16], F32))
                aneg = ph.enter_context(SBT("aneg", [128, 16], F32))
                dsk = ph.enter_context(SBT("dsk", [128, 2, 8], F32))
                snw = ph.enter_context(SBT("snw", [128, 512], F32))
                hs32 = ph.enter_context(SBT("hs32", [128, 2, 4, 64], F32))
                hsb = ph.enter_context(SBT("hsb", [128, 2, 4, 64], BF16))

                def mk(name, shape, dt, n):
                    ts_ = [ph.enter_context(SBT("%s%d" % (name, i), shape, dt)) for i in range(n)]
                    return Ring([t[:] for t in ts_])
                dAr = mk("dA", [128, 16], F32, 4)
                xdtr = mk("xdt", [128, 8, 64], BF16, 3)
                rcr = mk("rcum", [128, 4, 128], F32, 4)
                ssr = mk("ssub", [128, 4, 128], F32, 4)
                scr_ = mk("scm", [128, 128], F32, 4)
                wr_ = mk("wmat", [128, 4, 128], BF16, 4)
                ear = mk("eacs", [128, 4, 128], F32, 4)
                cdr = mk("cdec", [128, 4, 128], BF16, 4)
                xder = mk("xdec", [128, 4, 64], BF16, 4)
                yr_ = mk("ysb", [128, 512], F32, 3)
                yfr = mk("yfl", [128, 512], F32, 2)
                zbr = mk("zbl", [128, 512], BF16, 2)
                y2r = mk("y2", [128, 512], F32, 2)
                sor = mk("sob", [128, 512], BF16, 2)
                smr_ = mk("ssm_sm", [128, 8], F32, 3)
                r_res, r_par = Res(), Res()
                for i0 in range(0, NT, 8):
                    i1 = min(NT, i0 + 8)
                    P.dma(XSs[:, i0:i1, :], S["XS"][i0 * 128:i1 * 128, :].rearrange("(i p) c -> p i c", p=128), writes=[r_res], partial=True)
                P.dma(BTs[:], S["BT"], writes=[r_res], partial=True)
                P.dma(CTs[:], S["CT"], writes=[r_res], partial=True)
                P.dma(BTMs[:], S["BTM"].rearrange("(i p) c -> p i c", p=128), writes=[r_res], partial=True)
                P.dma(DTs[:], S["DT"].rearrange("(i p) c -> p i c", p=128), writes=[r_res], partial=True)
                P.dma(aneg[:], W["ssm_a_log"][layer].rearrange("a b -> (a b)").partition_broadcast(128), writes=[r_par], partial=True)
                P.dma(dsk[:], W["ssm_d"][layer].rearrange("a b -> (a b)").partition_broadcast(128), writes=[r_par], partial=True)
                P.dma(snw[:], W["ssm_norm_w"][layer].partition_broadcast(128), writes=[r_par], partial=True)
                P.act(aneg[:], aneg[:], AF.Exp, reads=[r_par], writes=[r_par])
                P.ts("dve", aneg[:], aneg[:], -1.0, None, ALU.mult, reads=[r_par], writes=[r_par])
                P.tt("dve", dsk[:, 0, :], dsk[:, 0, :], dsk[:, 1, :], ALU.add, reads=[r_par], writes=[r_par])
                ydone = {}

                def ssd_pass(d):
                    rh = Res()
                    hs32d, hsbd = hs32[:, d], hsb[:, d]
                    tri32 = trif32 if d == 0 else trib32
                    lend = 127 if d == 0 else 0
                    order = [0, 1] + list(range(2, NT)) if d == 0 else [1, 0] + list(range(NT - 1, 1, -1))
                    for ci, c in enumerate(order):
                        first = (ci == 0)
                        tsl = slice(c * 128, (c + 1) * 128)
                        dA, dAr_ = dAr.next()
                        P.tt("dve", dA[:, 0:8], DTs[:, c, d * 8:(d + 1) * 8], aneg[:, d * 8:(d + 1) * 8], ALU.mult, reads=[r_res, r_par], writes=[dAr_])
                        ps, pr = nextps()
                        P.mm(ps[:, 0:8], lhsT=tri32, rhs=dA[:, 0:8], start=True, stop=True, reads=[dAr_], writes=[pr])
                        xdt, xdtr_ = xdtr.next()
                        P.tt("dve", xdt, XSs[:, c, :].rearrange("p (h q) -> p h q", q=64), DTs[:, c, d * 8:(d + 1) * 8].unsqueeze(2).to_broadcast([128, 8, 64]),
                             ALU.mult, reads=[r_res], writes=[xdtr_])
                        rcs = []
                        for g in range(2):
                            rc, rcr_ = rcr.next()
                            P.tt("pool", rc, tri32.unsqueeze(1).to_broadcast([128, 4, 128]), dA[:, g * 4:(g + 1) * 4].unsqueeze(2).to_broadcast([128, 4, 128]),
                                 ALU.mult, reads=[dAr_], writes=[rcr_])
                            rcs.append((rc, rcr_))
                        yield
                        P.cp("dve", dA[:, 8:16], ps[:, 0:8], reads=[pr], writes=[dAr_], partial=True)
                        sps, spr = nextps()
                        yps = []
                        for g in range(2):
                            gs = slice(g * 64, (g + 1) * 64)
                            ps, pr = nextps()
                            P.mm(ps[:, 0:128], lhsT=BTs[gs, tsl], rhs=CTs[gs, tsl], start=True, stop=True, reads=[r_res], writes=[pr])
                            rc, rcr_ = rcs[g]
                            ps2, pr2 = nextps()
                            P.mm(ps2[:], lhsT=ones32, rhs=rc.rearrange("p h l -> p (h l)"), start=True, stop=True, reads=[rcr_], writes=[pr2])
                            yield
                            scm, scmr = scr_.next()
                            P.tt("dve", scm, ps[:, 0:128], tri32, ALU.mult, reads=[pr], writes=[scmr])
                            ps, pr = ps2, pr2
                            psv = ps.rearrange("p (h l) -> p h l", l=128)
                            ss, ssr_ = ssr.next()
                            for h in range(4):
                                P.ts("dve", ss[:, h, :], psv[:, h, :], dA[:, 8 + g * 4 + h:9 + g * 4 + h], 0.0, ALU.subtract, ALU.min, reads=[pr, dAr_], writes=[ssr_],
                                     partial=(h > 0))
                            ea, ear_ = ear.next()
                            P.act(ea[gs], psv[gs], AF.Exp, reads=[pr], writes=[ear_])
                            P.act(ss, ss, AF.Exp, reads=[ssr_], writes=[ssr_])
                            yield
                            wm_, wmr_ = wr_.next()
                            P.tt("dve", wm_, ss, scm.unsqueeze(1).to_broadcast([128, 4, 128]), ALU.mult, reads=[ssr_, scmr], writes=[wmr_])
                            yp, ypr = nextps()
                            yps.append((yp, ypr))
                            if not first:
                                cd, cdr_ = cdr.next()
                                P.tt("dve", cd[gs], ea[gs], CTs[gs, tsl].unsqueeze(1).to_broadcast([64, 4, 128]), ALU.mult, reads=[ear_, r_res], writes=[cdr_])
                            xde, xder_ = xder.next()
                            P.tt("dve", xde, xdt[:, g * 4:(g + 1) * 4, :], ss[:, :, lend:lend + 1].to_broadcast([128, 4, 64]), ALU.mult, reads=[xdtr_, ssr_], writes=[xder_])
                            yield
                            for h in range(4):
                                P.mm(yp[:, h * 64:(h + 1) * 64], lhsT=wm_[:, h, :], rhs=xdt[:, g * 4 + h, :], start=True, stop=first,
                                     reads=[wmr_, xdtr_], writes=[ypr], sig=(first and h == 3), partial=(h > 0))
                                if not first:
                                    P.mm(yp[:, h * 64:(h + 1) * 64], lhsT=cd[gs, h, :], rhs=hsbd[gs, h, :], start=False, stop=True,
                                         reads=[cdr_, rh], writes=[ypr], sig=(h == 3), partial=True)
                            P.mm(sps[gs, 0:256], lhsT=BTMs[:, c, gs], rhs=xde.rearrange("p h q -> p (h q)"), start=True, stop=True,
                                 reads=[r_res, xder_], writes=[spr], partial=(g > 0))
                            yield
                            if first:
                                P.cp("dve", hs32d[gs], sps[gs, 0:256].rearrange("p (h q) -> p h q", q=64), reads=[spr], writes=[rh], partial=(g > 0))
                            else:
                                P.tt("dve", hs32d[gs], hs32d[gs], ea[gs, :, lend:lend + 1].to_broadcast([64, 4, 64]), ALU.mult, reads=[rh, ear_], writes=[rh], partial=True)
                                P.tt("dve", hs32d[gs], hs32d[gs], sps[gs, 0:256].rearrange("p (h q) -> p h q", q=64), ALU.add, reads=[rh, spr], writes=[rh], partial=True)
                            P.act(hsbd[gs], hs32d[gs], AF.Copy, reads=[rh], writes=[rh], partial=True)
                        if c not in ydone:
                            ysb, ysr = yr_.next()
                            for g in range(2):
                                P.act(ysb[:, g * 256:(g + 1) * 256], yps[g][0][:, 0:256], AF.Copy, reads=[yps[g][1]], writes=[ysr], partial=(g > 0))
                            yield
                            ydone[c] = Res()
                            P.dma(S["YF"][tsl, :], ysb, reads=[ysr], writes=[ydone[c]])
                        elif c >= 2 or need_ctx:
                            yf, yfr_ = yfr.next()
                            zb, zbr_ = zbr.next()
                            P.dma(yf, S["YF"][tsl, :], reads=[ydone[c]], writes=[yfr_])
                            P.dma(zb, S["ZB"][tsl, :], writes=[zbr_])
                            y2, y2r_ = y2r.next()
                            for g in range(2):
                                P.tt("dve", y2[:, g * 256:(g + 1) * 256], yps[g][0][:, 0:256], yf[:, g * 256:(g + 1) * 256], ALU.add, reads=[yps[g][1], yfr_], writes=[y2r_],
                                     partial=(g > 0))
                            ysb, ysr = yr_.next()
                            P.tt("pool", ysb.rearrange("p (h q) -> p h q", q=64), XSs[:, c, :].rearrange("p (h q) -> p h q", q=64),
                                 dsk[:, 0, :].unsqueeze(2).to_broadcast([128, 8, 64]), ALU.mult, reads=[r_res, r_par], writes=[ysr])
                            yield
                            P.tt("dve", y2, y2, ysb, ALU.add, reads=[y2r_, ysr], writes=[y2r_])
                            P.tt("dve", y2, y2, zb, ALU.mult, reads=[y2r_, zbr_], writes=[y2r_])
                            sm, smr2 = smr_.next()
                            for g in range(2):
                                P.op("dve", lambda e, g=g, ysb=ysb, y2=y2, sm=sm: e.scalar_tensor_tensor(out=ysb[:, g * 256:(g + 1) * 256], in0=y2[:, g * 256:(g + 1) * 256], scalar=1.0,
                                                                      in1=y2[:, g * 256:(g + 1) * 256], op0=ALU.mult, op1=ALU.mult, accum_out=sm[:, g:g + 1]),
                                     reads=[y2r_], writes=[ysr, smr2], partial=(g > 0))
                            P.ts("dve", sm[:, 2:4], sm[:, 0:2], 1.0 / 256.0, EPS, ALU.mult, ALU.add, reads=[smr2], writes=[smr2])
                            yield
                            P.act(sm[:, 4:6], sm[:, 2:4], AF.Ln, reads=[smr2], writes=[smr2])
                            P.act(sm[:, 6:8], sm[:, 4:6], AF.Exp, reads=[smr2], writes=[smr2], scale=-0.5)
                            yield
                            so, sor_ = sor.next()
                            for g in range(2):
                                P.stt(so[:, g * 256:(g + 1) * 256], y2[:, g * 256:(g + 1) * 256], sm[:, 6 + g:7 + g], snw[:, g * 256:(g + 1) * 256], ALU.mult, ALU.mult,
                                      reads=[y2r_, smr2, r_par], writes=[sor_], partial=(g > 0))
                            yield
                            P.dma(S["SO"][tsl, :], so, reads=[sor_])
                        yield

                gens = [ssd_pass(0), ssd_pass(1)]
                while gens:
                    for g_ in list(gens):
                        if next(g_, "done") == "done":
                            gens.remove(g_)
                P.barrier()
            if stop_after == "ssd":
                break


            with ExitStack() as ph:
                GQa = ph.enter_context(SBT("GQs", [128, 2, 2, TALL], BF16))
                GKa = ph.enter_context(SBT("GKs", [128, 2, 2, TALL], BF16))
                GELa = ph.enter_context(SBT("GELs", [128, 2, 2, NCH], F32))
                gnw = ph.enter_context(SBT("gnw", [128, 128], F32))
                ghsa = ph.enter_context(SBT("ghs", [128, 2, 2, 128], F32))

                def mk(name, shape, dt, n):
                    ts_ = [ph.enter_context(SBT("%s%d" % (name, i), shape, dt)) for i in range(n)]
                    return Ring([t[:] for t in ts_])
                gvr = mk("gvl", [128, 512], BF16, 5)
                gker = mk("gkel", [128, 256], BF16, 5)
                atr = mk("attm", [128, 4, 128], BF16, 4)
                hyr = mk("ghy", [128, 2, 128], F32, 4)
                hxbr = mk("ghxb", [128, 2, 128], BF16, 4)
                hybr = mk("ghyb", [128, 2, 128], BF16, 4)
                gyr = mk("gysb", [128, 512], F32, 4)
                gyfr = mk("gyfl", [128, 512], F32, 3)
                zcr = mk("zcl", [128, 512], BF16, 3)
                gy2r = mk("gy2", [128, 512], F32, 3)
                gor = mk("gob", [128, 512], BF16, 3)
                gsmr = mk("gsm", [128, 16], F32, 3)
                rg_par = Res()
                P.dma(gnw[:], W["gla_norm_w"][layer].partition_broadcast(128), writes=[rg_par])
                rq = Res()
                for d in range(2):
                    for hp in range(2):
                        P.dma(GQa[:, d, hp, :], S["GQ"][d][hp * 128:(hp + 1) * 128, :], writes=[rq], partial=True)
                        P.dma(GKa[:, d, hp, :], S["GK"][d][hp * 128:(hp + 1) * 128, :], writes=[rq], partial=True)
                        P.dma(GELa[:, d, hp, :], S["GEL"][d][hp * 128:(hp + 1) * 128, :], writes=[rq], partial=True)
                gdone = {}

                def gla_pass(d):
                    GQs, GKs, GELs, ghs = GQa[:, d], GKa[:, d], GELa[:, d], ghsa[:, d]
                    gmask = gmfb if d == 0 else gmbb
                    order = [0, 1] + list(range(2, NT)) if d == 0 else [1, 0] + list(range(NT - 1, 1, -1))
                    rgh = Res()
                    loads = {}

                    def issue_loads(c):
                        gv, gvr_ = gvr.next()
                        gke, gker_ = gker.next()
                        P.dma(gv, S["GV"][c * 128:(c + 1) * 128, :], writes=[gvr_])
                        P.dma(gke, S["GKE"][d][c * 128:(c + 1) * 128, :], writes=[gker_])
                        loads[c] = (gv, gvr_, gke, gker_)
                    issue_loads(order[0])
                    for ci, c in enumerate(order):
                        first = (ci == 0)
                        tsl = slice(c * 128, (c + 1) * 128)
                        if ci + 1 < len(order):
                            issue_loads(order[ci + 1])
                        gv, gvr_, gke, gker_ = loads.pop(c)
                        X, Y = (0, 1) if d == 0 else (1, 0)
                        chX, chY = 2 * c + X, 2 * c + Y
                        attm, atr_ = atr.next()
                        aps = []
                        for j in range(2):
                            js = slice(j * 64, (j + 1) * 64)
                            ps, pr = nextps()
                            for hp in range(2):
                                P.mm(ps[:, hp * 128:(hp + 1) * 128], lhsT=GKs[js, hp, tsl], rhs=GQs[js, hp, tsl], start=True, stop=True, reads=[rq], writes=[pr],
                                     partial=(hp > 0))
                            aps.append((ps, pr))
                        sp = []
                        for ch in range(2):
                            cs_ = slice(ch * 64, (ch + 1) * 64)
                            ps, pr = nextps()
                            for hp in range(2):
                                for j in range(2):
                                    hd = hp * 2 + j
                                    P.mm(ps[j * 64:(j + 1) * 64, hp * 128:(hp + 1) * 128], lhsT=gke[cs_, hd * 64:(hd + 1) * 64], rhs=gv[cs_, hd * 128:(hd + 1) * 128],
                                         start=True, stop=True, reads=[gker_, gvr_], writes=[pr], partial=(hp + j > 0), sig=(hp + j == 2))
                            sp.append((ps, pr))
                        yield
                        for j in range(2):
                            ps, pr = aps[j]
                            for hp in range(2):
                                P.tt("dve", attm[:, hp * 2 + j, :], ps[:, hp * 128:(hp + 1) * 128], gmask, ALU.mult, reads=[pr], writes=[atr_], partial=(j + hp > 0))
                        hxb, hxbr_ = hxbr.next()
                        hyb, hybr_ = hybr.next()
                        hy, hyr_ = hyr.next()
                        if first:
                            P.cp("dve", hy, sp[X][0][:, 0:256].rearrange("p (a v) -> p a v", v=128), reads=[sp[X][1]], writes=[hyr_])
                        else:
                            P.act(hxb, ghs, AF.Copy, reads=[rgh], writes=[hxbr_])
                            for hp in range(2):
                                P.stt(hy[:, hp, :], ghs[:, hp, :], GELs[:, hp, chX:chX + 1], sp[X][0][:, hp * 128:(hp + 1) * 128], ALU.mult, ALU.add,
                                      reads=[rgh, rq, sp[X][1]], writes=[hyr_], partial=(hp > 0))
                        yield
                        P.act(hyb, hy, AF.Copy, reads=[hyr_], writes=[hybr_])
                        for hp in range(2):
                            P.stt(ghs[:, hp, :], hy[:, hp, :], GELs[:, hp, chY:chY + 1], sp[Y][0][:, hp * 128:(hp + 1) * 128], ALU.mult, ALU.add,
                                  reads=[hyr_, rq, sp[Y][1]], writes=[rgh], partial=(hp > 0))
                        yield
                        yps = []
                        for j in range(2):
                            js = slice(j * 64, (j + 1) * 64)
                            yp, ypr = nextps()
                            for hp in range(2):
                                hd = hp * 2 + j
                                ysl = yp[:, hp * 128:(hp + 1) * 128]
                                P.mm(ysl, lhsT=attm[:, hd, :], rhs=gv[:, hd * 128:(hd + 1) * 128], start=True, stop=False, reads=[atr_, gvr_], writes=[ypr],
                                     sig=False, partial=(hp > 0), skip_group_check=True)
                                if not first:
                                    P.mm(yp[X * 64:(X + 1) * 64, hp * 128:(hp + 1) * 128], lhsT=GQs[js, hp, c * 128 + X * 64:c * 128 + (X + 1) * 64], rhs=hxb[js, hp, :],
                                         start=False, stop=False, reads=[rq, hxbr_], writes=[ypr], sig=False, partial=True, skip_group_check=True)
                                P.mm(yp[Y * 64:(Y + 1) * 64, hp * 128:(hp + 1) * 128], lhsT=GQs[js, hp, c * 128 + Y * 64:c * 128 + (Y + 1) * 64], rhs=hyb[js, hp, :],
                                     start=False, stop=True, reads=[rq, hybr_], writes=[ypr], sig=(hp == 1), partial=True, skip_group_check=True)
                            yps.append((yp, ypr))
                        yield
                        if c not in gdone:
                            ysb, ysr = gyr.next()
                            ysv = ysb.rearrange("p (hp j v) -> p hp j v", j=2, v=128)
                            for j in range(2):
                                P.act(ysv[:, :, j, :], yps[j][0][:, 0:256].rearrange("p (hp v) -> p hp v", v=128), AF.Copy, reads=[yps[j][1]], writes=[ysr], partial=(j > 0))
                            yield
                            gdone[c] = Res()
                            P.dma(S["GYF"][tsl, :], ysb, reads=[ysr], writes=[gdone[c]])
                        elif c >= 2 or need_ctx:
                            yf, yfr_ = gyfr.next()
                            zc, zcr_ = zcr.next()
                            P.dma(yf, S["GYF"][tsl, :], reads=[gdone[c]], writes=[yfr_])
                            P.dma(zc, S["ZC"][tsl, :], writes=[zcr_])
                            y2, y2r_ = gy2r.next()
                            y2v = y2.rearrange("p (hp j v) -> p hp j v", j=2, v=128)
                            yfv = yf.rearrange("p (hp j v) -> p hp j v", j=2, v=128)
                            for j in range(2):
                                P.tt("dve", y2v[:, :, j, :], yps[j][0][:, 0:256].rearrange("p (hp v) -> p hp v", v=128), yfv[:, :, j, :], ALU.add,
                                     reads=[yps[j][1], yfr_], writes=[y2r_], partial=(j > 0))
                            ysb, ysr = gyr.next()
                            sm, smr2 = gsmr.next()
                            for hd in range(4):
                                P.op("dve", lambda e, hd=hd, ysb=ysb, y2=y2, sm=sm: e.scalar_tensor_tensor(out=ysb[:, hd * 128:(hd + 1) * 128], in0=y2[:, hd * 128:(hd + 1) * 128], scalar=1.0,
                                                                       in1=y2[:, hd * 128:(hd + 1) * 128], op0=ALU.mult, op1=ALU.mult, accum_out=sm[:, hd:hd + 1]),
                                     reads=[y2r_], writes=[ysr, smr2], partial=(hd > 0))
                            P.ts("dve", sm[:, 4:8], sm[:, 0:4], 1.0 / 128.0, EPS, ALU.mult, ALU.add, reads=[smr2], writes=[smr2])
                            yield
                            P.act(sm[:, 8:12], sm[:, 4:8], AF.Ln, reads=[smr2], writes=[smr2])
                            P.act(sm[:, 12:16], sm[:, 8:12], AF.Exp, reads=[smr2], writes=[smr2], scale=-0.5)
                            yield
                            for hd in range(4):
                                P.stt(y2[:, hd * 128:(hd + 1) * 128], y2[:, hd * 128:(hd + 1) * 128], sm[:, 12 + hd:13 + hd], gnw[:], ALU.mult, ALU.mult,
                                      reads=[y2r_, smr2, rg_par], writes=[y2r_], partial=(hd > 0))
                            go, gor_ = gor.next()
                            P.tt("dve", go, y2, zc, ALU.mult, reads=[y2r_, zcr_], writes=[gor_])
                            yield
                            P.dma(S["GO"][tsl, :], go, reads=[gor_])
                        yield

                gens = [gla_pass(0), gla_pass(1)]
                while gens:
                    for g_ in list(gens):
                        if next(g_, "done") == "done":
                            gens.remove(g_)
                P.barrier()
            if stop_after == "gla":
                break


            last = (layer == DEPTH - 1)
            with ExitStack() as ph:
                wob = [ph.enter_context(SBT("wob%d" % i, [128, 4, D], BF16)) for i in range(3)]
                wo = ph.enter_context(SBT("wo", [128, 8, D], BF16))
                fnw = ph.enter_context(SBT("fnw", [128, D], F32))
                oT = [ph.enter_context(SBT("oT%d" % i, [128, 4, 512], BF16)) for i in range(3)]
                uT = ph.enter_context(SBT("uT", [128, 8, 512], BF16))

                def mk(name, shape, dt, n):
                    ts_ = [ph.enter_context(SBT("%s%d" % (name, i), shape, dt)) for i in range(n)]
                    return Ring([t[:] for t in ts_])
                btr = mk("mbt", [128, 4, 512], BF16, 2)
                sgr = mk("msg", [128, 512], BF16, 12)
                mtr = mk("mt", [128, 512], F32, 6)
                mxr = mk("mx", [128, D], F32, 5)
                mnr = mk("mxn", [128, D], F32, 2)
                msm = mk("msm", [128, 8], F32, 2)
                mjk = ph.enter_context(SBT("mjk", [128, D], BF16))
                rw_ = Res()
                for b, nm in enumerate(("w_out_da", "w_out_ssm", "w_out_gla")):
                    P.dma(wob[b][:], W[nm][layer].rearrange("(k p) c -> p k c", p=128), writes=[rw_], q="pool", partial=True)
                for k0 in range(0, 8, 2):
                    P.dma(wo[:, k0:k0 + 2, :], W["w_o"][layer].rearrange("(k p) c -> p k c", p=128)[:, k0:k0 + 2, :], writes=[rw_], q="pool", partial=True)
                P.dma(fnw[:], W["final_norm_w"].partition_broadcast(128), writes=[rw_], partial=True)
                oTr = [Res() for _ in range(3)]
                uTr = Res()
                jkr = Res()
                for (t0, n, isc) in groups:
                    if isc and not need_ctx:
                        continue
                    who = 1 if isc else 0
                    nst = n // 128
                    for kc in range(4):
                        P.dma(oT[0][:, kc, 0:n], S["AOT"][kc * 128:(kc + 1) * 128, t0:t0 + n], writes=[oTr[0]], partial=(kc > 0))
                    for b, nm in ((1, "SO"), (2, "GO")):
                        bt, btr_ = btr.next()
                        P.dma(bt[:, 0:nst, :], S[nm][t0:t0 + n, :].rearrange("(j p) c -> p j c", p=128), writes=[btr_])
                        for kc in range(4):
                            ps, pr = nextps()
                            pb = ps.bitcast(BF16).rearrange("p (j t) -> p j t", t=128)
                            for j in range(nst):
                                P.tr(pb[:, j, :], bt[:, j, kc * 128:(kc + 1) * 128], identb, reads=[btr_], writes=[pr], sig=(j == nst - 1), partial=(j > 0))
                            if kc % 2 == 0:
                                P.act(oT[b][:, kc, 0:n], ps.bitcast(BF16)[:, 0:n], AF.Copy, reads=[pr], writes=[oTr[b]], partial=(kc > 0))
                            else:
                                P.cp("dve", oT[b][:, kc, 0:n], ps.bitcast(BF16)[:, 0:n], reads=[pr], writes=[oTr[b]], partial=True)
                    xld = []
                    for j in range(nst):
                        xt, xr = mxr.next()
                        P.dma(xt, tok_src(layer, t0 // 128 + j), writes=[xr])
                        xld.append((xt, xr))
                    sgl = {}

                    def issue_sg(oc):
                        for b in range(3):
                            sg, sgr_ = sgr.next()
                            P.dma(sg[:, 0:n], S["SG"][b][oc * 128:(oc + 1) * 128, t0:t0 + n], writes=[sgr_])
                            sgl[(oc, b)] = (sg, sgr_)
                    issue_sg(0)
                    issue_sg(1)
                    for oc in range(8):
                        if oc + 2 < 8:
                            issue_sg(oc + 2)
                        tms = []
                        for b in range(3):
                            sg, sgr_ = sgl.pop((oc, b))
                            ps, pr = nextps()
                            for kc in range(4):
                                P.mm(ps[:, 0:n], lhsT=wob[b][:, kc, oc * 128:(oc + 1) * 128], rhs=oT[b][:, kc, 0:n], start=(kc == 0), stop=(kc == 3),
                                     reads=[rw_, oTr[b]], writes=[pr])
                            tm, tmr = mtr.next()
                            P.tt("dve", tm[:, 0:n], ps[:, 0:n], sg[:, 0:n], ALU.mult, reads=[pr, sgr_], writes=[tmr])
                            tms.append((tm, tmr))
                        P.tt("pool", tms[0][0][:, 0:n], tms[0][0][:, 0:n], tms[1][0][:, 0:n], ALU.add, reads=[tms[0][1], tms[1][1]], writes=[tms[0][1]])
                        P.tt("dve", uT[:, oc, 0:n], tms[0][0][:, 0:n], tms[2][0][:, 0:n], ALU.add, reads=[tms[0][1], tms[2][1]], writes=[uTr], partial=(oc > 0))
                    for j in range(nst):
                        ti = t0 // 128 + j
                        xt, xr = xld[j]
                        xn, xnr = mnr.next()
                        for hf in range(2):
                            ps, pr = nextps()
                            for k in range(KC):
                                P.mm(ps[:], lhsT=uT[:, k, j * 128:(j + 1) * 128], rhs=wo[:, k, hf * 512:(hf + 1) * 512], start=(k == 0), stop=(k == KC - 1),
                                     reads=[rw_, uTr], writes=[pr])
                            P.tt("dve", xn[:, hf * 512:(hf + 1) * 512], ps[:], gate_bc[:, who, hf * 512:(hf + 1) * 512], ALU.mult, reads=[pr], writes=[xnr], partial=(hf > 0))
                        P.tt("pool", xn, xn, xt, ALU.add, reads=[xnr, xr], writes=[xnr])
                        if not last:
                            dst = S["CXR"][ti * 128:(ti + 1) * 128, :] if isc else S["XR"][(ti - 2) * 128:(ti - 1) * 128, :]
                            P.dma(dst, xn, reads=[xnr])
                        else:
                            sm, smr2 = msm.next()
                            P.act(mjk[:], xn, AF.Square, reads=[xnr], writes=[jkr, smr2], accum_out=sm[:, 0:1])
                            P.ts("dve", sm[:, 1:2], sm[:, 0:1], 1.0 / D, EPS, ALU.mult, ALU.add, reads=[smr2], writes=[smr2])
                            P.act(sm[:, 2:3], sm[:, 1:2], AF.Ln, reads=[smr2], writes=[smr2])
                            P.act(sm[:, 3:4], sm[:, 2:3], AF.Exp, reads=[smr2], writes=[smr2], scale=-0.5)
                            P.stt(xt, xn, sm[:, 3:4], fnw[:], ALU.mult, ALU.mult, reads=[xnr, smr2, rw_], writes=[xr])
                            P.dma(y_out[(ti - 2) * 128:(ti - 1) * 128, :], xt, reads=[xr])
                P.barrier()

        P.barrier()
        for e in ("sp", "pool", "act", "dve", "pe"):
            P.flush(e)
    return nc


_NC_CACHE = {}


def kernel(x, c, ctx, c_ctx, **weights):
    x = np.asarray(x, dtype=np.float32)
    B, TL, _ = x.shape
    if TL not in _NC_CACHE:
        _NC_CACHE[TL] = build(TL)
    nc = _NC_CACHE[TL]
    consts = host_consts(TL)
    shared = {"c_ctx": np.ascontiguousarray(np.asarray(c_ctx, dtype=np.float32))}
    for n, _s in WEIGHT_SPECS:
        shared[n] = np.ascontiguousarray(np.asarray(weights[n], dtype=np.float32))
    shared.update(consts)
    in_maps = []
    for b in range(B):
        m = dict(shared)
        m["x"] = np.ascontiguousarray(x[b])
        m["c"] = np.ascontiguousarray(np.asarray(c, dtype=np.float32)[b])
        m["ctx"] = np.ascontiguousarray(np.asarray(ctx, dtype=np.float32)[b])
        in_maps.append(m)
    res = run_bass_kernel_spmd(nc, in_maps, core_ids=list(range(B)))
    return np.stack([np.asarray(r["y"], dtype=np.float32) for r in res.results], axis=0)
```

```python
import math
from bisect import bisect_left
from contextlib import ExitStack

import numpy as np
import ml_dtypes
import concourse.bass as bass
import concourse.mybir as mybir
from concourse.bass_utils import run_bass_kernel_spmd

F32 = mybir.dt.float32
BF16 = mybir.dt.bfloat16
AF = mybir.ActivationFunctionType
ALU = mybir.AluOpType
AX = mybir.AxisListType

D = 1024
CTX = 256
KC = 8
EPS = 1e-6
IN_W = 7984
C_Q, C_K, C_V, C_ZA = 0, 512, 1024, 1536
C_BX, C_BZ, C_BB, C_BC, C_DT = 2048, 2560, 3072, 3200, 3328
C_GQ, C_GK, C_GV, C_GZ, C_LR = 3344, 3600, 3856, 4368, 4880
C_SG = 4912
DEPTH = 2


class Res:
    __slots__ = ("writers", "rd", "rdma", "excl")

    def __init__(self, excl=False):
        self.writers = []
        self.rd = {}
        self.rdma = []
        self.excl = excl


class Prog:
    def __init__(self, nc, es, n_dma=48):
        self.nc = nc
        self.E = {"pe": nc.tensor, "act": nc.scalar, "dve": nc.vector, "pool": nc.gpsimd, "sp": nc.sync}
        self.sem = {k: es.enter_context(nc.semaphore("s_" + k)) for k in ("pe", "act", "dve", "pool")}
        self.dsem = [es.enter_context(nc.semaphore("d%d" % i)) for i in range(n_dma)]
        self.dcnt = [0] * n_dma
        self.n_sw = 8
        self.drr = {"hw": 0, "sw": 0}
        self.cnt = {k: 0 for k in self.sem}
        self.nops = {k: 0 for k in self.sem}
        self.sigs = {k: ([], []) for k in self.sem}
        self.waited = {k: {} for k in self.E}
        self.pending = {k: [] for k in self.E}
        self.lastop = {k: None for k in self.sem}
        self.dma_open = []
        self.n_ins = 0

    def _resolve(self, tok):
        if tok[0] == "d":
            return ("d", tok[1]), tok[2]
        _, eng, idx = tok
        idxs, vals = self.sigs[eng]
        j = bisect_left(idxs, idx)
        assert j < len(idxs), "dependency on a non-signalling op with no later signal on " + eng
        return eng, vals[j]

    def op(self, eng, fn, reads=(), writes=(), sig=True, partial=False, dma=False):
        toks = []
        raw = []
        for r in reads:
            raw += r.writers
            if r.excl:
                toks += list(r.rd.values())
        toks += raw
        for w in writes:
            toks += w.writers
            toks += list(w.rd.values())
            toks += w.rdma
        toks += self.pending[eng]
        self.pending[eng] = []
        need = {}
        for t in toks:
            if t[0] == "c" and t[1] == eng and t not in raw:
                continue
            key, v = self._resolve(t)
            if need.get(key, 0) < v:
                need[key] = v
        k = None
        if dma:
            n_hw = len(self.dsem) - self.n_sw
            if eng == "pool":
                k = n_hw + self.drr["sw"]
                self.drr["sw"] = (self.drr["sw"] + 1) % self.n_sw
            else:
                k = self.drr["hw"]
                self.drr["hw"] = (self.drr["hw"] + 1) % n_hw
            if self.dcnt[k]:
                need[("d", k)] = max(need.get(("d", k), 0), 16 * self.dcnt[k])
        E = self.E[eng]
        wd = self.waited[eng]
        for key, v in need.items():
            if wd.get(key, 0) < v:
                E.wait_ge(self.sem[key] if isinstance(key, str) else self.dsem[key[1]], v)
                wd[key] = v
        ins = fn(E)
        self.n_ins += 1
        if dma:
            self.dcnt[k] += 1
            ins.then_inc(self.dsem[k], 16)
            tok = ("d", k, 16 * self.dcnt[k])
            self.dma_open.append(tok)
        else:
            idx = self.nops[eng]
            self.nops[eng] += 1
            tok = ("c", eng, idx)
            if sig:
                self.cnt[eng] += 1
                ins.then_inc(self.sem[eng], 1)
                self.sigs[eng][0].append(idx)
                self.sigs[eng][1].append(self.cnt[eng])
            self.lastop[eng] = tok
        for r in reads:
            if dma:
                r.rdma.append(tok)
            else:
                r.rd[eng] = tok
        for w in writes:
            if partial:
                w.writers.append(tok)
            else:
                w.writers = [tok]
                w.rd = {}
                w.rdma = []
        return tok

    def barrier(self):
        toks = [t for t in self.lastop.values() if t is not None] + self.dma_open
        for e in self.E:
            self.pending[e] = self.pending[e] + toks
        self.dma_open = []

    def flush(self, eng):
        need = {}
        for t in self.pending[eng]:
            key, v = self._resolve(t)
            if need.get(key, 0) < v:
                need[key] = v
        self.pending[eng] = []
        E = self.E[eng]
        wd = self.waited[eng]
        for key, v in need.items():
            if wd.get(key, 0) < v:
                E.wait_ge(self.sem[key] if isinstance(key, str) else self.dsem[key[1]], v)
                wd[key] = v

    def dma(self, out, in_, reads=(), writes=(), q="sp", partial=False):
        return self.op(q, lambda e: e.dma_start(out=out, in_=in_), reads, writes, dma=True, partial=partial)

    def mm(self, out, lhsT, rhs, start, stop, reads=(), writes=(), sig=None, partial=None, **kw):
        if sig is None:
            sig = stop
        if partial is None:
            partial = not start
        return self.op("pe", lambda e: e.matmul(out, lhsT=lhsT, rhs=rhs, start=start, stop=stop, **kw), reads, writes,
                       sig=sig, partial=partial)

    def tr(self, out, in_, ident, reads=(), writes=(), sig=True, partial=False):
        return self.op("pe", lambda e: e.transpose(out=out, in_=in_, identity=ident), reads, writes, sig=sig, partial=partial)

    def act(self, out, in_, func, reads=(), writes=(), partial=False, **kw):
        return self.op("act", lambda e: e.activation(out=out, in_=in_, func=func, **kw), reads, writes, partial=partial)

    def tt(self, eng, out, in0, in1, op, reads=(), writes=(), partial=False):
        return self.op(eng, lambda e: e.tensor_tensor(out=out, in0=in0, in1=in1, op=op), reads, writes, partial=partial)

    def ts(self, eng, out, in0, s1, s2, op0, op1=None, reads=(), writes=(), partial=False, **kw):
        if op1 is None:
            return self.op(eng, lambda e: e.tensor_scalar(out=out, in0=in0, scalar1=s1, scalar2=None, op0=op0, **kw), reads, writes, partial=partial)
        return self.op(eng, lambda e: e.tensor_scalar(out=out, in0=in0, scalar1=s1, scalar2=s2, op0=op0, op1=op1, **kw), reads, writes, partial=partial)

    def stt(self, out, in0, scalar, in1, op0, op1, reads=(), writes=(), partial=False):
        return self.op("dve", lambda e: e.scalar_tensor_tensor(out=out, in0=in0, scalar=scalar, in1=in1, op0=op0, op1=op1), reads, writes, partial=partial)

    def cp(self, eng, out, in_, reads=(), writes=(), partial=False):
        if eng == "act":
            return self.op("act", lambda e: e.copy(out=out, in_=in_), reads, writes, partial=partial)
        return self.op(eng, lambda e: e.tensor_copy(out=out, in_=in_), reads, writes, partial=partial)


class Ring:
    def __init__(self, aps):
        self.aps = aps
        self.res = [Res() for _ in aps]
        self.i = 0

    def next(self):
        j = self.i % len(self.aps)
        self.i += 1
        return self.aps[j], self.res[j]


def rope_tables(n):
    rows = n // 64
    row = np.repeat(np.arange(rows), 64)
    col = np.tile(np.arange(64), rows)
    pos = np.stack([row, col], axis=-1).astype(np.float32)
    nf = 16
    inv = (np.float32(10000.0) ** (-np.arange(nf, dtype=np.float32) / np.float32(nf))).astype(np.float32)
    ang = np.broadcast_to(pos[:, :, None, None] * inv, (n, 2, 2, nf)).reshape(n, 64).astype(np.float32)
    cos = np.cos(ang).astype(np.float32)
    sin = np.sin(ang).astype(np.float32)
    sgn = np.tile(np.concatenate([-np.ones(16), np.ones(16)]), 2).astype(np.float32)
    sin = sin * sgn[None, :]
    cosT = np.ascontiguousarray(np.concatenate([cos.T, cos.T], axis=0))
    sinT = np.ascontiguousarray(np.concatenate([sin.T, sin.T], axis=0))
    return cosT, sinT


def host_consts(TL):
    cosT, sinT = rope_tables(TL)
    k = np.arange(128)
    tri_f = (k[:, None] <= k[None, :]).astype(np.float32)
    tri_b = (k[:, None] >= k[None, :]).astype(np.float32)
    blk = (k[:, None] // 64) == (k[None, :] // 64)
    gm_f = (tri_f * blk).astype(np.float32)
    gm_b = (tri_b * blk).astype(np.float32)
    rst = np.ones((128, 512), np.float32)
    rst[:, ::64] = 0.0
    cmat = np.concatenate([np.eye(128, dtype=np.float32), tri_f, tri_b, gm_f, gm_b, np.ones((128, 128), np.float32)], axis=1)
    return {"cst_cos": cosT, "cst_sin": sinT, "cst_mat": np.ascontiguousarray(cmat), "cst_rst": rst}


WEIGHT_SPECS = [
    ("w_mod", [DEPTH, D, 3 * D]), ("b_mod", [DEPTH, 3 * D]), ("norm_w", [DEPTH, D]), ("w_in", [DEPTH, D, IN_W]),
    ("da_lambda", [DEPTH, 4, 64]), ("da_norm_w", [DEPTH, 128]), ("w_out_da", [DEPTH, 512, D]),
    ("ssm_conv_w", [DEPTH, 3, 768]), ("ssm_conv_b", [DEPTH, 768]), ("ssm_dt_bias", [DEPTH, 2, 8]),
    ("ssm_a_log", [DEPTH, 2, 8]), ("ssm_d", [DEPTH, 2, 8]), ("ssm_norm_w", [DEPTH, 512]), ("w_out_ssm", [DEPTH, 512, D]),
    ("gla_w_gate", [DEPTH, 2, 16, 256]), ("gla_b_gate", [DEPTH, 2, 256]), ("gla_norm_w", [DEPTH, 128]),
    ("w_out_gla", [DEPTH, 512, D]), ("w_o", [DEPTH, D, D]), ("final_norm_w", [D]),
]


def build(TL, n_layers=DEPTH, dbg=(), stop_after=None):
    TALL = CTX + TL
    NT = TALL // 128
    NCH = TALL // 64
    nc = bass.Bass("TRN2", target_bir_lowering=False)

    def din(name, shape):
        return nc.dram_tensor(name, list(shape), F32, kind="ExternalInput").ap()

    _cnt = [0]

    def SBT(name, shape, dt):
        _cnt[0] += 1
        return nc.sbuf_tensor("%s_%d" % (name, _cnt[0]), shape, dt)

    x_in = din("x", [TL, D])
    c_in = din("c", [D])
    ctx_in = din("ctx", [CTX, D])
    cctx_in = din("c_ctx", [D])
    W = {n: din(n, s) for n, s in WEIGHT_SPECS}
    cst_cos = din("cst_cos", [128, TL])
    cst_sin = din("cst_sin", [128, TL])
    cst_mat = din("cst_mat", [128, 6 * 128])
    cst_rst = din("cst_rst", [128, 512])
    y_out = nc.dram_tensor("y", [TL, D], F32, kind="ExternalOutput").ap()

    def scr(name, shape, dt):
        kind = "ExternalOutput" if name in dbg else "Internal"
        return nc.dram_tensor(name, list(shape), dt, kind=kind).ap()

    S = dict(
        QT=scr("QT", [4, 128, TALL], BF16), KT=scr("KT", [4, 128, TALL], BF16), V=scr("V", [TALL, 512], BF16),
        ZAT=scr("ZAT", [512, TALL], BF16), ZB=scr("ZB", [TALL, 512], BF16), ZC=scr("ZC", [TALL, 512], BF16),
        XBC=scr("XBC", [768, TALL], F32), DT=scr("DT", [TALL, 16], F32), GV=scr("GV", [TALL, 512], BF16),
        GQ=scr("GQ", [2, 256, TALL], BF16), GK=scr("GK", [2, 256, TALL], BF16), GKE=scr("GKE", [2, TALL, 256], BF16),
        GEL=scr("GEL", [2, 256, NCH], F32), SG=scr("SG", [3, D, TALL], BF16),
        AOT=scr("AOT", [512, TALL], BF16), SO=scr("SO", [TALL, 512], BF16), GO=scr("GO", [TALL, 512], BF16),
        XR=scr("XR", [TL, D], F32), CXR=scr("CXR", [CTX, D], F32),
        XS=scr("XS", [TALL, 512], BF16), BT=scr("BT", [128, TALL], BF16), CT=scr("CT", [128, TALL], BF16),
        BTM=scr("BTM", [TALL, 128], BF16), YF=scr("YF", [TALL, 512], F32), GYF=scr("GYF", [TALL, 512], F32),
        MODT=scr("MODT", [128, 64], F32),
    )

    es = ExitStack()
    with es:
        es.enter_context(nc.allow_non_contiguous_dma(reason="tiny transposed parameter loads"))
        P = Prog(nc, es)
        cmat32 = es.enter_context(SBT("cmat32", [128, 768], F32))
        cmatb = es.enter_context(SBT("cmatb", [128, 768], BF16))
        rst = es.enter_context(SBT("rst", [128, 512], F32))
        cs = es.enter_context(SBT("cs", [128, 8, 2], F32))
        csb = es.enter_context(SBT("csb", [128, 8, 2, 128], F32))
        modA = es.enter_context(SBT("modA", [128, 2, 8], F32))
        modB = es.enter_context(SBT("modB", [128, 2, 8], F32))
        gate_bc = es.enter_context(SBT("gate_bc", [128, 2, D], F32))
        PSALL = es.enter_context(nc.psum_tensor("psall", [128, 8, 512], F32))
        psb = [PSALL[:, i, :] for i in range(8)]
        psr = [Res(excl=True) for _ in range(8)]
        ident32, trif32, trib32 = cmat32[:, 0:128], cmat32[:, 128:256], cmat32[:, 256:384]
        ones32 = cmat32[:, 640:768]
        identb = cmatb[:, 0:128]
        gmfb, gmbb = cmatb[:, 384:512], cmatb[:, 512:640]

        r0 = Res()
        P.dma(cmat32[:], cst_mat[:, :], writes=[r0])
        P.cp("dve", cmatb[:], cmat32[:], reads=[r0], writes=[Res()])
        P.dma(rst[:], cst_rst[:, :], writes=[Res()])
        r1 = Res()
        P.dma(cs[:, :, 0], c_in.rearrange("(k p) -> p k", p=128), writes=[r1], partial=True)
        P.dma(cs[:, :, 1], cctx_in.rearrange("(k p) -> p k", p=128), writes=[r1], partial=True)
        P.act(cs[:], cs[:], AF.Silu, reads=[r1], writes=[r1])
        for who in range(2):
            P.cp("dve", csb[:, :, who, :], cs[:, :, who:who + 1].to_broadcast([128, 8, 128]), reads=[r1], writes=[Res()])
        P.barrier()

        psi = [0]

        def nextps():
            j = psi[0] % 8
            psi[0] += 1
            return psb[j], psr[j]

        def tok_src(layer, i):
            if layer == 0:
                return ctx_in[i * 128:(i + 1) * 128, :] if i < 2 else x_in[(i - 2) * 128:(i - 1) * 128, :]
            return S["CXR"][i * 128:(i + 1) * 128, :] if i < 2 else S["XR"][(i - 2) * 128:(i - 1) * 128, :]

        groups = [(0, CTX, True)] + [(CTX + 512 * g, 512, False) for g in range(TL // 512)]

        for layer in range(n_layers):
            need_ctx = layer < DEPTH - 1
            w_in = W["w_in"][layer].rearrange("(k p) c -> p k c", p=128)
            with ExitStack() as ph:
                wm = [ph.enter_context(SBT("wm%d" % i, [128, 8, 512], F32)) for i in range(2)]
                wmr = Ring([t[:] for t in wm])
                bmT = ph.enter_context(SBT("bmT", [128, 24], F32))
                nwT = ph.enter_context(SBT("nwT", [128, 8], F32))
                bmg = ph.enter_context(SBT("bmg", [128, D], F32))
                modT = ph.enter_context(SBT("modT", [128, 24, 2], F32))
                rb, rn, rg, rm = Res(), Res(), Res(), Res()
                P.dma(bmT[:], W["b_mod"][layer].rearrange("(j p) -> p j", p=128), writes=[rb])
                P.dma(nwT[:], W["norm_w"][layer].rearrange("(j p) -> p j", p=128), writes=[rn])
                P.dma(bmg[:], W["b_mod"][layer][2 * D:3 * D].partition_broadcast(128), writes=[rg])
                w_mod = W["w_mod"][layer].rearrange("(k p) c -> p k c", p=128)
                for t in range(6):
                    wt, wr = wmr.next()
                    P.dma(wt, w_mod[:, :, t * 512:(t + 1) * 512], writes=[wr])
                    for jj in range(4):
                        j = t * 4 + jj
                        ps, pr = nextps()
                        for k in range(KC):
                            P.mm(ps[:, 0:2], lhsT=wt[:, k, jj * 128:(jj + 1) * 128], rhs=cs[:, k, :], start=(k == 0), stop=(k == KC - 1),
                                 reads=[wr], writes=[pr])
                        P.cp("dve", modT[:, j, :], ps[:, 0:2], reads=[pr], writes=[rm], partial=True)
                    if t >= 4:
                        for who in range(2 if need_ctx else 1):
                            ps, pr = nextps()
                            for k in range(KC):
                                P.mm(ps[:], lhsT=csb[:, k, who, :], rhs=wt[:, k, :], start=(k == 0), stop=(k == KC - 1),
                                     reads=[wr], writes=[pr])
                            P.tt("dve", gate_bc[:, who, (t - 4) * 512:(t - 3) * 512], ps[:], bmg[:, (t - 4) * 512:(t - 3) * 512], ALU.add,
                                 reads=[pr, rg], writes=[Res()])
                ra = Res()
                for who in range(2):
                    P.tt("dve", modB[:, who, :], modT[:, 0:8, who], bmT[:, 0:8], ALU.add, reads=[rm, rb], writes=[ra], partial=True)
                    P.tt("dve", modA[:, who, :], modT[:, 8:16, who], bmT[:, 8:16], ALU.add, reads=[rm, rb], writes=[ra], partial=True)
                    P.stt(modA[:, who, :], modA[:, who, :], 1.0, nwT[:], ALU.add, ALU.mult, reads=[ra, rn], writes=[ra])
                if "MODT" in dbg:
                    P.dma(S["MODT"][:, 0:16], modA[:].rearrange("p a b -> p (a b)"), reads=[ra])
                    P.dma(S["MODT"][:, 16:32], modB[:].rearrange("p a b -> p (a b)"), reads=[ra])
                P.barrier()
            if stop_after == "mod":
                break

            with ExitStack() as ph:
                hT = ph.enter_context(SBT("hT", [128, KC, TALL], BF16))
                with ExitStack() as ph2:
                    xts = [ph2.enter_context(SBT("xt%d" % i, [128, D], F32)) for i in range(2)]
                    xns = [ph2.enter_context(SBT("xn%d" % i, [128, D], BF16)) for i in range(2)]
                    junk = ph2.enter_context(SBT("junk", [128, D], BF16))
                    sst = ph2.enter_context(SBT("sst", [128, 2, 4], F32))
                    xr_ = Ring([t[:] for t in xts])
                    xn_ = Ring([t[:] for t in xns])
                    ss_ = Ring([sst[:, i, :] for i in range(2)])
                    jr = Res()
                    for i in range(NT):
                        xt, xr = xr_.next()
                        xn, xnr = xn_.next()
                        st, sr = ss_.next()
                        import os
                        stage = int(os.environ.get("DBG_1A", "9"))
                        P.dma(xt, tok_src(layer, i), writes=[xr])
                        if stage < 1: continue
                        P.act(junk[:], xt, AF.Square, reads=[xr], writes=[jr, sr], accum_out=st[:, 0:1])
                        if stage < 2: continue
                        P.act(st[:, 1:2], st[:, 0:1], AF.Sqrt, reads=[sr], writes=[sr], scale=1.0 / D, bias=EPS)
                        if stage < 3: continue
                        P.op("dve", lambda e, st=st: e.reciprocal(out=st[:, 2:3], in_=st[:, 1:2]), reads=[sr], writes=[sr])
                        P.ts("dve", xn, xt, st[:, 2:3], None, ALU.mult, reads=[xr, sr], writes=[xnr])
                        if stage < 4: continue
                        ps, pr = nextps()
                        pb = ps.bitcast(BF16).rearrange("p (j t) -> p j t", t=128)
                        for j in range(KC):
                            P.tr(pb[:, j, :], xn[:, j * 128:(j + 1) * 128], identb, reads=[xnr], writes=[pr], sig=(j == KC - 1), partial=(j > 0))
                        if stage < 5: continue
                        who = 1 if i < 2 else 0
                        for j in range(KC):
                            dst = hT[:, j, i * 128:(i + 1) * 128]
                            if i % 2 == 0:
                                P.act(dst, pb[:, j, :], AF.Identity, reads=[pr], scale=modA[:, who, j:j + 1], bias=modB[:, who, j:j + 1])
                            else:
                                P.ts("dve", dst, pb[:, j, :], modA[:, who, j:j + 1], modB[:, who, j:j + 1], ALU.mult, ALU.add, reads=[pr])
                    P.barrier()
                if stop_after == "norm":
                    break

                with ExitStack() as ph2:
                    wts = [ph2.enter_context(SBT("wt%d" % i, [128, KC, 512], BF16)) for i in range(2)]
                    wring = Ring([t[:] for t in wts])
                    stg = [ph2.enter_context(SBT("stg%d" % i, [128, 512], F32)) for i in range(4)]
                    sring = Ring([t[:] for t in stg])
                    sgb = [ph2.enter_context(SBT("sgb%d" % i, [128, 512], BF16)) for i in range(4)]
                    bring = Ring([t[:] for t in sgb])

                    def load_w(c0, n):
                        wt, wr = wring.next()
                        P.dma(wt[:, :, 0:n], w_in[:, :, c0:c0 + n], writes=[wr], q="pool")
                        return wt, wr

                    def mm_fm(wt, wr, cc, ncol, t0, n):
                        ps, pr = nextps()
                        for k in range(KC):
                            P.mm(ps[0:ncol, 0:n], lhsT=wt[:, k, cc:cc + ncol], rhs=hT[:, k, t0:t0 + n], start=(k == 0), stop=(k == KC - 1),
                                 reads=[wr], writes=[pr])
                        return ps, pr

                    def mm_tm(wt, wr, ncol, i):
                        ps, pr = nextps()
                        for k in range(KC):
                            P.mm(ps[:, 0:ncol], lhsT=hT[:, k, i * 128:(i + 1) * 128], rhs=wt[:, k, 0:ncol], start=(k == 0), stop=(k == KC - 1),
                                 reads=[wr], writes=[pr])
                        return ps, pr

                    for (c0, name, func) in ((C_V, "V", AF.Copy), (C_BZ, "ZB", AF.Silu), (C_GZ, "ZC", AF.Silu), (C_GV, "GV", AF.Copy)):
                        wt, wr = load_w(c0, 512)
                        for i in range(NT):
                            ps, pr = mm_tm(wt, wr, 512, i)
                            sb, sr = bring.next()
                            if func == AF.Copy and i % 2 == 1:
                                P.cp("dve", sb, ps[:], reads=[pr], writes=[sr])
                            else:
                                P.act(sb, ps[:], func, reads=[pr], writes=[sr])
                            P.dma(S[name][i * 128:(i + 1) * 128, :], sb, reads=[sr])
                    with ExitStack() as ph3:
                        dtb = ph3.enter_context(SBT("dtb", [128, 16], F32))
                        dtt = ph3.enter_context(SBT("dtt", [128, 4, 16], F32))
                        dring = Ring([dtt[:, i, :] for i in range(4)])
                        rdb = Res()
                        P.dma(dtb[:], W["ssm_dt_bias"][layer].rearrange("a b -> (a b)").partition_broadcast(128), writes=[rdb])
                        wt, wr = load_w(C_DT, 16)
                        for i in range(NT):
                            ps, pr = mm_tm(wt, wr, 16, i)
                            d_, dr = dring.next()
                            P.tt("dve", d_, ps[:, 0:16], dtb[:], ALU.add, reads=[pr, rdb], writes=[dr])
                            P.act(d_, d_, AF.Exp, reads=[dr], writes=[dr])
                            P.act(d_, d_, AF.Ln, reads=[dr], writes=[dr], bias=1.0)
                            P.dma(S["DT"][i * 128:(i + 1) * 128, :], d_, reads=[dr])
                    for (c0, n, row0) in ((C_BX, 512, 0), (C_BB, 256, 512)):
                        wt, wr = load_w(c0, n)
                        for (t0, nt, isc) in groups:
                            for cc in range(n // 128):
                                ps, pr = mm_fm(wt, wr, cc * 128, 128, t0, nt)
                                sb, sr = sring.next()
                                P.act(sb[:, 0:nt], ps[:, 0:nt], AF.Copy, reads=[pr], writes=[sr])
                                P.dma(S["XBC"][row0 + cc * 128:row0 + (cc + 1) * 128, t0:t0 + nt], sb[:, 0:nt], reads=[sr])
                    wt, wr = load_w(C_ZA, 512)
                    for (t0, nt, isc) in groups:
                        if isc and not need_ctx:
                            continue
                        for cc in range(4):
                            ps, pr = mm_fm(wt, wr, cc * 128, 128, t0, nt)
                            sb, sr = bring.next()
                            P.act(sb[:, 0:nt], ps[:, 0:nt], AF.Silu, reads=[pr], writes=[sr])
                            P.dma(S["ZAT"][cc * 128:(cc + 1) * 128, t0:t0 + nt], sb[:, 0:nt], reads=[sr])
                    for t in range(6):
                        wt, wr = load_w(C_SG + t * 512, 512)
                        for (t0, nt, isc) in groups:
                            if isc and not need_ctx:
                                continue
                            for cc in range(4):
                                ps, pr = mm_fm(wt, wr, cc * 128, 128, t0, nt)
                                sb, sr = bring.next()
                                P.act(sb[:, 0:nt], ps[:, 0:nt], AF.Sigmoid, reads=[pr], writes=[sr])
                                r_ = t * 512 + cc * 128
                                P.dma(S["SG"][r_ // D][r_ % D:r_ % D + 128, t0:t0 + nt], sb[:, 0:nt], reads=[sr])

                    with ExitStack() as ph3:
                        wg32 = ph3.enter_context(SBT("wg32", [16, 2, 256], F32))
                        wgb = ph3.enter_context(SBT("wgb", [16, 2, 256], BF16))
                        nbg = ph3.enter_context(SBT("nbg", [128, 2, 2], F32))
                        lrb = [ph3.enter_context(SBT("lrb%d" % i, [16, 512], BF16)) for i in range(2)]
                        qks = [ph3.enter_context(SBT("qks%d" % i, [128, 512], F32)) for i in range(4)]
                        tmpf = [ph3.enter_context(SBT("gtmp%d" % i, [128, 512], F32)) for i in range(5)]
                        tring = Ring([t[:] for t in tmpf])
                        egs = ph3.enter_context(SBT("egs", [128, 4, 8], F32))
                        ering = Ring([egs[:, i, :] for i in range(4)])
                        rw, rnb = Res(), Res()
                        P.dma(wg32[:], W["gla_w_gate"][layer].rearrange("d r c -> r d c"), writes=[rw])
                        P.cp("dve", wgb[:], wg32[:], reads=[rw], writes=[rw])
                        P.dma(nbg[:], W["gla_b_gate"][layer].rearrange("d (h p) -> p d h", p=128), writes=[rnb])
                        P.ts("dve", nbg[:], nbg[:], -1.0, None, ALU.mult, reads=[rnb], writes=[rnb])
                        wgq, wgqr = load_w(C_GQ, 256)
                        wgk, wgkr = load_w(C_GK, 256)
                        wlr_t = ph3.enter_context(SBT("wlr", [128, KC, 32], BF16))
                        wlr, wlrr = wlr_t[:], Res()
                        P.dma(wlr, w_in[:, :, C_LR:C_LR + 32], writes=[wlrr], q="pool")
                        lrr = [Res(), Res()]
                        qkr = [Res() for _ in range(4)]
                        for (t0, n, isc) in groups:
                            ncks = n // 64
                            for d in range(2):
                                ps, pr = mm_fm(wlr, wlrr, d * 16, 16, t0, n)
                                P.cp("dve", lrb[d][:, 0:n], ps[0:16, 0:n], reads=[pr], writes=[lrr[d]])
                            for hp in range(2):
                                ps, pr = mm_fm(wgq, wgqr, hp * 128, 128, t0, n)
                                P.act(qks[hp][:, 0:n], ps[:, 0:n], AF.Copy, reads=[pr], writes=[qkr[hp]])
                                ps, pr = mm_fm(wgk, wgkr, hp * 128, 128, t0, n)
                                P.act(qks[2 + hp][:, 0:n], ps[:, 0:n], AF.Copy, reads=[pr], writes=[qkr[2 + hp]])
                            for d in range(2):
                                for hp in range(2):
                                    ps, pr = nextps()
                                    P.mm(ps[:, 0:n], lhsT=wgb[:, d, hp * 128:(hp + 1) * 128], rhs=lrb[d][:, 0:n], start=True, stop=True,
                                         reads=[rw, lrr[d]], writes=[pr])
                                    A_, ar = tring.next()
                                    B_, br = tring.next()
                                    P.act(A_[:, 0:n], ps[:, 0:n], AF.Exp, reads=[pr, rnb], writes=[ar], scale=-1.0, bias=nbg[:, d, hp:hp + 1])
                                    P.act(A_[:, 0:n], A_[:, 0:n], AF.Ln, reads=[ar], writes=[ar], bias=1.0)
                                    if d == 0:
                                        so_, si_ = B_[:, 0:n], A_[:, 0:n]
                                    else:
                                        so_, si_ = B_[:, 0:n][:, ::-1], A_[:, 0:n][:, ::-1]
                                    P.op("dve", lambda e, so_=so_, si_=si_, n=n: e.tensor_tensor_scan(out=so_, data0=rst[:, 0:n], data1=si_, initial=0.0,
                                                                                         op0=ALU.mult, op1=ALU.add), reads=[ar], writes=[br])
                                    Bv = B_[:, 0:n].rearrange("p (c t) -> p c t", t=64)
                                    gl = Bv[:, :, 63:64] if d == 0 else Bv[:, :, 0:1]
                                    C_, cr_ = tring.next()
                                    P.act(C_[:, 0:n], B_[:, 0:n], AF.Exp, reads=[br], writes=[cr_], scale=-1.0 / 16.0)
                                    ob, obr = bring.next()
                                    P.stt(ob[:, 0:n], qks[hp][:, 0:n], 0.125, C_[:, 0:n], ALU.mult, ALU.mult, reads=[qkr[hp], cr_], writes=[obr])
                                    P.dma(S["GQ"][d][hp * 128:(hp + 1) * 128, t0:t0 + n], ob[:, 0:n], reads=[obr])
                                    C2, cr2 = tring.next()
                                    P.act(C2[:, 0:n], B_[:, 0:n], AF.Exp, reads=[br], writes=[cr2], scale=1.0 / 16.0)
                                    ob, obr = bring.next()
                                    P.tt("dve", ob[:, 0:n], qks[2 + hp][:, 0:n], C2[:, 0:n], ALU.mult, reads=[qkr[2 + hp], cr2], writes=[obr])
                                    P.dma(S["GK"][d][hp * 128:(hp + 1) * 128, t0:t0 + n], ob[:, 0:n], reads=[obr])
                                    D_, dr_ = tring.next()
                                    P.tt("dve", D_[:, 0:n].rearrange("p (c t) -> p c t", t=64), gl.to_broadcast([128, ncks, 64]), Bv, ALU.subtract,
                                         reads=[br], writes=[dr_])
                                    P.act(D_[:, 0:n], D_[:, 0:n], AF.Exp, reads=[dr_], writes=[dr_], scale=-1.0 / 16.0)
                                    ke, ker = bring.next()
                                    P.tt("dve", ke[:, 0:n], qks[2 + hp][:, 0:n], D_[:, 0:n], ALU.mult, reads=[qkr[2 + hp], dr_], writes=[ker])
                                    ps, pr = nextps()
                                    pb = ps.bitcast(BF16).rearrange("p (j t) -> p j t", t=128)
                                    nst = n // 128
                                    for j in range(nst):
                                        P.tr(pb[:, j, :], ke[:, j * 128:(j + 1) * 128], identb, reads=[ker], writes=[pr], sig=(j == nst - 1), partial=(j > 0))
                                    kt_, ktr = bring.next()
                                    P.cp("dve", kt_[:, 0:n], ps.bitcast(BF16)[:, 0:n], reads=[pr], writes=[ktr])
                                    P.dma(S["GKE"][d][t0:t0 + n, hp * 128:(hp + 1) * 128].rearrange("(j p) c -> p j c", p=128),
                                          kt_[:, 0:n].rearrange("p (j c) -> p j c", c=128), reads=[ktr])
                                    eg, egr = ering.next()
                                    P.act(eg[:, 0:ncks], gl.rearrange("p c o -> p (c o)"), AF.Exp, reads=[br], writes=[egr], scale=-1.0 / 16.0)
                                    P.dma(S["GEL"][d][hp * 128:(hp + 1) * 128, t0 // 64:t0 // 64 + ncks], eg[:, 0:ncks], reads=[egr])
                    with ExitStack() as ph3:
                        cst = [ph3.enter_context(SBT("cst%d" % i, [128, 2, 512], F32)) for i in range(2)]
                        cring = Ring([t[:] for t in cst])
                        for (c0, name) in ((C_Q, "QT"), (C_K, "KT")):
                            wt, wr = load_w(c0, 512)
                            wp, wpr = wring.next()
                            wv = wt.rearrange("p k (a h f) -> p k a h f", h=2, f=16)
                            wpv = wp.rearrange("p k (a h f) -> p k a h f", h=2, f=16)
                            for k in range(KC):
                                for h in range(2):
                                    P.cp("dve" if (k + h) % 2 else "pool", wpv[:, k, :, h, :], wv[:, k, :, 1 - h, :], reads=[wr], writes=[wpr], partial=(k + h > 0))
                            for (t0, nt, isc) in groups:
                                if not isc:
                                    ct, cr = cring.next()
                                    P.dma(ct[:, 0, :], cst_cos[:, t0 - CTX:t0 - CTX + 512], writes=[cr], partial=True)
                                    P.dma(ct[:, 1, :], cst_sin[:, t0 - CTX:t0 - CTX + 512], writes=[cr], partial=True)
                                for cp_ in range(4):
                                    ps, pr = mm_fm(wt, wr, cp_ * 128, 128, t0, nt)
                                    ob, obr = bring.next()
                                    if isc:
                                        P.act(ob[:, 0:nt], ps[:, 0:nt], AF.Copy, reads=[pr], writes=[obr])
                                    else:
                                        ps2, pr2 = mm_fm(wp, wpr, cp_ * 128, 128, t0, nt)
                                        s1, s1r = sring.next()
                                        s2, s2r = sring.next()
                                        P.tt("dve", s1, ps[:], ct[:, 0, :], ALU.mult, reads=[pr, cr], writes=[s1r])
                                        P.tt("dve", s2, ps2[:], ct[:, 1, :], ALU.mult, reads=[pr2, cr], writes=[s2r])
                                        P.tt("pool", ob, s1, s2, ALU.add, reads=[s1r, s2r], writes=[obr])
                                    P.dma(S[name][cp_][:, t0:t0 + nt], ob[:, 0:nt], reads=[obr])
                    P.barrier()
            if stop_after == "proj":
                break


            lam_init = 0.8 - 0.6 * math.exp(-0.3 * layer)
            with ExitStack() as ph:
                KTs = ph.enter_context(SBT("KTs", [128, 4, TALL], BF16))
                Vs = ph.enter_context(SBT("Vs", [128, NT, 4, 130], BF16))
                lamt = ph.enter_context(SBT("lamt", [128, 4, 64], F32))
                lsc = ph.enter_context(SBT("lsc", [128, 8], F32))
                nwc = ph.enter_context(SBT("nwc", [128, 1], F32))

                def mk(name, shape, dt, n):
                    ts_ = [ph.enter_context(SBT("%s%d" % (name, i), shape, dt)) for i in range(n)]
                    return Ring([t[:] for t in ts_])
                qring = mk("qt", [128, 2, 256], BF16, 3)
                for qa, qres in zip(qring.aps, qring.res):
                    P.op("pool", lambda e, qa=qa: e.memset(qa, 0.0), writes=[qres])
                pring = mk("pt", [128, 2, 256], BF16, 3)
                zring = mk("za", [128, 256], BF16, 3)
                rsring = mk("ars", [128, 512], F32, 2)
                t0ring = mk("at0", [128, 512], F32, 2)
                oring = mk("ao_", [128, 256], F32, 2)
                sqring = mk("asq", [128, 256], F32, 2)
                msring = mk("ams", [128, 256], F32, 2)
                aoring = mk("aob", [128, 256], BF16, 2)
                rk, rv, rl, rnw = Res(), Res(), Res(), Res()
                rkh = [Res() for _ in range(4)]
                rvt = [Res() for _ in range(NT)]
                P.dma(KTs[:, 0, :], S["KT"][0], writes=[rkh[0]])
                for i in range(NT):
                    P.dma(Vs[:, i, :, 0:128], S["V"][i * 128:(i + 1) * 128, :].rearrange("p (h v) -> p h v", v=128), writes=[rvt[i]])
                for h in range(1, 4):
                    P.dma(KTs[:, h, :], S["KT"][h], writes=[rkh[h]])
                P.dma(lamt[:], W["da_lambda"][layer].rearrange("a b -> (a b)").partition_broadcast(128), writes=[rl])
                P.dma(nwc[:], W["da_norm_w"][layer].rearrange("(p o) -> p o", o=1), writes=[rnw])
                P.ts("dve", nwc[:], nwc[:], 1.0 - lam_init, None, ALU.mult, reads=[rnw], writes=[rnw])
                P.stt(lamt[:, 0, :], lamt[:, 0, :], 1.0, lamt[:, 1, :], ALU.mult, ALU.mult, reads=[rl], writes=[rl])
                P.stt(lamt[:, 2, :], lamt[:, 2, :], 1.0, lamt[:, 3, :], ALU.mult, ALU.mult, reads=[rl], writes=[rl])
                P.op("dve", lambda e: e.reduce_sum(out=lsc[:, 0:1], in_=lamt[:, 0, :], axis=AX.X), reads=[rl], writes=[rl])
                P.op("dve", lambda e: e.reduce_sum(out=lsc[:, 1:2], in_=lamt[:, 2, :], axis=AX.X), reads=[rl], writes=[rl])
                P.act(lsc[:, 2:4], lsc[:, 0:2], AF.Exp, reads=[rl], writes=[rl])
                P.tt("dve", lsc[:, 4:5], lsc[:, 3:4], lsc[:, 2:3], ALU.subtract, reads=[rl], writes=[rl])
                P.ts("dve", lsc[:, 4:5], lsc[:, 4:5], -lam_init, None, ALU.add, reads=[rl], writes=[rl])
                neglam = lsc[:, 4:5]
                onesb = cmatb[:, 640:768]
                acc_set = [0]
                pend_epi = [None]

                def flush_epi():
                    if pend_epi[0] is not None:
                        for _ in pend_epi[0]:
                            pass
                        pend_epi[0] = None

                def epilogue(h, q0, OUT, outr, SUM, sumr, za, zr):
                    rs, rsr = rsring.next()
                    P.op("dve", lambda e: e.reciprocal(out=rs, in_=SUM), reads=[sumr], writes=[rsr])
                    t0_, t0r = t0ring.next()
                    P.tt("dve", t0_, OUT, rs, ALU.mult, reads=[outr, rsr], writes=[t0r])
                    o_, o_r = oring.next()
                    P.stt(o_, t0_[:, 256:512], neglam, t0_[:, 0:256], ALU.mult, ALU.add, reads=[t0r, rl], writes=[o_r])
                    sq, sqr = sqring.next()
                    P.tt("pool", sq, o_, o_, ALU.mult, reads=[o_r], writes=[sqr])
                    yield 1
                    P.mm(SUM[:, 0:256], lhsT=ones32, rhs=sq, start=True, stop=True, reads=[sqr], writes=[sumr])
                    ms, msr = msring.next()
                    P.ts("dve", ms, SUM[:, 0:256], 1.0 / 128.0, EPS, ALU.mult, ALU.add, reads=[sumr], writes=[msr])
                    yield 2
                    P.act(ms, ms, AF.Ln, reads=[msr], writes=[msr])
                    P.act(ms, ms, AF.Exp, reads=[msr], writes=[msr], scale=-0.5)
                    yield 3
                    P.stt(o_, o_, nwc[:, 0:1], ms, ALU.mult, ALU.mult, reads=[o_r, msr, rnw], writes=[o_r])
                    ao, aor = aoring.next()
                    P.tt("dve", ao, o_, za, ALU.mult, reads=[o_r, zr], writes=[aor])
                    P.dma(S["AOT"][h * 128:(h + 1) * 128, q0:q0 + 256], ao, reads=[aor])

                tiles = []
                for h in range(4):
                    if need_ctx:
                        tiles.append((h, 0, [0, 1]))
                    for qi in range(TL // 256):
                        tiles.append((h, CTX + qi * 256, list(range(NT))))
                flat = [(ti, ii) for ti, (h_, q0_, kbs_) in enumerate(tiles) for ii in range(len(kbs_))]
                tstate = {}

                def tile_setup(ti):
                    h, q0, kbs = tiles[ti]
                    si = ti % 2
                    qt, qr = qring.next()
                    P.dma(qt[0:64, 0, :], S["QT"][h][0:64, q0:q0 + 256], writes=[qr], partial=True)
                    P.dma(qt[64:128, 1, :], S["QT"][h][64:128, q0:q0 + 256], writes=[qr], partial=True)
                    za, zr = zring.next()
                    P.dma(za, S["ZAT"][h * 128:(h + 1) * 128, q0:q0 + 256], writes=[zr])
                    tstate[ti] = (psb[4 + 2 * si], psr[4 + 2 * si], psb[5 + 2 * si], psr[5 + 2 * si], qt.rearrange("p c q -> p (c q)"), qr, za, zr)

                def emit_qk(f):
                    ti, ii = flat[f]
                    h, q0, kbs = tiles[ti]
                    kb = kbs[ii]
                    b_ = f % 3
                    P.mm(psb[b_], lhsT=KTs[:, h, kb * 128:(kb + 1) * 128], rhs=tstate[ti][4], start=True, stop=True, reads=[rkh[h], tstate[ti][5]], writes=[psr[b_]])

                tile_setup(0)
                emit_qk(0)
                if len(flat) > 1:
                    if flat[1][0] != 0:
                        tile_setup(flat[1][0])
                    emit_qk(1)
                for f, (ti, ii) in enumerate(flat):
                    h, q0, kbs = tiles[ti]
                    nk = len(kbs)
                    kb = kbs[ii]
                    OUT, outr, SUM, sumr, qtf, qr, za, zr = tstate[ti]
                    if ii == 0 and ti + 1 < len(tiles) and (ti + 1) not in tstate:
                        tile_setup(ti + 1)
                    if f + 2 < len(flat):
                        if flat[f + 2][0] not in tstate:
                            tile_setup(flat[f + 2][0])
                        emit_qk(f + 2)
                    b_ = f % 3
                    pt, ptr = pring.next()
                    ptf = pt.rearrange("p c q -> p (c q)")
                    P.act(ptf, psb[b_], AF.Exp, reads=[psr[b_]], writes=[ptr], scale=0.125)
                    P.mm(OUT, lhsT=Vs[:, kb, h, 0:128], rhs=ptf, start=(ii == 0), stop=(ii == nk - 1), reads=[ptr, rvt[kb]], writes=[outr])
                    P.mm(SUM, lhsT=onesb, rhs=ptf, start=(ii == 0), stop=(ii == nk - 1), reads=[ptr], writes=[sumr])
                    if ii in (8, 28, 40, 50) and pend_epi[0] is not None:
                        if next(pend_epi[0], "done") == "done":
                            pend_epi[0] = None
                    if ii == nk - 1:
                        flush_epi()
                        pend_epi[0] = epilogue(h, q0, OUT, outr, SUM, sumr, za, zr)
                        del tstate[ti]
                flush_epi()
                P.barrier()
            if stop_after == "attn":
                break


            with ExitStack() as ph:
                cw = ph.enter_context(SBT("cw", [128, 6, 3], F32))
                cb = ph.enter_context(SBT("cb", [128, 6], F32))
                xins = [ph.enter_context(SBT("xin%d" % i, [128, 516], F32)) for i in range(5)]
                xiring = Ring([t[:] for t in xins])
                cacc = [ph.enter_context(SBT("cacc%d" % i, [128, 512], F32)) for i in range(3)]
                caring = Ring([t[:] for t in cacc])
                cyb = [ph.enter_context(SBT("cyb%d" % i, [128, 512], BF16)) for i in range(3)]
                cyring = Ring([t[:] for t in cyb])
                ctb = [ph.enter_context(SBT("ctb%d" % i, [128, 512], BF16)) for i in range(3)]
                ctring = Ring([t[:] for t in ctb])
                rcw = Res()
                for j in range(3):
                    P.dma(cw[:, :, j], W["ssm_conv_w"][layer][j].rearrange("(f p) -> p f", p=128), writes=[rcw], partial=True)
                P.dma(cb[:], W["ssm_conv_b"][layer].rearrange("(f p) -> p f", p=128), writes=[rcw], partial=True)
                items = [(t0, n, isc, fc) for (t0, n, isc) in groups for fc in range(6)]
                loaded = {}

                def issue_xin(i):
                    t0, n, isc, fc = items[i]
                    seg_lo, seg_hi = (0, CTX) if isc else (CTX, TALL)
                    lo = max(t0 - 1, seg_lo)
                    hi = min(t0 + n + 1, seg_hi)
                    xin, xir = xiring.next()
                    if lo > t0 - 1:
                        P.op("pool", lambda e, xin=xin: e.memset(xin[:, 0:1], 0.0), writes=[xir])
                    if hi < t0 + n + 1:
                        P.op("pool", lambda e, xin=xin, n=n: e.memset(xin[:, n + 1:n + 2], 0.0), writes=[xir], partial=True)
                    P.dma(xin[:, lo - (t0 - 1):hi - (t0 - 1)], S["XBC"][fc * 128:(fc + 1) * 128, lo:hi], writes=[xir], partial=True)
                    loaded[i] = (xin, xir)
                for i in range(min(3, len(items))):
                    issue_xin(i)
                for i, (t0, n, isc, fc) in enumerate(items):
                    if True:
                        if i + 3 < len(items):
                            issue_xin(i + 3)
                        xin, xir = loaded.pop(i)
                        ca, car = caring.next()
                        P.ts("dve", ca[:, 0:n], xin[:, 1:n + 1], cw[:, fc, 1:2], cb[:, fc:fc + 1], ALU.mult, ALU.add, reads=[xir, rcw], writes=[car])
                        P.stt(ca[:, 0:n], xin[:, 0:n], cw[:, fc, 0:1], ca[:, 0:n], ALU.mult, ALU.add, reads=[xir, rcw, car], writes=[car])
                        P.stt(ca[:, 0:n], xin[:, 2:n + 2], cw[:, fc, 2:3], ca[:, 0:n], ALU.mult, ALU.add, reads=[xir, rcw, car], writes=[car])
                        cy, cyr = cyring.next()
                        P.act(cy[:, 0:n], ca[:, 0:n], AF.Silu, reads=[car], writes=[cyr])
                        if fc == 4:
                            P.dma(S["BT"][:, t0:t0 + n], cy[:, 0:n], reads=[cyr])
                        if fc == 5:
                            P.dma(S["CT"][:, t0:t0 + n], cy[:, 0:n], reads=[cyr])
                            continue
                        ps, pr = nextps()
                        pb = ps.bitcast(BF16).rearrange("p (j t) -> p j t", t=128)
                        nst = n // 128
                        for j in range(nst):
                            P.tr(pb[:, j, :], cy[:, j * 128:(j + 1) * 128], identb, reads=[cyr], writes=[pr], sig=(j == nst - 1), partial=(j > 0))
                        ct_, ctr = ctring.next()
                        P.cp("dve", ct_[:, 0:n], ps.bitcast(BF16)[:, 0:n], reads=[pr], writes=[ctr])
                        if fc < 4:
                            dst = S["XS"][t0:t0 + n, fc * 128:(fc + 1) * 128]
                        else:
                            dst = S["BTM"][t0:t0 + n, :]
                        P.dma(dst.rearrange("(j p) c -> p j c", p=128), ct_[:, 0:n].rearrange("p (j c) -> p j c", c=128), reads=[ctr])
                P.barrier()
            if stop_after == "ssmprep":
                break

            with ExitStack() as ph:
                XSs = ph.enter_context(SBT("XSs", [128, NT, 512], BF16))
                BTs = ph.enter_context(SBT("BTs", [128, TALL], BF16))
                CTs = ph.enter_context(SBT("CTs", [128, TALL], BF16))
                BTMs = ph.enter_context(SBT("BTMs", [128, NT, 128], BF16))
                DTs = ph.enter_context(SBT("DTs", [128, NT, 16], F32))
                aneg = ph.enter_context(SBT("aneg", [128, 16], F32))
                dsk = ph.enter_context(SBT("dsk", [128, 2, 8], F32))
                snw = ph.enter_context(SBT("snw", [128, 512], F32))
                hs32 = ph.enter_context(SBT("hs32", [128, 2, 4, 64], F32))
                hsb = ph.enter_context(SBT("hsb", [128, 2, 4, 64], BF16))

                def mk(name, shape, dt, n):
                    ts_ = [ph.enter_context(SBT("%s%d" % (name, i), shape, dt)) for i in range(n)]
                    return Ring([t[:] for t in ts_])
                dAr = mk("dA", [128, 16], F32, 4)
                xdtr = mk("xdt", [128, 8, 64], BF16, 3)
                rcr = mk("rcum", [128, 4, 128], F32, 4)
                ssr = mk("ssub", [128, 4, 128], F32, 4)
                scr_ = mk("scm", [128, 128], F32, 4)
                wr_ = mk("wmat", [128, 4, 128], BF16, 4)
                ear = mk("eacs", [128, 4, 128], F32, 4)
                cdr = mk("cdec", [128, 4, 128], BF16, 4)
                xder = mk("xdec", [128, 4, 64], BF16, 4)
                yr_ = mk("ysb", [128, 512], F32, 3)
                yfr = mk("yfl", [128, 512], F32, 2)
                zbr = mk("zbl", [128, 512], BF16, 2)
                y2r = mk("y2", [128, 512], F32, 2)
                sor = mk("sob", [128, 512], BF16, 2)
                smr_ = mk("ssm_sm", [128, 8], F32, 3)
                r_res, r_par = Res(), Res()
                for i0 in range(0, NT, 8):
                    i1 = min(NT, i0 + 8)
                    P.dma(XSs[:, i0:i1, :], S["XS"][i0 * 128:i1 * 128, :].rearrange("(i p) c -> p i c", p=128), writes=[r_res], partial=True)
                P.dma(BTs[:], S["BT"], writes=[r_res], partial=True)
                P.dma(CTs[:], S["CT"], writes=[r_res], partial=True)
                P.dma(BTMs[:], S["BTM"].rearrange("(i p) c -> p i c", p=128), writes=[r_res], partial=True)
                P.dma(DTs[:], S["DT"].rearrange("(i p) c -> p i c", p=128), writes=[r_res], partial=True)
                P.dma(aneg[:], W["ssm_a_log"][layer].rearrange("a b -> (a b)").partition_broadcast(128), writes=[r_par], partial=True)
                P.dma(dsk[:], W["ssm_d"][layer].rearrange("a b -> (a b)").partition_broadcast(128), writes=[r_par], partial=True)
                P.dma(snw[:], W["ssm_norm_w"][layer].partition_broadcast(128), writes=[r_par], partial=True)
                P.act(aneg[:], aneg[:], AF.Exp, reads=[r_par], writes=[r_par])
                P.ts("dve", aneg[:], aneg[:], -1.0, None, ALU.mult, reads=[r_par], writes=[r_par])
                P.tt("dve", dsk[:, 0, :], dsk[:, 0, :], dsk[:, 1, :], ALU.add, reads=[r_par], writes=[r_par])
                ydone = {}

                def ssd_pass(d):
                    rh = Res()
                    hs32d, hsbd = hs32[:, d], hsb[:, d]
                    tri32 = trif32 if d == 0 else trib32
                    lend = 127 if d == 0 else 0
                    order = [0, 1] + list(range(2, NT)) if d == 0 else [1, 0] + list(range(NT - 1, 1, -1))
                    for ci, c in enumerate(order):
                        first = (ci == 0)
                        tsl = slice(c * 128, (c + 1) * 128)
                        dA, dAr_ = dAr.next()
                        P.tt("dve", dA[:, 0:8], DTs[:, c, d * 8:(d + 1) * 8], aneg[:, d * 8:(d + 1) * 8], ALU.mult, reads=[r_res, r_par], writes=[dAr_])
                        ps, pr = nextps()
                        P.mm(ps[:, 0:8], lhsT=tri32, rhs=dA[:, 0:8], start=True, stop=True, reads=[dAr_], writes=[pr])
                        xdt, xdtr_ = xdtr.next()
                        P.tt("dve", xdt, XSs[:, c, :].rearrange("p (h q) -> p h q", q=64), DTs[:, c, d * 8:(d + 1) * 8].unsqueeze(2).to_broadcast([128, 8, 64]),
                             ALU.mult, reads=[r_res], writes=[xdtr_])
                        rcs = []
                        for g in range(2):
                            rc, rcr_ = rcr.next()
                            P.tt("pool", rc, tri32.unsqueeze(1).to_broadcast([128, 4, 128]), dA[:, g * 4:(g + 1) * 4].unsqueeze(2).to_broadcast([128, 4, 128]),
                                 ALU.mult, reads=[dAr_], writes=[rcr_])
                            rcs.append((rc, rcr_))
                        yield
                        P.cp("dve", dA[:, 8:16], ps[:, 0:8], reads=[pr], writes=[dAr_], partial=True)
                        sps, spr = nextps()
                        yps = []
                        for g in range(2):
                            gs = slice(g * 64, (g + 1) * 64)
                            ps, pr = nextps()
                            P.mm(ps[:, 0:128], lhsT=BTs[gs, tsl], rhs=CTs[gs, tsl], start=True, stop=True, reads=[r_res], writes=[pr])
                            rc, rcr_ = rcs[g]
                            ps2, pr2 = nextps()
                            P.mm(ps2[:], lhsT=ones32, rhs=rc.rearrange("p h l -> p (h l)"), start=True, stop=True, reads=[rcr_], writes=[pr2])
                            yield
                            scm, scmr = scr_.next()
                            P.tt("dve", scm, ps[:, 0:128], tri32, ALU.mult, reads=[pr], writes=[scmr])
                            ps, pr = ps2, pr2
                            psv = ps.rearrange("p (h l) -> p h l", l=128)
                            ss, ssr_ = ssr.next()
                            for h in range(4):
                                P.ts("dve", ss[:, h, :], psv[:, h, :], dA[:, 8 + g * 4 + h:9 + g * 4 + h], 0.0, ALU.subtract, ALU.min, reads=[pr, dAr_], writes=[ssr_],
                                     partial=(h > 0))
                            ea, ear_ = ear.next()
                            P.act(ea[gs], psv[gs], AF.Exp, reads=[pr], writes=[ear_])
                            P.act(ss, ss, AF.Exp, reads=[ssr_], writes=[ssr_])
                            yield
                            wm_, wmr_ = wr_.next()
                            P.tt("dve", wm_, ss, scm.unsqueeze(1).to_broadcast([128, 4, 128]), ALU.mult, reads=[ssr_, scmr], writes=[wmr_])
                            yp, ypr = nextps()
                            yps.append((yp, ypr))
                            if not first:
                                cd, cdr_ = cdr.next()
                                P.tt("dve", cd[gs], ea[gs], CTs[gs, tsl].unsqueeze(1).to_broadcast([64, 4, 128]), ALU.mult, reads=[ear_, r_res], writes=[cdr_])
                            xde, xder_ = xder.next()
                            P.tt("dve", xde, xdt[:, g * 4:(g + 1) * 4, :], ss[:, :, lend:lend + 1].to_broadcast([128, 4, 64]), ALU.mult, reads=[xdtr_, ssr_], writes=[xder_])
                            yield
                            for h in range(4):
                                P.mm(yp[:, h * 64:(h + 1) * 64], lhsT=wm_[:, h, :], rhs=xdt[:, g * 4 + h, :], start=True, stop=first,
                                     reads=[wmr_, xdtr_], writes=[ypr], sig=(first and h == 3), partial=(h > 0))
                                if not first:
                                    P.mm(yp[:, h * 64:(h + 1) * 64], lhsT=cd[gs, h, :], rhs=hsbd[gs, h, :], start=False, stop=True,
                                         reads=[cdr_, rh], writes=[ypr], sig=(h == 3), partial=True)
                            P.mm(sps[gs, 0:256], lhsT=BTMs[:, c, gs], rhs=xde.rearrange("p h q -> p (h q)"), start=True, stop=True,
                                 reads=[r_res, xder_], writes=[spr], partial=(g > 0))
                            yield
                            if first:
                                P.cp("dve", hs32d[gs], sps[gs, 0:256].rearrange("p (h q) -> p h q", q=64), reads=[spr], writes=[rh], partial=(g > 0))
                            else:
                                P.tt("dve", hs32d[gs], hs32d[gs], ea[gs, :, lend:lend + 1].to_broadcast([64, 4, 64]), ALU.mult, reads=[rh, ear_], writes=[rh], partial=True)
                                P.tt("dve", hs32d[gs], hs32d[gs], sps[gs, 0:256].rearrange("p (h q) -> p h q", q=64), ALU.add, reads=[rh, spr], writes=[rh], partial=True)
                            P.act(hsbd[gs], hs32d[gs], AF.Copy, reads=[rh], writes=[rh], partial=True)
                        if c not in ydone:
                            ysb, ysr = yr_.next()
                            for g in range(2):
                                P.act(ysb[:, g * 256:(g + 1) * 256], yps[g][0][:, 0:256], AF.Copy, reads=[yps[g][1]], writes=[ysr], partial=(g > 0))
                            yield
                            ydone[c] = Res()
                            P.dma(S["YF"][tsl, :], ysb, reads=[ysr], writes=[ydone[c]])
                        elif c >= 2 or need_ctx:
                            yf, yfr_ = yfr.next()
                            zb, zbr_ = zbr.next()
                            P.dma(yf, S["YF"][tsl, :], reads=[ydone[c]], writes=[yfr_])
                            P.dma(zb, S["ZB"][tsl, :], writes=[zbr_])
                            y2, y2r_ = y2r.next()
                            for g in range(2):
                                P.tt("dve", y2[:, g * 256:(g + 1) * 256], yps[g][0][:, 0:256], yf[:, g * 256:(g + 1) * 256], ALU.add, reads=[yps[g][1], yfr_], writes=[y2r_],
                                     partial=(g > 0))
                            ysb, ysr = yr_.next()
                            P.tt("pool", ysb.rearrange("p (h q) -> p h q", q=64), XSs[:, c, :].rearrange("p (h q) -> p h q", q=64),
                                 dsk[:, 0, :].unsqueeze(2).to_broadcast([128, 8, 64]), ALU.mult, reads=[r_res, r_par], writes=[ysr])
                            yield
                            P.tt("dve", y2, y2, ysb, ALU.add, reads=[y2r_, ysr], writes=[y2r_])
                            P.tt("dve", y2, y2, zb, ALU.mult, reads=[y2r_, zbr_], writes=[y2r_])
                            sm, smr2 = smr_.next()
                            for g in range(2):
                                P.op("dve", lambda e, g=g, ysb=ysb, y2=y2, sm=sm: e.scalar_tensor_tensor(out=ysb[:, g * 256:(g + 1) * 256], in0=y2[:, g * 256:(g + 1) * 256], scalar=1.0,
                                                                      in1=y2[:, g * 256:(g + 1) * 256], op0=ALU.mult, op1=ALU.mult, accum_out=sm[:, g:g + 1]),
                                     reads=[y2r_], writes=[ysr, smr2], partial=(g > 0))
                            P.ts("dve", sm[:, 2:4], sm[:, 0:2], 1.0 / 256.0, EPS, ALU.mult, ALU.add, reads=[smr2], writes=[smr2])
                            yield
                            P.act(sm[:, 4:6], sm[:, 2:4], AF.Ln, reads=[smr2], writes=[smr2])
                            P.act(sm[:, 6:8], sm[:, 4:6], AF.Exp, reads=[smr2], writes=[smr2], scale=-0.5)
                            yield
                            so, sor_ = sor.next()
                            for g in range(2):
                                P.stt(so[:, g * 256:(g + 1) * 256], y2[:, g * 256:(g + 1) * 256], sm[:, 6 + g:7 + g], snw[:, g * 256:(g + 1) * 256], ALU.mult, ALU.mult,
                                      reads=[y2r_, smr2, r_par], writes=[sor_], partial=(g > 0))
                            yield
                            P.dma(S["SO"][tsl, :], so, reads=[sor_])
                        yield

                gens = [ssd_pass(0), ssd_pass(1)]
                while gens:
                    for g_ in list(gens):
                        if next(g_, "done") == "done":
                            gens.remove(g_)
                P.barrier()
            if stop_after == "ssd":
                break


            with ExitStack() as ph:
                GQa = ph.enter_context(SBT("GQs", [128, 2, 2, TALL], BF16))
                GKa = ph.enter_context(SBT("GKs", [128, 2, 2, TALL], BF16))
                GELa = ph.enter_context(SBT("GELs", [128, 2, 2, NCH], F32))
                gnw = ph.enter_context(SBT("gnw", [128, 128], F32))
                ghsa = ph.enter_context(SBT("ghs", [128, 2, 2, 128], F32))

                def mk(name, shape, dt, n):
                    ts_ = [ph.enter_context(SBT("%s%d" % (name, i), shape, dt)) for i in range(n)]
                    return Ring([t[:] for t in ts_])
                gvr = mk("gvl", [128, 512], BF16, 5)
                gker = mk("gkel", [128, 256], BF16, 5)
                atr = mk("attm", [128, 4, 128], BF16, 4)
                hyr = mk("ghy", [128, 2, 128], F32, 4)
                hxbr = mk("ghxb", [128, 2, 128], BF16, 4)
                hybr = mk("ghyb", [128, 2, 128], BF16, 4)
                gyr = mk("gysb", [128, 512], F32, 4)
                gyfr = mk("gyfl", [128, 512], F32, 3)
                zcr = mk("zcl", [128, 512], BF16, 3)
                gy2r = mk("gy2", [128, 512], F32, 3)
                gor = mk("gob", [128, 512], BF16, 3)
                gsmr = mk("gsm", [128, 16], F32, 3)
                rg_par = Res()
                P.dma(gnw[:], W["gla_norm_w"][layer].partition_broadcast(128), writes=[rg_par])
                rq = Res()
                for d in range(2):
                    for hp in range(2):
                        P.dma(GQa[:, d, hp, :], S["GQ"][d][hp * 128:(hp + 1) * 128, :], writes=[rq], partial=True)
                        P.dma(GKa[:, d, hp, :], S["GK"][d][hp * 128:(hp + 1) * 128, :], writes=[rq], partial=True)
                        P.dma(GELa[:, d, hp, :], S["GEL"][d][hp * 128:(hp + 1) * 128, :], writes=[rq], partial=True)
                gdone = {}

                def gla_pass(d):
                    GQs, GKs, GELs, ghs = GQa[:, d], GKa[:, d], GELa[:, d], ghsa[:, d]
                    gmask = gmfb if d == 0 else gmbb
                    order = [0, 1] + list(range(2, NT)) if d == 0 else [1, 0] + list(range(NT - 1, 1, -1))
                    rgh = Res()
                    loads = {}

                    def issue_loads(c):
                        gv, gvr_ = gvr.next()
                        gke, gker_ = gker.next()
                        P.dma(gv, S["GV"][c * 128:(c + 1) * 128, :], writes=[gvr_])
                        P.dma(gke, S["GKE"][d][c * 128:(c + 1) * 128, :], writes=[gker_])
                        loads[c] = (gv, gvr_, gke, gker_)
                    issue_loads(order[0])
                    for ci, c in enumerate(order):
                        first = (ci == 0)
                        tsl = slice(c * 128, (c + 1) * 128)
                        if ci + 1 < len(order):
                            issue_loads(order[ci + 1])
                        gv, gvr_, gke, gker_ = loads.pop(c)
                        X, Y = (0, 1) if d == 0 else (1, 0)
                        chX, chY = 2 * c + X, 2 * c + Y
                        attm, atr_ = atr.next()
                        aps = []
                        for j in range(2):
                            js = slice(j * 64, (j + 1) * 64)
                            ps, pr = nextps()
                            for hp in range(2):
                                P.mm(ps[:, hp * 128:(hp + 1) * 128], lhsT=GKs[js, hp, tsl], rhs=GQs[js, hp, tsl], start=True, stop=True, reads=[rq], writes=[pr],
                                     partial=(hp > 0))
                            aps.append((ps, pr))
                        sp = []
                        for ch in range(2):
                            cs_ = slice(ch * 64, (ch + 1) * 64)
                            ps, pr = nextps()
                            for hp in range(2):
                                for j in range(2):
                                    hd = hp * 2 + j
                                    P.mm(ps[j * 64:(j + 1) * 64, hp * 128:(hp + 1) * 128], lhsT=gke[cs_, hd * 64:(hd + 1) * 64], rhs=gv[cs_, hd * 128:(hd + 1) * 128],
                                         start=True, stop=True, reads=[gker_, gvr_], writes=[pr], partial=(hp + j > 0), sig=(hp + j == 2))
                            sp.append((ps, pr))
                        yield
                        for j in range(2):
                            ps, pr = aps[j]
                            for hp in range(2):
                                P.tt("dve", attm[:, hp * 2 + j, :], ps[:, hp * 128:(hp + 1) * 128], gmask, ALU.mult, reads=[pr], writes=[atr_], partial=(j + hp > 0))
                        hxb, hxbr_ = hxbr.next()
                        hyb, hybr_ = hybr.next()
                        hy, hyr_ = hyr.next()
                        if first:
                            P.cp("dve", hy, sp[X][0][:, 0:256].rearrange("p (a v) -> p a v", v=128), reads=[sp[X][1]], writes=[hyr_])
                        else:
                            P.act(hxb, ghs, AF.Copy, reads=[rgh], writes=[hxbr_])
                            for hp in range(2):
                                P.stt(hy[:, hp, :], ghs[:, hp, :], GELs[:, hp, chX:chX + 1], sp[X][0][:, hp * 128:(hp + 1) * 128], ALU.mult, ALU.add,
                                      reads=[rgh, rq, sp[X][1]], writes=[hyr_], partial=(hp > 0))
                        yield
                        P.act(hyb, hy, AF.Copy, reads=[hyr_], writes=[hybr_])
                        for hp in range(2):
                            P.stt(ghs[:, hp, :], hy[:, hp, :], GELs[:, hp, chY:chY + 1], sp[Y][0][:, hp * 128:(hp + 1) * 128], ALU.mult, ALU.add,
                                  reads=[hyr_, rq, sp[Y][1]], writes=[rgh], partial=(hp > 0))
                        yield
                        yps = []
                        for j in range(2):
                            js = slice(j * 64, (j + 1) * 64)
                            yp, ypr = nextps()
                            for hp in range(2):
                                hd = hp * 2 + j
                                ysl = yp[:, hp * 128:(hp + 1) * 128]
                                P.mm(ysl, lhsT=attm[:, hd, :], rhs=gv[:, hd * 128:(hd + 1) * 128], start=True, stop=False, reads=[atr_, gvr_], writes=[ypr],
                                     sig=False, partial=(hp > 0), skip_group_check=True)
                                if not first:
                                    P.mm(yp[X * 64:(X + 1) * 64, hp * 128:(hp + 1) * 128], lhsT=GQs[js, hp, c * 128 + X * 64:c * 128 + (X + 1) * 64], rhs=hxb[js, hp, :],
                                         start=False, stop=False, reads=[rq, hxbr_], writes=[ypr], sig=False, partial=True, skip_group_check=True)
                                P.mm(yp[Y * 64:(Y + 1) * 64, hp * 128:(hp + 1) * 128], lhsT=GQs[js, hp, c * 128 + Y * 64:c * 128 + (Y + 1) * 64], rhs=hyb[js, hp, :],
                                     start=False, stop=True, reads=[rq, hybr_], writes=[ypr], sig=(hp == 1), partial=True, skip_group_check=True)
                            yps.append((yp, ypr))
                        yield
                        if c not in gdone:
                            ysb, ysr = gyr.next()
                            ysv = ysb.rearrange("p (hp j v) -> p hp j v", j=2, v=128)
                            for j in range(2):
                                P.act(ysv[:, :, j, :], yps[j][0][:, 0:256].rearrange("p (hp v) -> p hp v", v=128), AF.Copy, reads=[yps[j][1]], writes=[ysr], partial=(j > 0))
                            yield
                            gdone[c] = Res()
                            P.dma(S["GYF"][tsl, :], ysb, reads=[ysr], writes=[gdone[c]])
                        elif c >= 2 or need_ctx:
                            yf, yfr_ = gyfr.next()
                            zc, zcr_ = zcr.next()
                            P.dma(yf, S["GYF"][tsl, :], reads=[gdone[c]], writes=[yfr_])
                            P.dma(zc, S["ZC"][tsl, :], writes=[zcr_])
                            y2, y2r_ = gy2r.next()
                            y2v = y2.rearrange("p (hp j v) -> p hp j v", j=2, v=128)
                            yfv = yf.rearrange("p (hp j v) -> p hp j v", j=2, v=128)
                            for j in range(2):
                                P.tt("dve", y2v[:, :, j, :], yps[j][0][:, 0:256].rearrange("p (hp v) -> p hp v", v=128), yfv[:, :, j, :], ALU.add,
                                     reads=[yps[j][1], yfr_], writes=[y2r_], partial=(j > 0))
                            ysb, ysr = gyr.next()
                            sm, smr2 = gsmr.next()
                            for hd in range(4):
                                P.op("dve", lambda e, hd=hd, ysb=ysb, y2=y2, sm=sm: e.scalar_tensor_tensor(out=ysb[:, hd * 128:(hd + 1) * 128], in0=y2[:, hd * 128:(hd + 1) * 128], scalar=1.0,
                                                                       in1=y2[:, hd * 128:(hd + 1) * 128], op0=ALU.mult, op1=ALU.mult, accum_out=sm[:, hd:hd + 1]),
                                     reads=[y2r_], writes=[ysr, smr2], partial=(hd > 0))
                            P.ts("dve", sm[:, 4:8], sm[:, 0:4], 1.0 / 128.0, EPS, ALU.mult, ALU.add, reads=[smr2], writes=[smr2])
                            yield
                            P.act(sm[:, 8:12], sm[:, 4:8], AF.Ln, reads=[smr2], writes=[smr2])
                            P.act(sm[:, 12:16], sm[:, 8:12], AF.Exp, reads=[smr2], writes=[smr2], scale=-0.5)
                            yield
                            for hd in range(4):
                                P.stt(y2[:, hd * 128:(hd + 1) * 128], y2[:, hd * 128:(hd + 1) * 128], sm[:, 12 + hd:13 + hd], gnw[:], ALU.mult, ALU.mult,
                                      reads=[y2r_, smr2, rg_par], writes=[y2r_], partial=(hd > 0))
                            go, gor_ = gor.next()
                            P.tt("dve", go, y2, zc, ALU.mult, reads=[y2r_, zcr_], writes=[gor_])
                            yield
                            P.dma(S["GO"][tsl, :], go, reads=[gor_])
                        yield

                gens = [gla_pass(0), gla_pass(1)]
                while gens:
                    for g_ in list(gens):
                        if next(g_, "done") == "done":
                            gens.remove(g_)
                P.barrier()
            if stop_after == "gla":
                break


            last = (layer == DEPTH - 1)
            with ExitStack() as ph:
                wob = [ph.enter_context(SBT("wob%d" % i, [128, 4, D], BF16)) for i in range(3)]
                wo = ph.enter_context(SBT("wo", [128, 8, D], BF16))
                fnw = ph.enter_context(SBT("fnw", [128, D], F32))
                oT = [ph.enter_context(SBT("oT%d" % i, [128, 4, 512], BF16)) for i in range(3)]
                uT = ph.enter_context(SBT("uT", [128, 8, 512], BF16))

                def mk(name, shape, dt, n):
                    ts_ = [ph.enter_context(SBT("%s%d" % (name, i), shape, dt)) for i in range(n)]
                    return Ring([t[:] for t in ts_])
                btr = mk("mbt", [128, 4, 512], BF16, 2)
                sgr = mk("msg", [128, 512], BF16, 12)
                mtr = mk("mt", [128, 512], F32, 6)
                mxr = mk("mx", [128, D], F32, 5)
                mnr = mk("mxn", [128, D], F32, 2)
                msm = mk("msm", [128, 8], F32, 2)
                mjk = ph.enter_context(SBT("mjk", [128, D], BF16))
                rw_ = Res()
                for b, nm in enumerate(("w_out_da", "w_out_ssm", "w_out_gla")):
                    P.dma(wob[b][:], W[nm][layer].rearrange("(k p) c -> p k c", p=128), writes=[rw_], q="pool", partial=True)
                for k0 in range(0, 8, 2):
                    P.dma(wo[:, k0:k0 + 2, :], W["w_o"][layer].rearrange("(k p) c -> p k c", p=128)[:, k0:k0 + 2, :], writes=[rw_], q="pool", partial=True)
                P.dma(fnw[:], W["final_norm_w"].partition_broadcast(128), writes=[rw_], partial=True)
                oTr = [Res() for _ in range(3)]
                uTr = Res()
                jkr = Res()
                for (t0, n, isc) in groups:
                    if isc and not need_ctx:
                        continue
                    who = 1 if isc else 0
                    nst = n // 128
                    for kc in range(4):
                        P.dma(oT[0][:, kc, 0:n], S["AOT"][kc * 128:(kc + 1) * 128, t0:t0 + n], writes=[oTr[0]], partial=(kc > 0))
                    for b, nm in ((1, "SO"), (2, "GO")):
                        bt, btr_ = btr.next()
                        P.dma(bt[:, 0:nst, :], S[nm][t0:t0 + n, :].rearrange("(j p) c -> p j c", p=128), writes=[btr_])
                        for kc in range(4):
                            ps, pr = nextps()
                            pb = ps.bitcast(BF16).rearrange("p (j t) -> p j t", t=128)
                            for j in range(nst):
                                P.tr(pb[:, j, :], bt[:, j, kc * 128:(kc + 1) * 128], identb, reads=[btr_], writes=[pr], sig=(j == nst - 1), partial=(j > 0))
                            if kc % 2 == 0:
                                P.act(oT[b][:, kc, 0:n], ps.bitcast(BF16)[:, 0:n], AF.Copy, reads=[pr], writes=[oTr[b]], partial=(kc > 0))
                            else:
                                P.cp("dve", oT[b][:, kc, 0:n], ps.bitcast(BF16)[:, 0:n], reads=[pr], writes=[oTr[b]], partial=True)
                    xld = []
                    for j in range(nst):
                        xt, xr = mxr.next()
                        P.dma(xt, tok_src(layer, t0 // 128 + j), writes=[xr])
                        xld.append((xt, xr))
                    sgl = {}

                    def issue_sg(oc):
                        for b in range(3):
                            sg, sgr_ = sgr.next()
                            P.dma(sg[:, 0:n], S["SG"][b][oc * 128:(oc + 1) * 128, t0:t0 + n], writes=[sgr_])
                            sgl[(oc, b)] = (sg, sgr_)
                    issue_sg(0)
                    issue_sg(1)
                    for oc in range(8):
                        if oc + 2 < 8:
                            issue_sg(oc + 2)
                        tms = []
                        for b in range(3):
                            sg, sgr_ = sgl.pop((oc, b))
                            ps, pr = nextps()
                            for kc in range(4):
                                P.mm(ps[:, 0:n], lhsT=wob[b][:, kc, oc * 128:(oc + 1) * 128], rhs=oT[b][:, kc, 0:n], start=(kc == 0), stop=(kc == 3),
                                     reads=[rw_, oTr[b]], writes=[pr])
                            tm, tmr = mtr.next()
                            P.tt("dve", tm[:, 0:n], ps[:, 0:n], sg[:, 0:n], ALU.mult, reads=[pr, sgr_], writes=[tmr])
                            tms.append((tm, tmr))
                        P.tt("pool", tms[0][0][:, 0:n], tms[0][0][:, 0:n], tms[1][0][:, 0:n], ALU.add, reads=[tms[0][1], tms[1][1]], writes=[tms[0][1]])
                        P.tt("dve", uT[:, oc, 0:n], tms[0][0][:, 0:n], tms[2][0][:, 0:n], ALU.add, reads=[tms[0][1], tms[2][1]], writes=[uTr], partial=(oc > 0))
                    for j in range(nst):
                        ti = t0 // 128 + j
                        xt, xr = xld[j]
                        xn, xnr = mnr.next()
                        for hf in range(2):
                            ps, pr = nextps()
                            for k in range(KC):
                                P.mm(ps[:], lhsT=uT[:, k, j * 128:(j + 1) * 128], rhs=wo[:, k, hf * 512:(hf + 1) * 512], start=(k == 0), stop=(k == KC - 1),
                                     reads=[rw_, uTr], writes=[pr])
                            P.tt("dve", xn[:, hf * 512:(hf + 1) * 512], ps[:], gate_bc[:, who, hf * 512:(hf + 1) * 512], ALU.mult, reads=[pr], writes=[xnr], partial=(hf > 0))
                        P.tt("pool", xn, xn, xt, ALU.add, reads=[xnr, xr], writes=[xnr])
                        if not last:
                            dst = S["CXR"][ti * 128:(ti + 1) * 128, :] if isc else S["XR"][(ti - 2) * 128:(ti - 1) * 128, :]
                            P.dma(dst, xn, reads=[xnr])
                        else:
                            sm, smr2 = msm.next()
                            P.act(mjk[:], xn, AF.Square, reads=[xnr], writes=[jkr, smr2], accum_out=sm[:, 0:1])
                            P.ts("dve", sm[:, 1:2], sm[:, 0:1], 1.0 / D, EPS, ALU.mult, ALU.add, reads=[smr2], writes=[smr2])
                            P.act(sm[:, 2:3], sm[:, 1:2], AF.Ln, reads=[smr2], writes=[smr2])
                            P.act(sm[:, 3:4], sm[:, 2:3], AF.Exp, reads=[smr2], writes=[smr2], scale=-0.5)
                            P.stt(xt, xn, sm[:, 3:4], fnw[:], ALU.mult, ALU.mult, reads=[xnr, smr2, rw_], writes=[xr])
                            P.dma(y_out[(ti - 2) * 128:(ti - 1) * 128, :], xt, reads=[xr])
                P.barrier()

        P.barrier()
        for e in ("sp", "pool", "act", "dve", "pe"):
            P.flush(e)
    return nc


_NC_CACHE = {}


def kernel(x, c, ctx, c_ctx, **weights):
    x = np.asarray(x, dtype=np.float32)
    B, TL, _ = x.shape
    if TL not in _NC_CACHE:
        _NC_CACHE[TL] = build(TL)
    nc = _NC_CACHE[TL]
    consts = host_consts(TL)
    shared = {"c_ctx": np.ascontiguousarray(np.asarray(c_ctx, dtype=np.float32))}
    for n, _s in WEIGHT_SPECS:
        shared[n] = np.ascontiguousarray(np.asarray(weights[n], dtype=np.float32))
    shared.update(consts)
    in_maps = []
    for b in range(B):
        m = dict(shared)
        m["x"] = np.ascontiguousarray(x[b])
        m["c"] = np.ascontiguousarray(np.asarray(c, dtype=np.float32)[b])
        m["ctx"] = np.ascontiguousarray(np.asarray(ctx, dtype=np.float32)[b])
        in_maps.append(m)
    res = run_bass_kernel_spmd(nc, in_maps, core_ids=list(range(B)))
    return np.stack([np.asarray(r["y"], dtype=np.float32) for r in res.results], axis=0)
```
